# Optimizing a Trainium2 kernel written in Bass

```python
import math
import jax
import jax.numpy as jnp
from jax import lax
import numpy as np

D_MODEL = 1024
BATCH = 8
SEQ = 4096
DEPTH = 1
DEC_BATCH = 128
DEC_SEQ = 4
PAST_LEN = 8192
PAGE_SIZE = 128

SSM_HEADS = 8
SSM_HEAD_DIM = 64
D_SSM = SSM_HEADS * SSM_HEAD_DIM
SSM_GROUPS = 2
D_STATE = 64
CONV_W = 4
CONV_DIM = D_SSM + 2 * SSM_GROUPS * D_STATE
SSD_CHUNK = 128
NSA_HEADS = 8
NSA_KV_HEADS = 2
HEAD_DIM = 64
D_NSA = NSA_HEADS * HEAD_DIM
KV_DIM = NSA_KV_HEADS * HEAD_DIM
N_KV_PROJ = 6
CMP_STRIDE = 16
CMP_BLOCK = 2 * CMP_STRIDE
CMP_HIDDEN = 128
SEL_BLOCK = 64
SEL_TOP_N = 16
WINDOW = 512
Q_BLOCK = 64
N_BRANCH = 3
ROPE_THETA = 10000.0
D_FF = 2816
D_MIX = D_SSM + D_NSA
IN_SPLITS = (D_SSM, CONV_DIM, SSM_HEADS, D_NSA, N_KV_PROJ * KV_DIM, NSA_HEADS * N_BRANCH)
IN_OFFSETS = tuple(int(v) for v in np.cumsum(IN_SPLITS)[:-1])
D_IN = sum(IN_SPLITS)
ALPHA = (2.0 * DEPTH) ** 0.25
BETA = (8.0 * DEPTH) ** -0.25
LN_EPS = 1e-5
RMS_EPS = 1e-5
NEG_INF = -1e30

kernel_name = "hymba_ssd_nsa_macaron_deepnorm_step"


def layer_norm(x, g, b):
    xf = x.astype(jnp.float32)
    mu = xf.mean(-1, keepdims=True)
    var = jnp.square(xf - mu).mean(-1, keepdims=True)
    return ((xf - mu) * lax.rsqrt(var + LN_EPS) * g + b).astype(x.dtype)


def swiglu(x, w_gate, w_up, w_down):
    return (jax.nn.silu(x @ w_gate) * (x @ w_up)) @ w_down


def rope(x, pos):
    half = HEAD_DIM // 2
    inv = ROPE_THETA ** (-jnp.arange(half, dtype=jnp.float32) / half)
    ang = pos.astype(jnp.float32)[:, None] * inv[None, :]
    cos = jnp.cos(ang)[None, :, None, :]
    sin = jnp.sin(ang)[None, :, None, :]
    xf = x.astype(jnp.float32)
    x1, x2 = xf[..., :half], xf[..., half:]
    return jnp.concatenate([x1 * cos - x2 * sin, x2 * cos + x1 * sin], axis=-1).astype(x.dtype)


def masked_softmax(s, mask):
    return jax.nn.softmax(jnp.where(mask, s, NEG_INF), axis=-1) * mask


def causal_dwconv(x_hist, w, b):
    y = lax.conv_general_dilated(x_hist, w[:, None, :], window_strides=(1,), padding='VALID',
                                 dimension_numbers=('NWC', 'WIO', 'NWC'),
                                 feature_group_count=x_hist.shape[-1])
    return y + b


def project_in(h, lw, pos):
    bsz, t = h.shape[:2]
    z, xbc, dt_raw, q, kv, gate = jnp.split(h @ lw['w_in'], IN_OFFSETS, axis=-1)
    q = rope(q.reshape(bsz, t, NSA_HEADS, HEAD_DIM), pos)
    kv = kv.reshape(bsz, t, N_KV_PROJ, NSA_KV_HEADS, HEAD_DIM)
    k_c, v_c, k_s, v_s, k_w, v_w = [kv[:, :, i] for i in range(N_KV_PROJ)]
    k_c, k_s, k_w = rope(k_c, pos), rope(k_s, pos), rope(k_w, pos)
    gate = (gate + lw['b_gate']).reshape(bsz, t, NSA_HEADS, N_BRANCH)
    return z, xbc, dt_raw, q, k_c, v_c, k_s, v_s, k_w, v_w, gate


def ssd_scan(x, dt, a, bm, cm, h0):
    bsz, t = x.shape[:2]
    lc = min(SSD_CHUNK, t)
    nc = t // lc

    def to_chunks(u):
        return jnp.moveaxis(u.reshape(bsz, nc, lc, *u.shape[2:]), 1, 0)

    causal = jnp.tril(jnp.ones((lc, lc), bool))

    def step(h, inp):
        xc, dtc, bc, cc = inp
        cum = jnp.cumsum(dtc * a, axis=1)
        seg = cum[:, :, None, :] - cum[:, None, :, :]
        decay = jnp.exp(jnp.where(causal[None, :, :, None], seg, -jnp.inf))
        xdt = xc * dtc[..., None]
        w = jnp.einsum('blhn,bshn->blsh', cc, bc) * decay
        y = jnp.einsum('blsh,bshp->blhp', w, xdt)
        y = y + jnp.einsum('blhn,bhpn->blhp', cc, h) * jnp.exp(cum)[..., None]
        tail = jnp.exp(cum[:, -1:, :] - cum)
        h_new = h * jnp.exp(cum[:, -1])[:, :, None, None] + jnp.einsum('bshn,bsh,bshp->bhpn', bc, tail, xdt)
        return h_new, y

    h, ys = lax.scan(step, h0, (to_chunks(x), to_chunks(dt), to_chunks(bm), to_chunks(cm)))
    return jnp.moveaxis(ys, 0, 1).reshape(x.shape), h


def ssd_mix(z, xbc_hist, dt_raw, h0, lw):
    f32 = jnp.float32
    bsz, t = z.shape[:2]
    xbc = jax.nn.silu(causal_dwconv(xbc_hist, lw['conv_w'], lw['conv_b']))
    xs, bm, cm = jnp.split(xbc, [D_SSM, D_SSM + SSM_GROUPS * D_STATE], axis=-1)
    rep = SSM_HEADS // SSM_GROUPS
    xs = xs.reshape(bsz, t, SSM_HEADS, SSM_HEAD_DIM).astype(f32)
    bm = jnp.repeat(bm.reshape(bsz, t, SSM_GROUPS, D_STATE), rep, axis=2).astype(f32)
    cm = jnp.repeat(cm.reshape(bsz, t, SSM_GROUPS, D_STATE), rep, axis=2).astype(f32)
    dt = jax.nn.softplus(dt_raw.astype(f32) + lw['dt_bias'])
    a = -jnp.exp(lw['a_log'].astype(f32))
    y, h = ssd_scan(xs, dt, a, bm, cm, h0.astype(f32))
    y = (y + lw['d_skip'][:, None] * xs).reshape(bsz, t, D_SSM) * jax.nn.silu(z.astype(f32))
    y = y * lax.rsqrt(jnp.mean(y * y, axis=-1, keepdims=True) + RMS_EPS) * lw['ssm_norm_w']
    return y.astype(z.dtype), h.astype(h0.dtype)


def compress(k, w1, w2, pe):
    bsz, length = k.shape[:2]
    n_str = -(-length // CMP_STRIDE)
    k = jnp.pad(k, ((0, 0), (0, n_str * CMP_STRIDE - length), (0, 0), (0, 0)))
    ch = k.reshape(bsz, n_str, CMP_STRIDE, NSA_KV_HEADS, HEAD_DIM)
    first = jnp.einsum('bnsgd,sdh->bngh', ch, w1[:CMP_STRIDE])
    second = jnp.einsum('bnsgd,sdh->bngh', ch, w1[CMP_STRIDE:])
    pre = first[:, :-1] + second[:, 1:] + jnp.einsum('sd,sdh->h', pe, w1)
    return jax.nn.gelu(pre) @ w2


def sel_blocks(k):
    bsz, length = k.shape[:2]
    ns = -(-length // SEL_BLOCK)
    k = jnp.pad(k, ((0, 0), (0, ns * SEL_BLOCK - length), (0, 0), (0, 0)))
    return k.reshape(bsz, ns, SEL_BLOCK, NSA_KV_HEADS, HEAD_DIM)


def nsa_keys(k_c, v_c, k_s, v_s, lw):
    kc = compress(k_c, lw['cmp_k_w1'], lw['cmp_k_w2'], lw['cmp_k_pe'])
    vc = compress(v_c, lw['cmp_v_w1'], lw['cmp_v_w2'], lw['cmp_v_pe'])
    cmp_end = jnp.arange(kc.shape[1]) * CMP_STRIDE + CMP_BLOCK - 1
    return kc, vc, cmp_end, sel_blocks(k_s), sel_blocks(v_s)


def nsa_attend(q, q_pos, kc, vc, cmp_end, ks_blk, vs_blk, kw, vw, kw_pos, gate):
    f32 = jnp.float32
    bsz, tq = q.shape[:2]
    rep = NSA_HEADS // NSA_KV_HEADS
    qg = q.reshape(bsz, tq, NSA_KV_HEADS, rep, HEAD_DIM).astype(f32) * HEAD_DIM ** -0.5
    m_c = cmp_end[None, :] <= q_pos[:, None]
    p_c = masked_softmax(jnp.einsum('btgrd,bcgd->btgrc', qg, kc.astype(f32)), m_c[None, :, None, None, :])
    o_c = jnp.einsum('btgrc,bcgd->btgrd', p_c, vc.astype(f32))
    nc, ns = kc.shape[1], ks_blk.shape[1]
    c_start = jnp.arange(nc) * CMP_STRIDE
    s_start = jnp.arange(ns) * SEL_BLOCK
    overlap = ((c_start[:, None] + CMP_BLOCK > s_start[None, :]) &
               (c_start[:, None] < s_start[None, :] + SEL_BLOCK)).astype(f32)
    imp = jnp.einsum('btgrc,cj->btgj', p_c, overlap)
    blk = jnp.arange(ns)[None, :]
    cur = (q_pos // SEL_BLOCK)[:, None]
    valid = (s_start[None, :] <= q_pos[:, None])[None, :, None, :]
    forced = ((blk == 0) | (blk == cur) | (blk == cur - 1))[None, :, None, :]
    score = jnp.where(valid, jnp.where(forced, jnp.inf, imp), -jnp.inf)
    _, idx = lax.top_k(score, min(SEL_TOP_N, ns))
    n_sel = idx.shape[-1]
    bi = jnp.arange(bsz)[:, None, None, None]
    gidx = jnp.arange(NSA_KV_HEADS)[None, None, :, None]
    k_sel = ks_blk[bi, idx, :, gidx].reshape(bsz, tq, NSA_KV_HEADS, n_sel * SEL_BLOCK, HEAD_DIM)
    v_sel = vs_blk[bi, idx, :, gidx].reshape(bsz, tq, NSA_KV_HEADS, n_sel * SEL_BLOCK, HEAD_DIM)
    kpos = (idx[..., None] * SEL_BLOCK + jnp.arange(SEL_BLOCK)).reshape(bsz, tq, NSA_KV_HEADS, n_sel * SEL_BLOCK)
    m_s = (kpos <= q_pos[None, :, None, None])[:, :, :, None, :]
    p_s = masked_softmax(jnp.einsum('btgrd,btgkd->btgrk', qg, k_sel.astype(f32)), m_s)
    o_s = jnp.einsum('btgrk,btgkd->btgrd', p_s, v_sel.astype(f32))
    m_w = ((kw_pos[None, :] <= q_pos[:, None]) & (kw_pos[None, :] > q_pos[:, None] - WINDOW)
           & (kw_pos[None, :] >= 0))
    p_w = masked_softmax(jnp.einsum('btgrd,bsgd->btgrs', qg, kw.astype(f32)), m_w[None, :, None, None, :])
    o_w = jnp.einsum('btgrs,bsgd->btgrd', p_w, vw.astype(f32))
    g = jax.nn.sigmoid(gate.astype(f32)).reshape(bsz, tq, NSA_KV_HEADS, rep, N_BRANCH)
    o = g[..., 0:1] * o_c + g[..., 1:2] * o_s + g[..., 2:3] * o_w
    return o.reshape(bsz, tq, D_NSA).astype(q.dtype)


def nsa_prompt(q, kc, vc, cmp_end, ks_blk, vs_blk, k_w, v_w, gate):
    bsz, t = q.shape[:2]
    nb = t // Q_BLOCK
    kw_pad = jnp.pad(k_w, ((0, 0), (WINDOW, 0), (0, 0), (0, 0)))
    vw_pad = jnp.pad(v_w, ((0, 0), (WINDOW, 0), (0, 0), (0, 0)))
    qb = jnp.moveaxis(q.reshape(bsz, nb, Q_BLOCK, NSA_HEADS, HEAD_DIM), 1, 0)
    gb = jnp.moveaxis(gate.reshape(bsz, nb, Q_BLOCK, NSA_HEADS, N_BRANCH), 1, 0)

    def block(args):
        qi, gi, i = args
        start = i * Q_BLOCK
        q_pos = start + jnp.arange(Q_BLOCK)
        kwi = lax.dynamic_slice_in_dim(kw_pad, start, WINDOW + Q_BLOCK, axis=1)
        vwi = lax.dynamic_slice_in_dim(vw_pad, start, WINDOW + Q_BLOCK, axis=1)
        kw_pos = start - WINDOW + jnp.arange(WINDOW + Q_BLOCK)
        return nsa_attend(qi, q_pos, kc, vc, cmp_end, ks_blk, vs_blk, kwi, vwi, kw_pos, gi)

    o = lax.map(block, (qb, gb, jnp.arange(nb)))
    return jnp.moveaxis(o, 0, 1).reshape(bsz, t, D_NSA)


def mix_prompt(h, lw):
    bsz, t = h.shape[:2]
    pos = jnp.arange(t)
    z, xbc, dt_raw, q, k_c, v_c, k_s, v_s, k_w, v_w, gate = project_in(h, lw, pos)
    xbc_hist = jnp.pad(xbc, ((0, 0), (CONV_W - 1, 0), (0, 0)))
    h0 = jnp.zeros((bsz, SSM_HEADS, SSM_HEAD_DIM, D_STATE), h.dtype)
    y_ssd, h_ssm = ssd_mix(z, xbc_hist, dt_raw, h0, lw)
    kc, vc, cmp_end, ks_blk, vs_blk = nsa_keys(k_c, v_c, k_s, v_s, lw)
    o_nsa = nsa_prompt(q, kc, vc, cmp_end, ks_blk, vs_blk, k_w, v_w, gate)
    out = jnp.concatenate([y_ssd, o_nsa], axis=-1) @ lw['w_out']
    wb = min(WINDOW, t)
    state = (k_c, v_c, k_s, v_s, k_w[:, -wb:], v_w[:, -wb:], h_ssm, xbc_hist[:, -(CONV_W - 1):])
    return out, state


def gather_pages(cache, l, page_table):
    g = cache[l, page_table]
    return g.reshape(page_table.shape[0], -1, NSA_KV_HEADS, HEAD_DIM)


def mix_sample(h, lw, l, cache_k_cmp, cache_v_cmp, cache_k_slc, cache_v_slc, cache_k_win, cache_v_win,
               state_ssm, state_conv, page_table):
    s_len = h.shape[1]
    pos = PAST_LEN + jnp.arange(s_len)
    z, xbc, dt_raw, q, k_c, v_c, k_s, v_s, k_w, v_w, gate = project_in(h, lw, pos)
    xbc_hist = jnp.concatenate([state_conv[l], xbc], axis=1)
    y_ssd, h_ssm = ssd_mix(z, xbc_hist, dt_raw, state_ssm[l], lw)
    kc, vc, cmp_end, ks_blk, vs_blk = nsa_keys(
        jnp.concatenate([gather_pages(cache_k_cmp, l, page_table), k_c], axis=1),
        jnp.concatenate([gather_pages(cache_v_cmp, l, page_table), v_c], axis=1),
        jnp.concatenate([gather_pages(cache_k_slc, l, page_table), k_s], axis=1),
        jnp.concatenate([gather_pages(cache_v_slc, l, page_table), v_s], axis=1), lw)
    wb = cache_k_win.shape[2]
    kw = jnp.concatenate([cache_k_win[l], k_w], axis=1)
    vw = jnp.concatenate([cache_v_win[l], v_w], axis=1)
    kw_pos = PAST_LEN - wb + jnp.arange(wb + s_len)
    o_nsa = nsa_attend(q, pos, kc, vc, cmp_end, ks_blk, vs_blk, kw, vw, kw_pos, gate)
    out = jnp.concatenate([y_ssd, o_nsa], axis=-1) @ lw['w_out']
    state = (k_c, v_c, k_s, v_s, kw[:, -wb:], vw[:, -wb:], h_ssm, xbc_hist[:, -(CONV_W - 1):])
    return out, state


def layer_forward(x, lw, mix_fn, *mix_args):
    h = layer_norm(ALPHA * x + 0.5 * swiglu(x, lw['ffn1_w_gate'], lw['ffn1_w_up'], lw['ffn1_w_down']),
                   lw['ln1_g'], lw['ln1_b'])
    mixed, state = mix_fn(h, lw, *mix_args)
    h = layer_norm(ALPHA * h + mixed, lw['ln2_g'], lw['ln2_b'])
    h = layer_norm(ALPHA * h + 0.5 * swiglu(h, lw['ffn2_w_gate'], lw['ffn2_w_up'], lw['ffn2_w_down']),
                   lw['ln3_g'], lw['ln3_b'])
    return h, state


def setup_inputs(seed: int = 0) -> dict:
    key = jax.random.key(seed)
    ks = jax.random.split(key, 40)
    f32 = jnp.float32
    n_pages = PAST_LEN // PAGE_SIZE
    n_used = DEC_BATCH * n_pages
    n_pool = n_used + max(1, n_used // 4)
    win_buf = min(WINDOW, PAST_LEN)

    def nrm(k, shape, s):
        return s * jax.random.normal(k, shape, f32)

    page_shape = (DEPTH, n_pool, PAGE_SIZE, NSA_KV_HEADS, HEAD_DIM)
    win_shape = (DEPTH, DEC_BATCH, win_buf, NSA_KV_HEADS, HEAD_DIM)
    dt = jnp.exp(jax.random.uniform(ks[12], (DEPTH, SSM_HEADS), f32) * (math.log(0.1) - math.log(0.001))
                 + math.log(0.001))
    page_table = jax.random.permutation(ks[10], n_pool)[:n_used].reshape(DEC_BATCH, n_pages).astype(jnp.int32)
    return {
        'x_prompt': nrm(ks[0], (BATCH, SEQ, D_MODEL), 1.0),
        'x_sample': nrm(ks[1], (DEC_BATCH, DEC_SEQ, D_MODEL), 1.0),
        'cache_k_cmp': nrm(ks[2], page_shape, 1.0),
        'cache_v_cmp': nrm(ks[3], page_shape, 1.0),
        'cache_k_slc': nrm(ks[4], page_shape, 1.0),
        'cache_v_slc': nrm(ks[5], page_shape, 1.0),
        'cache_k_win': nrm(ks[6], win_shape, 1.0),
        'cache_v_win': nrm(ks[7], win_shape, 1.0),
        'state_ssm': nrm(ks[8], (DEPTH, DEC_BATCH, SSM_HEADS, SSM_HEAD_DIM, D_STATE), 0.5),
        'state_conv': nrm(ks[9], (DEPTH, DEC_BATCH, CONV_W - 1, CONV_DIM), 1.0),
        'page_table': page_table,
        'w_in': nrm(ks[11], (DEPTH, D_MODEL, D_IN), D_MODEL ** -0.5),
        'b_gate': nrm(ks[13], (DEPTH, NSA_HEADS * N_BRANCH), 0.1),
        'conv_w': nrm(ks[14], (DEPTH, CONV_W, CONV_DIM), CONV_W ** -0.5),
        'conv_b': nrm(ks[15], (DEPTH, CONV_DIM), 0.02),
        'dt_bias': dt + jnp.log(-jnp.expm1(-dt)),
        'a_log': jnp.log(jax.random.uniform(ks[16], (DEPTH, SSM_HEADS), f32, 1.0, 16.0)),
        'd_skip': 1.0 + nrm(ks[17], (DEPTH, SSM_HEADS), 0.1),
        'ssm_norm_w': 1.0 + nrm(ks[18], (DEPTH, D_SSM), 0.02),
        'cmp_k_w1': nrm(ks[19], (DEPTH, CMP_BLOCK, HEAD_DIM, CMP_HIDDEN), (CMP_BLOCK * HEAD_DIM) ** -0.5),
        'cmp_k_w2': nrm(ks[20], (DEPTH, CMP_HIDDEN, HEAD_DIM), CMP_HIDDEN ** -0.5),
        'cmp_k_pe': nrm(ks[21], (DEPTH, CMP_BLOCK, HEAD_DIM), 0.1),
        'cmp_v_w1': nrm(ks[22], (DEPTH, CMP_BLOCK, HEAD_DIM, CMP_HIDDEN), (CMP_BLOCK * HEAD_DIM) ** -0.5),
        'cmp_v_w2': nrm(ks[23], (DEPTH, CMP_HIDDEN, HEAD_DIM), CMP_HIDDEN ** -0.5),
        'cmp_v_pe': nrm(ks[24], (DEPTH, CMP_BLOCK, HEAD_DIM), 0.1),
        'w_out': nrm(ks[25], (DEPTH, D_MIX, D_MODEL), BETA * D_MIX ** -0.5),
        'ln1_g': 1.0 + nrm(ks[26], (DEPTH, D_MODEL), 0.02),
        'ln1_b': nrm(ks[27], (DEPTH, D_MODEL), 0.02),
        'ln2_g': 1.0 + nrm(ks[28], (DEPTH, D_MODEL), 0.02),
        'ln2_b': nrm(ks[29], (DEPTH, D_MODEL), 0.02),
        'ln3_g': 1.0 + nrm(ks[30], (DEPTH, D_MODEL), 0.02),
        'ln3_b': nrm(ks[31], (DEPTH, D_MODEL), 0.02),
        'ffn1_w_gate': nrm(ks[32], (DEPTH, D_MODEL, D_FF), D_MODEL ** -0.5),
        'ffn1_w_up': nrm(ks[33], (DEPTH, D_MODEL, D_FF), D_MODEL ** -0.5),
        'ffn1_w_down': nrm(ks[34], (DEPTH, D_FF, D_MODEL), BETA * D_FF ** -0.5),
        'ffn2_w_gate': nrm(ks[35], (DEPTH, D_MODEL, D_FF), D_MODEL ** -0.5),
        'ffn2_w_up': nrm(ks[36], (DEPTH, D_MODEL, D_FF), D_MODEL ** -0.5),
        'ffn2_w_down': nrm(ks[37], (DEPTH, D_FF, D_MODEL), BETA * D_FF ** -0.5),
    }


def reference(x_prompt, x_sample, cache_k_cmp, cache_v_cmp, cache_k_slc, cache_v_slc, cache_k_win,
              cache_v_win, state_ssm, state_conv, page_table, w_in, b_gate, conv_w, conv_b, dt_bias, a_log,
              d_skip, ssm_norm_w, cmp_k_w1, cmp_k_w2, cmp_k_pe, cmp_v_w1, cmp_v_w2, cmp_v_pe, w_out,
              ln1_g, ln1_b, ln2_g, ln2_b, ln3_g, ln3_b, ffn1_w_gate, ffn1_w_up, ffn1_w_down,
              ffn2_w_gate, ffn2_w_up, ffn2_w_down):
    weights = dict(w_in=w_in, b_gate=b_gate, conv_w=conv_w, conv_b=conv_b, dt_bias=dt_bias, a_log=a_log,
                   d_skip=d_skip, ssm_norm_w=ssm_norm_w, cmp_k_w1=cmp_k_w1, cmp_k_w2=cmp_k_w2,
                   cmp_k_pe=cmp_k_pe, cmp_v_w1=cmp_v_w1, cmp_v_w2=cmp_v_w2, cmp_v_pe=cmp_v_pe, w_out=w_out,
                   ln1_g=ln1_g, ln1_b=ln1_b, ln2_g=ln2_g, ln2_b=ln2_b, ln3_g=ln3_g, ln3_b=ln3_b,
                   ffn1_w_gate=ffn1_w_gate, ffn1_w_up=ffn1_w_up, ffn1_w_down=ffn1_w_down,
                   ffn2_w_gate=ffn2_w_gate, ffn2_w_up=ffn2_w_up, ffn2_w_down=ffn2_w_down)
    y_p, y_s = x_prompt, x_sample
    p_states, s_states = [], []
    for l in range(DEPTH):
        lw = {name: arr[l] for name, arr in weights.items()}
        y_p, st_p = layer_forward(y_p, lw, mix_prompt)
        y_s, st_s = layer_forward(y_s, lw, mix_sample, l, cache_k_cmp, cache_v_cmp, cache_k_slc, cache_v_slc,
                                  cache_k_win, cache_v_win, state_ssm, state_conv, page_table)
        p_states.append(st_p)
        s_states.append(st_s)
    p_kc, p_vc, p_ks, p_vs, p_kw, p_vw, p_ssm, p_conv = [jnp.stack(a) for a in zip(*p_states)]
    s_kc, s_vc, s_ks, s_vs, s_kw, s_vw, s_ssm, s_conv = [jnp.stack(a) for a in zip(*s_states)]
    return (y_p, y_s, p_kc, s_kc, p_vc, s_vc, p_ks, s_ks, p_vs, s_vs, p_kw, s_kw, p_vw, s_vw,
            p_ssm, s_ssm, p_conv, s_conv)
```

```python
import os
import numpy as np
from contextlib import ExitStack
import concourse.bass as bass
import concourse.mybir as mybir
from concourse.bass_utils import run_bass_kernel_spmd

F32 = mybir.dt.float32
BF16 = mybir.dt.bfloat16
I32 = mybir.dt.int32
AF = mybir.ActivationFunctionType
ALU = mybir.AluOpType
AX = mybir.AxisListType
AP = bass.AP

NCORES = 8
D = 1024
DFF = 2816
NJ = DFF // 128
SEQ = 4096
NSEQ_S = 16
TS = 4
NTOK = SEQ + NSEQ_S * TS
DIN = 2592
N_POOL = int(os.environ.get('KN_POOL', '10240'))
ALPHA = 2.0 ** 0.25
LN_EPS = 1e-5
O_Z, O_XBC, O_DT, O_Q, O_KV, O_GATE = 0, 512, 1280, 1288, 1800, 2568

N_DMA_SEMS = 40
SYNC_SAME_ENGINE = True


PSUM_NAMES = {"pT", "pGU", "pD", "pP", "pA", "pM", "pC", "pY", "pY0", "pS", "pO", "pPT"}
PSUM_ALIAS = {"pA0": "pA", "pA1": "pA", "pO0": ("pO", 0)}


def _norm_key(k):
    if isinstance(k, str) and k in PSUM_ALIAS:
        return PSUM_ALIAS[k]
    if isinstance(k, tuple) and k[0] == "pTq":
        return "pT"
    return k


def _is_psum_key(k):
    n = k[0] if isinstance(k, tuple) else k
    return isinstance(n, str) and n in PSUM_NAMES


class _Op:
    __slots__ = ("eng", "fn", "dma", "deps", "signals", "sem", "val", "prewait")

    def __init__(self, eng, fn, dma):
        self.eng = eng
        self.fn = fn
        self.dma = dma
        self.deps = []
        self.signals = False
        self.sem = None
        self.val = None
        self.prewait = None


class Prog:
    ENGS = ("pe", "act", "dve", "pool", "sp")

    def __init__(self, nc, stack):
        self.nc = nc
        self.esem = {e: stack.enter_context(nc.semaphore("es_" + e)) for e in ("pe", "act", "dve", "pool")}
        self.ecount = {e: 0 for e in self.esem}
        self.dsems = [stack.enter_context(nc.semaphore("ds%d" % i)) for i in range(N_DMA_SEMS)]
        self.dcount = [0] * N_DMA_SEMS
        self.dlast = [None] * N_DMA_SEMS
        self.ndma = 0
        self.waited = {e: {} for e in self.ENGS}
        self.n_inst = 0
        self._reset_phase()

    def _reset_phase(self):
        self.ops = []
        self.last_w = {}
        self.readers = {}

    def op(self, eng, fn, r=(), w=(), dma=False):
        r = [_norm_key(k) for k in r]
        w = [_norm_key(k) for k in w]
        w = w + [k for k in r if _is_psum_key(k) and k not in w]
        o = _Op(eng, fn, dma)
        deps = []
        for k in r:
            lw = self.last_w.get(k)
            if lw is not None:
                deps.append(lw)
        for k in w:
            lw = self.last_w.get(k)
            if lw is not None:
                deps.append(lw)
            deps.extend(self.readers.get(k, ()))
        seen = set()
        for d in deps:
            if id(d) in seen:
                continue
            seen.add(id(d))
            if (not d.dma) and d.eng == eng and (not dma) and (eng == "pe" or not SYNC_SAME_ENGINE):
                continue
            o.deps.append(d)
            d.signals = True
        if dma:
            o.signals = True
            s = self.ndma % N_DMA_SEMS
            self.ndma += 1
            o.prewait = self.dlast[s]
            self.dcount[s] += 1
            o.sem = self.dsems[s]
            o.val = 16 * self.dcount[s]
            self.dlast[s] = o
        for k in r:
            self.readers.setdefault(k, []).append(o)
        for k in w:
            self.last_w[k] = o
            self.readers[k] = []
        self.ops.append(o)
        return o

    def dma(self, eng, out, in_, r=(), w=(), **kw):
        return self.op(eng, lambda e: e.dma_start(out=out, in_=in_, **kw), r=r, w=w, dma=True)

    def emit_phase(self):
        nc = self.nc
        by_eng = {e: [o for o in self.ops if o.eng == e] for e in self.ENGS}
        for e in self.esem:
            lst = [o for o in by_eng[e] if not o.dma]
            if lst:
                lst[-1].signals = True
        for o in self.ops:
            if not o.dma and o.signals:
                self.ecount[o.eng] += 1
                o.sem = self.esem[o.eng]
                o.val = self.ecount[o.eng]
        targets = [(self.esem[e], self.ecount[e]) for e in self.esem if self.ecount[e] > 0]
        targets += [(self.dsems[i], 16 * self.dcount[i]) for i in range(N_DMA_SEMS) if self.dcount[i] > 0]
        prog = self

        def emit_engine(ename, eng):
            waited = prog.waited[ename]

            def wait(sem, val):
                key = sem.num
                if waited.get(key, 0) >= val:
                    return
                waited[key] = val
                eng.wait_ge(sem, val)
                prog.n_inst += 1

            for o in by_eng[ename]:
                if o.prewait is not None:
                    wait(o.prewait.sem, o.prewait.val)
                for d in o.deps:
                    wait(d.sem, d.val)
                inst = o.fn(eng)
                prog.n_inst += 1
                if o.signals:
                    inst.then_inc(o.sem, 16 if o.dma else 1)
            for (s, v) in targets:
                wait(s, v)

        with nc.Block() as block:
            @block.sync
            def _(e):
                emit_engine("sp", e)

            @block.tensor
            def _(e):
                emit_engine("pe", e)

            @block.scalar
            def _(e):
                emit_engine("act", e)

            @block.vector
            def _(e):
                emit_engine("dve", e)

            @block.gpsimd
            def _(e):
                emit_engine("pool", e)
        self._reset_phase()


def bcast(ap, dims):
    return AP(ap.tensor, ap.offset, [list(ap.ap[0])] + [list(d) for d in dims])


TILES = [(i * 128, 128) for i in range(32)] + [(SEQ, 64)]
GROUPS = [[TILES[2 * g], TILES[2 * g + 1]] for g in range(16)] + [[TILES[32]]]


class K:
    def __init__(self):
        self.nc = bass.Bass("TRN2", target_bir_lowering=False)
        self.ins = {}
        self.outs = {}

    def din(self, name, shape, dt=F32):
        t = self.nc.dram_tensor(name, list(shape), dt, kind="ExternalInput").ap()
        self.ins[name] = t
        return t

    def dout(self, name, shape, dt=F32):
        t = self.nc.dram_tensor(name, list(shape), dt, kind="ExternalOutput").ap()
        self.outs[name] = t
        return t

    def dscr(self, name, shape, dt=F32):
        return self.nc.dram_tensor(name, list(shape), dt, kind="Internal").ap()


def layer_norm_tile(P, sb, z, nr, gt, bt, out, key_z, key_out, tag):
    stats, mv, rstd, nb = sb["stats"], sb["mv"], sb["rstd"], sb["nb"]
    for c in range(2):
        P.op("dve", lambda e, c=c: e.bn_stats(out=stats[:nr, c, :], in_=z[:nr, c * 512:(c + 1) * 512]),
             r=[key_z], w=["stats"])
    P.op("dve", lambda e: e.bn_aggr(out=mv[:nr, :], in_=stats[:nr, :, :]), r=["stats"], w=["mv"])
    P.op("act", lambda e: e.activation(out=rstd[:nr, :], in_=mv[:nr, 1:2], func=AF.Sqrt, bias=sb["eps"][:nr, :], scale=1.0),
         r=["mv"], w=["rstd"])
    P.op("dve", lambda e: e.reciprocal(out=rstd[:nr, :], in_=rstd[:nr, :]), r=["rstd"], w=["rstd"])
    P.op("dve", lambda e: e.scalar_tensor_tensor(out=nb[:nr, :], in0=mv[:nr, 0:1], scalar=-1.0, in1=rstd[:nr, :],
                                                 op0=ALU.mult, op1=ALU.mult), r=["mv", "rstd"], w=["nb"])
    P.op("act", lambda e: e.activation(out=out[:nr, :], in_=z[:nr, :], func=AF.Identity, bias=nb[:nr, 0:1], scale=rstd[:nr, 0:1]),
         r=[key_z, "nb", "rstd"], w=[key_out])
    P.op("pool", lambda e: e.tensor_tensor(out=out[:nr, :], in0=out[:nr, :], in1=gt[:nr, :], op=ALU.mult),
         r=[key_out, tag + "g"], w=[key_out])
    P.op("pool", lambda e: e.tensor_tensor(out=out[:nr, :], in0=out[:nr, :], in1=bt[:nr, :], op=ALU.add),
         r=[key_out, tag + "b"], w=[key_out])


def transpose_tile(P, src, nr, key_src, pT, dst_fn, key_dst, ident, evac_engs=("act", "dve")):
    for kh in range(2):
        for kk in range(4):
            k = kh * 4 + kk
            P.op("pe", lambda e, kk=kk, k=k, kh=kh: e.transpose(out=pT[kh][:, kk * 128:kk * 128 + nr],
                                                                 in_=src[:nr, k * 128:(k + 1) * 128], identity=ident[:nr, :nr]),
                 r=[key_src, "ident"], w=[("pT", kh)])
        eng = evac_engs[kh % len(evac_engs)]
        src_ap = pT[kh][:, :].rearrange("p (k t) -> p k t", k=4)[:, :, :nr]
        if eng == "act":
            P.op("act", lambda e, kh=kh, src_ap=src_ap: e.activation(out=dst_fn(kh), in_=src_ap, func=AF.Copy),
                 r=[("pT", kh)], w=[key_dst])
        else:
            P.op(eng, lambda e, kh=kh, src_ap=src_ap: e.tensor_copy(out=dst_fn(kh), in_=src_ap),
                 r=[("pT", kh)], w=[key_dst])


def ffn_phase(kk, P, tag, w_gate, w_up, w_down, ln_g, ln_b, src, dst, ident_d, prologue=None):
    nc = kk.nc
    with ExitStack() as st:
        def sbt(n, s, d=F32):
            return st.enter_context(nc.sbuf_tensor(tag + n, s, d))
        Wg = sbt("Wg", [128, 8, DFF], BF16)
        Wu = sbt("Wu", [128, 8, DFF], BF16)
        Wd = sbt("Wd", [128, NJ, D], BF16)
        ident = sbt("ident", [128, 128])
        gt = sbt("gt", [128, D])
        bt = sbt("bt", [128, D])
        xs = [[sbt("xs%d%d" % (s, t), [128, D]) for t in range(2)] for s in range(2)]
        xT = [sbt("xT%d" % s, [128, 8, 256], BF16) for s in range(2)]
        sg = [sbt("sg%d" % s, [128, 256]) for s in range(2)]
        hT = [sbt("hT%d" % s, [128, 256], BF16) for s in range(3)]
        zt = [sbt("zt%d" % s, [128, D]) for s in range(2)]
        ot = [sbt("ot%d" % s, [128, D]) for s in range(2)]
        sb = dict(stats=sbt("stats", [128, 2, 6]), mv=sbt("mv", [128, 2]), rstd=sbt("rstd", [128, 1]),
                  nb=sbt("nb", [128, 1]), eps=sbt("eps", [128, 1]))
        pT = [st.enter_context(nc.psum_tensor(tag + "pT%d" % i, [128, 512], F32)) for i in range(2)]
        pGU = [st.enter_context(nc.psum_tensor(tag + "pGU%d" % i, [128, 512], F32)) for i in range(2)]
        pD = [[st.enter_context(nc.psum_tensor(tag + "pD%d%d" % (t, h), [128, 512], F32)) for h in range(2)] for t in range(2)]
        extra = prologue.alloc(st) if prologue is not None else None

        P.op("pool", lambda e: e.memset(sb["eps"][:, :], LN_EPS), w=["eps"])
        P.dma("sp", ident[:, :], ident_d[:, :], w=["ident"])
        P.dma("sp", gt[:, :], ln_g[0:1, :].partition_broadcast(128), w=[tag + "g"])
        P.dma("sp", bt[:, :], ln_b[0:1, :].partition_broadcast(128), w=[tag + "b"])
        if prologue is not None:
            prologue.load_consts(P, extra)
        for k in range(8):
            P.dma("pool", Wg[:, k, :], w_gate[k * 128:(k + 1) * 128, :], w=["Wg"])
            P.dma("pool", Wu[:, k, :], w_up[k * 128:(k + 1) * 128, :], w=["Wu"])
        for j in range(NJ):
            P.dma("pool", Wd[:, j, :], w_down[j * 128:(j + 1) * 128, :], w=["Wd"])

        def load_group(G):
            slot = G % 2
            for ti, (r0, nr) in enumerate(GROUPS[G]):
                key = ("xs", slot, ti)
                if prologue is None:
                    P.dma("sp", xs[slot][ti][:nr, :], src[r0:r0 + nr, :], w=[key])
                else:
                    prologue.emit(P, extra, slot, ti, r0, nr, xs[slot][ti], key, pT, ident)
                transpose_tile(P, xs[slot][ti], nr, key, pT,
                               lambda kh, slot=slot, ti=ti, nr=nr: xT[slot][:, kh * 4:(kh + 1) * 4, ti * 128:ti * 128 + nr],
                               ("xT", slot), ident)
                P.op("pool", lambda e, slot=slot, ti=ti, nr=nr: e.tensor_scalar(out=xs[slot][ti][:nr, :], in0=xs[slot][ti][:nr, :],
                                                                                 scalar1=ALPHA, scalar2=None, op0=ALU.mult),
                     r=[key], w=[key])

        def gu(G, j):
            slot = G % 2
            ntok = sum(nr for _, nr in GROUPS[G])
            b = j % 2
            for (W, off, wk) in ((Wg, 0, "Wg"), (Wu, 256, "Wu")):
                for k in range(8):
                    P.op("pe", lambda e, W=W, off=off, k=k: e.matmul(pGU[b][:, off:off + ntok], lhsT=W[:, k, j * 128:(j + 1) * 128],
                                                                      rhs=xT[slot][:, k, :ntok], start=(k == 0), stop=(k == 7)),
                         r=[("xT", slot), wk], w=[("pGU", b)])
            P.op("act", lambda e: e.activation(out=sg[b][:, :ntok], in_=pGU[b][:, 0:ntok], func=AF.Silu),
                 r=[("pGU", b)], w=[("sg", b)])
            h3 = j % 3
            P.op("dve", lambda e: e.tensor_tensor(out=hT[h3][:, :ntok], in0=sg[b][:, :ntok], in1=pGU[b][:, 256:256 + ntok], op=ALU.mult),
                 r=[("sg", b), ("pGU", b)], w=[("hT", h3)])

        def down(G, j):
            h3 = j % 3
            for ti, (r0, nr) in enumerate(GROUPS[G]):
                for half in range(2):
                    P.op("pe", lambda e, ti=ti, nr=nr, half=half: e.matmul(pD[ti][half][:nr, :], lhsT=hT[h3][:, ti * 128:ti * 128 + nr],
                                                                            rhs=Wd[:, j, half * 512:(half + 1) * 512],
                                                                            start=(j == 0), stop=(j == NJ - 1)),
                         r=[("hT", h3), "Wd"], w=[("pD", ti, half)])

        def epilogue(G):
            slot = G % 2
            for ti, (r0, nr) in enumerate(GROUPS[G]):
                key = ("xs", slot, ti)
                zs = (G * 2 + ti) % 2
                for half in range(2):
                    P.op("dve", lambda e, ti=ti, nr=nr, half=half, zs=zs: e.scalar_tensor_tensor(
                        out=zt[zs][:nr, half * 512:(half + 1) * 512], in0=pD[ti][half][:nr, :], scalar=0.5,
                        in1=xs[slot][ti][:nr, half * 512:(half + 1) * 512], op0=ALU.mult, op1=ALU.add),
                        r=[("pD", ti, half), key], w=[("zt", zs)])
                layer_norm_tile(P, sb, zt[zs], nr, gt, bt, ot[zs], ("zt", zs), ("ot", zs), tag)
                P.dma("sp", dst[r0:r0 + nr, :], ot[zs][:nr, :], r=[("ot", zs)], w=[("dst", r0)])

        nG = len(GROUPS)
        load_group(0)
        for G in range(nG):
            for j in range(NJ):
                gu(G, j)
                if j > 0:
                    down(G, j - 1)
                if j == 8 and G + 1 < nG:
                    load_group(G + 1)
            down(G, NJ - 1)
            epilogue(G)
        P.emit_phase()


def inproj_phase(kk, P, h1, projd, kv_out, w_in, b_gate, cos_d, sin_d, ident_d):
    nc = kk.nc
    with ExitStack() as st:
        def sbt(n, s, d=F32):
            return st.enter_context(nc.sbuf_tensor("ip_" + n, s, d))
        Win = sbt("Win", [128, 8, DIN], BF16)
        ident = sbt("ident", [128, 128])
        cosT = sbt("cos", [128, 33, 32])
        sinT = sbt("sin", [128, 33, 32])
        bg = sbt("bg", [128, 24])
        ht = [sbt("ht%d" % s, [128, D]) for s in range(2)]
        hT = [sbt("hT%d" % s, [128, 8, 128], BF16) for s in range(2)]
        proj = [sbt("proj%d" % s, [128, DIN]) for s in range(2)]
        tmps = {"dve": [sbt("tmpa%d" % i, [128, 8, 32]) for i in range(4)],
                "pool": [sbt("tmpb%d" % i, [128, 8, 32]) for i in range(4)]}
        pT = [st.enter_context(nc.psum_tensor("ip_pT%d" % i, [128, 512], F32)) for i in range(2)]
        pP = [st.enter_context(nc.psum_tensor("ip_pP%d" % i, [128, 512], F32)) for i in range(6)]
        P.dma("sp", ident[:, :], ident_d[:, :], w=["ident"])
        P.dma("sp", cosT[:, 0:32, :], cos_d[0:SEQ, :].rearrange("(t p) d -> p t d", p=128), w=["cos"])
        P.dma("sp", sinT[:, 0:32, :], sin_d[0:SEQ, :].rearrange("(t p) d -> p t d", p=128), w=["sin"])
        P.dma("sp", cosT[0:64, 32, :], cos_d[SEQ:NTOK, :], w=["cos"])
        P.dma("sp", sinT[0:64, 32, :], sin_d[SEQ:NTOK, :], w=["sin"])
        P.dma("sp", bg[:, :], b_gate[0:1, :].partition_broadcast(128), w=["bg"])
        for k in range(8):
            P.dma("pool", Win[:, k, :], w_in[k * 128:(k + 1) * 128, :], w=["Win"])
        chunks = [(0, 512), (512, 512), (1024, 264), (1288, 512), (1800, 512), (2312, 280)]
        for t, (r0, nr) in enumerate(TILES):
            s = t % 2
            P.dma("sp", ht[s][:nr, :], h1[r0:r0 + nr, :], w=[("ht", s)])
            transpose_tile(P, ht[s], nr, ("ht", s), pT, lambda kh, s=s, nr=nr: hT[s][:, kh * 4:(kh + 1) * 4, :nr], ("hT", s), ident)
            for ci, (c0, wd) in enumerate(chunks):
                for k in range(8):
                    P.op("pe", lambda e, ci=ci, c0=c0, wd=wd, k=k, s=s, nr=nr: e.matmul(
                        pP[ci][:nr, :wd], lhsT=hT[s][:, k, :nr], rhs=Win[:, k, c0:c0 + wd], start=(k == 0), stop=(k == 7)),
                        r=[("hT", s), "Win"], w=[("pP", ci)])
                if ci % 2 == 0:
                    P.op("act", lambda e, ci=ci, c0=c0, wd=wd, s=s, nr=nr: e.activation(out=proj[s][:nr, c0:c0 + wd], in_=pP[ci][:nr, :wd], func=AF.Copy),
                         r=[("pP", ci)], w=[("proj", s, ci)])
                else:
                    P.op("dve", lambda e, ci=ci, c0=c0, wd=wd, s=s, nr=nr: e.tensor_copy(out=proj[s][:nr, c0:c0 + wd], in_=pP[ci][:nr, :wd]),
                         r=[("pP", ci)], w=[("proj", s, ci)])
            groups = [(O_Q, 8, 3), (O_KV, 2, 4), (O_KV + 256, 2, 4), (O_KV + 512, 2, 5)]
            for gi, (c0, H, ci) in enumerate(groups):
                eng = "dve" if gi % 2 == 0 else "pool"
                X = proj[s][:nr, c0:c0 + H * 64].rearrange("p (h d) -> p h d", h=H)
                x1 = X[:, :, 0:32]
                x2 = X[:, :, 32:64]
                cb = bcast(cosT[:nr, t, :], [[0, H], [1, 32]])
                sn = bcast(sinT[:nr, t, :], [[0, H], [1, 32]])
                key = ("proj", s, ci)
                t1, t2, t3, t4 = [tmps[eng][i][:nr, 0:H, :] for i in range(4)]
                tk = ("tmp", eng)
                P.op(eng, lambda e, t1=t1, x1=x1, cb=cb: e.tensor_tensor(out=t1, in0=x1, in1=cb, op=ALU.mult), r=[key, "cos"], w=[tk])
                P.op(eng, lambda e, t2=t2, x2=x2, sn=sn: e.tensor_tensor(out=t2, in0=x2, in1=sn, op=ALU.mult), r=[key, "sin"], w=[tk])
                P.op(eng, lambda e, t3=t3, x2=x2, cb=cb: e.tensor_tensor(out=t3, in0=x2, in1=cb, op=ALU.mult), r=[key, "cos"], w=[tk])
                P.op(eng, lambda e, t4=t4, x1=x1, sn=sn: e.tensor_tensor(out=t4, in0=x1, in1=sn, op=ALU.mult), r=[key, "sin"], w=[tk])
                P.op(eng, lambda e, t1=t1, t2=t2, x1=x1: e.tensor_tensor(out=x1, in0=t1, in1=t2, op=ALU.subtract), r=[tk], w=[key])
                P.op(eng, lambda e, t3=t3, t4=t4, x2=x2: e.tensor_tensor(out=x2, in0=t3, in1=t4, op=ALU.add), r=[tk], w=[key])
            P.op("dve", lambda e, s=s, nr=nr: e.tensor_tensor(out=proj[s][:nr, O_GATE:O_GATE + 24], in0=proj[s][:nr, O_GATE:O_GATE + 24],
                                                              in1=bg[:nr, :], op=ALU.add), r=[("proj", s, 5), "bg"], w=[("proj", s, 5)])
            allk = [("proj", s, ci) for ci in range(6)]
            P.dma("sp", projd[r0:r0 + nr, :], proj[s][:nr, :], r=allk, w=[("projd", t)])
            P.dma("sp", kv_out[r0:r0 + nr, :], proj[s][:nr, O_KV:O_KV + 768], r=allk, w=[("kvo", t)])
        P.emit_phase()


def conv_silu(P, nr, xsh, s, cw, cb, acc, acc2, tP, tD, xa):
    kA, kB, kX = ("acc", s), ("acc2", s), ("xa", s)
    P.op("pool", lambda e: e.tensor_tensor(out=acc[s][:nr, :], in0=xsh[0][s][:nr, :], in1=cw[:nr, 0, :], op=ALU.mult), r=[("xsh", 0, s), "cw"], w=[kA])
    P.op("pool", lambda e: e.tensor_tensor(out=tP[:nr, :], in0=xsh[1][s][:nr, :], in1=cw[:nr, 1, :], op=ALU.mult), r=[("xsh", 1, s), "cw"], w=["tP"])
    P.op("pool", lambda e: e.tensor_tensor(out=acc[s][:nr, :], in0=acc[s][:nr, :], in1=tP[:nr, :], op=ALU.add), r=[kA, "tP"], w=[kA])
    P.op("dve", lambda e: e.tensor_tensor(out=acc2[s][:nr, :], in0=xsh[2][s][:nr, :], in1=cw[:nr, 2, :], op=ALU.mult), r=[("xsh", 2, s), "cw"], w=[kB])
    P.op("dve", lambda e: e.tensor_tensor(out=tD[:nr, :], in0=xsh[3][s][:nr, :], in1=cw[:nr, 3, :], op=ALU.mult), r=[("xsh", 3, s), "cw"], w=["tD"])
    P.op("dve", lambda e: e.tensor_tensor(out=acc2[s][:nr, :], in0=acc2[s][:nr, :], in1=tD[:nr, :], op=ALU.add), r=[kB, "tD"], w=[kB])
    P.op("dve", lambda e: e.tensor_tensor(out=acc2[s][:nr, :], in0=acc2[s][:nr, :], in1=cb[:nr, :], op=ALU.add), r=[kB, "cb"], w=[kB])
    P.op("dve", lambda e: e.tensor_tensor(out=acc2[s][:nr, :], in0=acc2[s][:nr, :], in1=acc[s][:nr, :], op=ALU.add), r=[kB, kA], w=[kB])
    P.op("act", lambda e: e.activation(out=xa[s][:nr, :], in_=acc2[s][:nr, :], func=AF.Silu), r=[kB], w=[kX])


def softplus_dt(P, nr, dtr, s, dtb, aneg, dt, dA):
    P.op("dve", lambda e: e.tensor_tensor(out=dtr[s][:nr, :], in0=dtr[s][:nr, :], in1=dtb[:nr, :], op=ALU.add), r=[("dtr", s), "dtb"], w=[("dtr", s)])
    P.op("act", lambda e: e.activation(out=dtr[s][:nr, :], in_=dtr[s][:nr, :], func=AF.Exp), r=[("dtr", s)], w=[("dtr", s)])
    P.op("act", lambda e: e.activation(out=dt[s][:nr, :], in_=dtr[s][:nr, :], func=AF.Ln, bias=1.0, scale=1.0), r=[("dtr", s)], w=[("dt", s)])
    P.op("dve", lambda e: e.tensor_tensor(out=dA[s][:nr, :], in0=dt[s][:nr, :], in1=aneg[:nr, :], op=ALU.mult), r=[("dt", s), "aneg"], w=[("dA", s)])


def gate_norm_store(P, nr, yt, ky, xa_x, kx, zt, kz, dsk, nw, xd, sz, junk, ss, rstd, eps, dst_ap):
    P.op("pool", lambda e: e.tensor_tensor(out=xd[:nr, :].rearrange("p (h d) -> p h d", h=8), in0=xa_x.rearrange("p (h d) -> p h d", h=8),
                                           in1=bcast(dsk[:nr, :], [[1, 8], [0, 64]]), op=ALU.mult), r=[kx, "dsk"], w=["xd"])
    P.op("pool", lambda e: e.tensor_tensor(out=yt[:nr, :], in0=yt[:nr, :], in1=xd[:nr, :], op=ALU.add), r=[ky, "xd"], w=[ky])
    P.op("act", lambda e: e.activation(out=sz[:nr, :], in_=zt[:nr, :], func=AF.Silu), r=[kz], w=["sz"])
    P.op("pool", lambda e: e.tensor_tensor(out=yt[:nr, :], in0=yt[:nr, :], in1=sz[:nr, :], op=ALU.mult), r=[ky, "sz"], w=[ky])
    P.op("pool", lambda e: e.memset(ss[:nr, :], 0.0), w=["ss"])
    P.op("act", lambda e: e.activation(out=junk[:nr, :], in_=yt[:nr, :], func=AF.Square, accum_out=ss[:nr, :]), r=[ky, "ss"], w=["junk", "ss"])
    P.op("act", lambda e: e.activation(out=rstd[:nr, :], in_=ss[:nr, :], func=AF.Sqrt, bias=eps[:nr, :], scale=1.0 / 512.0), r=["ss", "eps"], w=["rstd"])
    P.op("dve", lambda e: e.reciprocal(out=rstd[:nr, :], in_=rstd[:nr, :]), r=["rstd"], w=["rstd"])
    P.op("dve", lambda e: e.scalar_tensor_tensor(out=yt[:nr, :], in0=yt[:nr, :], scalar=rstd[:nr, 0:1], in1=nw[:nr, :], op0=ALU.mult, op1=ALU.mult),
         r=[ky, "rstd", "nw"], w=[ky])
    P.dma("sp", dst_ap, yt[:nr, :], r=[ky])


def ssd_phase(kk, P, projd, mixd, p_ssm, s_ssm, state_ssm, state_conv, conv_w, conv_b, dt_bias, a_log, d_skip, ssm_norm_w, consts_d):
    nc = kk.nc
    hist_s = kk.dscr("hist_s", [112, 768])
    xs_d = kk.dscr("xs_d", [64, 512])
    bc_d = kk.dscr("bc_d", [64, 256])
    dts_d = kk.dscr("dts_d", [64, 16])
    ys_d = kk.dscr("ys_d", [64, 512])
    XBC = slice(O_XBC, O_XBC + 768)
    with ExitStack() as st:
        def sbt(n, s, d=F32):
            return st.enter_context(nc.sbuf_tensor("sd_" + n, s, d))
        cst = sbt("cst", [128, 514])
        ident, U, Lt, ones = cst[:, 0:128], cst[:, 128:256], cst[:, 256:384], cst[:, 384:512]
        rmask = cst[:, 512:514]
        cw = sbt("cw", [128, 4, 768]); cb = sbt("cb", [128, 768])
        dtb = sbt("dtb", [128, 8]); aneg = sbt("aneg", [128, 8]); dsk = sbt("dsk", [128, 8]); nw = sbt("nw", [128, 512])
        eps = sbt("eps", [128, 1])
        xsh = [[sbt("xsh%d%d" % (i, s), [128, 768]) for s in range(2)] for i in range(4)]
        zt = [sbt("zt%d" % s, [128, 512]) for s in range(2)]
        acc = [sbt("acc%d" % s, [128, 768]) for s in range(2)]
        acc2 = [sbt("acc2%d" % s, [128, 768]) for s in range(2)]
        xa = [sbt("xa%d" % s, [128, 768]) for s in range(2)]
        tP = sbt("tP", [128, 768]); tD = sbt("tD", [128, 768])
        dtr = [sbt("dtr%d" % s, [128, 8]) for s in range(2)]
        dt = [sbt("dt%d" % s, [128, 8]) for s in range(2)]
        dA = [sbt("dA%d" % s, [128, 8]) for s in range(2)]
        xdt = [sbt("xdt%d" % s, [128, 512], BF16) for s in range(2)]
        Bbf = [sbt("Bbf%d" % s, [128, 128], BF16) for s in range(2)]
        BCT = [[sbt("BCT%d%d" % (s, g), [128, 256], BF16) for g in range(2)] for s in range(2)]
        LdA = [sbt("LdA%d" % s, [128, 8, 128]) for s in range(2)]
        dec = [sbt("dec%d" % s, [128, 8, 128]) for s in range(2)]
        CBm = [sbt("CBm%d" % s, [128, 2, 128]) for s in range(2)]
        WT = [sbt("WT%d" % s, [128, 8, 128], BF16) for s in range(2)]
        expcum = [sbt("expcum%d" % s, [128, 8]) for s in range(2)]
        Edec = sbt("Edec", [128, 4]); dtt = sbt("dtt", [128, 8]); xdtt = sbt("xdtt", [128, 512], BF16)
        ST = sbt("ST", [128, 256]); STb = sbt("STb", [128, 256], BF16); tmpS = sbt("tmpS", [128, 256])
        yt = [sbt("yt%d" % s, [128, 512]) for s in range(2)]
        xd = sbt("xd", [128, 512]); sz = sbt("sz", [128, 512]); junk = sbt("junk", [128, 512])
        ss = sbt("ss", [128, 1]); rstd = sbt("rstd", [128, 1])
        stT = sbt("stT", [128, 4, 64])
        pA = st.enter_context(nc.psum_tensor("sd_pA", [128, 512], F32))
        pM = [st.enter_context(nc.psum_tensor("sd_pM%d" % i, [128, 512], F32)) for i in range(2)]
        pY = st.enter_context(nc.psum_tensor("sd_pY", [128, 512], F32))
        pY0 = st.enter_context(nc.psum_tensor("sd_pY0", [128, 512], F32))
        pC = st.enter_context(nc.psum_tensor("sd_pC", [128, 512], F32))
        pS = st.enter_context(nc.psum_tensor("sd_pS", [128, 512], F32))

        P.dma("sp", cst[:, :], consts_d[:, 0:514], w=["cst"])
        SKIP = os.environ.get("SSD_SKIP", "")
        if "b" not in SKIP:
            P.dma("sp", cw[:, :, :].rearrange("p a b -> p (a b)"), conv_w.rearrange("(o a) b -> o (a b)", o=1).partition_broadcast(128), w=["cw"])
        P.dma("sp", cb[:, :], conv_b[0:1, :].partition_broadcast(128), w=["cb"])
        P.dma("sp", dtb[:, :], dt_bias[0:1, :].partition_broadcast(128), w=["dtb"])
        P.dma("sp", aneg[:, :], a_log[0:1, :].partition_broadcast(128), w=["aneg"])
        P.dma("sp", dsk[:, :], d_skip[0:1, :].partition_broadcast(128), w=["dsk"])
        P.dma("sp", nw[:, :], ssm_norm_w[0:1, :].partition_broadcast(128), w=["nw"])
        P.op("pool", lambda e: e.memset(eps[:, :], 1e-5), w=["eps"])
        P.op("act", lambda e: e.activation(out=aneg[:, :], in_=aneg[:, :], func=AF.Exp), r=["aneg"], w=["aneg"])
        P.op("dve", lambda e: e.tensor_scalar(out=aneg[:, :], in0=aneg[:, :], scalar1=-1.0, scalar2=None, op0=ALU.mult), r=["aneg"], w=["aneg"])
        P.op("pool", lambda e: e.memset(ST[:, :], 0.0), w=["ST"])
        P.op("pool", lambda e: e.memset(STb[:, :], 0.0), w=["STb"])
        if "c" not in SKIP:
            P.dma("act", hist_s[0:48, :].rearrange("(j s) f -> s j f", j=3), state_conv[:, :, :], w=["hist"])
            P.dma("act", hist_s[48:112, :], projd[SEQ:NTOK, XBC], w=["hist2"])

        NCH = int(os.environ.get("SSD_NCH", "32"))
        DO_S = os.environ.get("SSD_SAMPLE", "1") == "1"
        for c in range(NCH):
            s = c % 2
            r0 = c * 128
            for i in range(4):
                key = ("xsh", i, s)
                off = 3 - i
                if c == 0 and off > 0:
                    P.op("pool", lambda e, i=i, s=s: e.memset(xsh[i][s][:, :], 0.0), w=[key])
                    P.dma("sp", xsh[i][s][off:128, :], projd[0:128 - off, XBC], w=[key])
                else:
                    P.dma("sp", xsh[i][s][:, :], projd[r0 - off:r0 - off + 128, XBC], w=[key])
            P.dma("sp", zt[s][:, :], projd[r0:r0 + 128, 0:512], w=[("zt", s)])
            P.dma("sp", dtr[s][:, :], projd[r0:r0 + 128, O_DT:O_DT + 8], w=[("dtr", s)])
            conv_silu(P, 128, xsh, s, cw, cb, acc, acc2, tP, tD, xa)
            softplus_dt(P, 128, dtr, s, dtb, aneg, dt, dA)
            kX = ("xa", s)
            xv = xa[s][:, 0:512].rearrange("p (h d) -> p h d", h=8)
            P.op("dve", lambda e, s=s, xv=xv: e.tensor_tensor(out=xdt[s][:, :].rearrange("p (h d) -> p h d", h=8), in0=xv,
                                                              in1=bcast(dt[s][:, :], [[1, 8], [0, 64]]), op=ALU.mult), r=[kX, ("dt", s)], w=[("xdt", s)])
            P.op("pool", lambda e, s=s: e.tensor_copy(out=Bbf[s][:, :], in_=xa[s][:, 512:640]), r=[kX], w=[("Bbf", s)])
            P.op("pe", lambda e, s=s: e.transpose(out=pA[:, 0:128], in_=xa[s][:, 512:640], identity=ident), r=[kX, "cst"], w=["pA0"])
            P.op("pe", lambda e, s=s: e.transpose(out=pA[:, 128:256], in_=xa[s][:, 640:768], identity=ident), r=[kX, "cst"], w=["pA0"])
            for g in range(2):
                P.op("act", lambda e, s=s, g=g: e.activation(out=BCT[s][g][:, :], in_=pA[:, 0:256], func=AF.Copy, scale=rmask[:, g:g + 1]),
                     r=["pA0", "cst"], w=[("BCT", s)])
            for g in range(2):
                P.op("pe", lambda e, s=s, g=g: e.matmul(pA[:, 256 + g * 128:256 + (g + 1) * 128], lhsT=BCT[s][g][:, 0:128],
                                                         rhs=BCT[s][g][:, 128:256], start=True, stop=True), r=[("BCT", s)], w=["pA1"])
            P.op("dve", lambda e, s=s: e.tensor_tensor(out=CBm[s][:, :, :], in0=pA[:, 256:512].rearrange("p (g l) -> p g l", g=2),
                                                       in1=bcast(U, [[0, 2], [1, 128]]), op=ALU.mult), r=["pA1", "cst"], w=[("CBm", s)])
            P.op("dve", lambda e, s=s: e.tensor_tensor(out=LdA[s][:, :, :], in0=bcast(Lt, [[0, 8], [1, 128]]),
                                                       in1=bcast(dA[s][:, :], [[1, 8], [0, 128]]), op=ALU.mult), r=[("dA", s), "cst"], w=[("LdA", s)])
            for h in range(8):
                P.op("pe", lambda e, s=s, h=h: e.matmul(pM[h // 4][:, (h % 4) * 128:(h % 4 + 1) * 128], lhsT=LdA[s][:, h, :], rhs=U,
                                                         start=True, stop=True), r=[("LdA", s), "cst"], w=[("pM", h // 4)])
            for b in range(2):
                P.op("act", lambda e, s=s, b=b: e.activation(out=dec[s][:, b * 4:(b + 1) * 4, :].rearrange("p h l -> p (h l)"), in_=pM[b][:, :], func=AF.Exp),
                     r=[("pM", b)], w=[("dec", s)])
            P.op("dve", lambda e, s=s: e.tensor_tensor(out=WT[s][:, :, :].rearrange("p (g h) l -> p g h l", g=2),
                                                       in0=dec[s][:, :, :].rearrange("p (g h) l -> p g h l", g=2),
                                                       in1=bcast(CBm[s][:, :, :], [[128, 2], [0, 4], [1, 128]]), op=ALU.mult),
                 r=[("dec", s), ("CBm", s)], w=[("WT", s)])
            P.op("pe", lambda e, s=s: e.matmul(pC[:, 0:8], lhsT=U, rhs=dA[s][:, :], start=True, stop=True), r=[("dA", s), "cst"], w=["pC"])
            P.op("pe", lambda e, s=s: e.matmul(pC[:, 8:16], lhsT=ones, rhs=dA[s][:, :], start=True, stop=True), r=[("dA", s), "cst"], w=["pC"])
            P.op("act", lambda e, s=s: e.activation(out=expcum[s][:, :], in_=pC[:, 0:8], func=AF.Exp), r=["pC"], w=[("expcum", s)])
            P.op("act", lambda e: e.activation(out=Edec[0:64, :], in_=pC[0:64, 8:12], func=AF.Exp), r=["pC"], w=["Edec"])
            if "h" not in SKIP:
                P.op("act", lambda e: e.activation(out=Edec[64:128, :], in_=pC[64:128, 12:16], func=AF.Exp), r=["pC"], w=["Edec"])
            for h in range(8):
                P.op("pe", lambda e, s=s, h=h: e.matmul(pY[:, h * 64:(h + 1) * 64], lhsT=WT[s][:, h, :], rhs=xdt[s][:, h * 64:(h + 1) * 64],
                                                         start=True, stop=True), r=[("WT", s), ("xdt", s)], w=["pY"])
            for g in range(2):
                P.op("pe", lambda e, s=s, g=g: e.matmul(pY0[:, g * 256:(g + 1) * 256], lhsT=BCT[s][g][:, 128:256],
                                                         rhs=STb[:, :], start=True, stop=True), r=[("BCT", s), "STb"], w=["pY0"])
            ky = ("yt", s)
            P.op("dve", lambda e, s=s: e.tensor_tensor(out=yt[s][:, :].rearrange("p (h d) -> p h d", h=8), in0=pY0[:, :].rearrange("p (h d) -> p h d", h=8),
                                                       in1=bcast(expcum[s][:, :], [[1, 8], [0, 64]]), op=ALU.mult), r=["pY0", ("expcum", s)], w=[ky])
            P.op("dve", lambda e, s=s: e.tensor_tensor(out=yt[s][:, :], in0=yt[s][:, :], in1=pY[:, :], op=ALU.add), r=[ky, "pY"], w=[ky])
            gate_norm_store(P, 128, yt[s], ky, xa[s][:, 0:512], kX, zt[s], ("zt", s), dsk, nw, xd, sz, junk, ss, rstd, eps, mixd[r0:r0 + 128, 0:512])
            P.op("dve", lambda e, s=s: e.tensor_tensor(out=dtt[:, :], in0=dt[s][:, :], in1=dec[s][:, :, 127], op=ALU.mult), r=[("dt", s), ("dec", s)], w=["dtt"])
            P.op("dve", lambda e, s=s, xv=xv: e.tensor_tensor(out=xdtt[:, :].rearrange("p (h d) -> p h d", h=8), in0=xv,
                                                              in1=bcast(dtt[:, :], [[1, 8], [0, 64]]), op=ALU.mult), r=[kX, "dtt"], w=["xdtt"])
            P.op("pe", lambda e, s=s: e.matmul(pS[:, :], lhsT=Bbf[s][:, :], rhs=xdtt[:, :], start=True, stop=True), r=[("Bbf", s), "xdtt"], w=["pS"])
            for g in range(1 if "h" in SKIP else 2):
                rows = slice(g * 64, (g + 1) * 64)
                P.op("dve", lambda e, rows=rows: e.tensor_tensor(out=tmpS[rows, :].rearrange("p (h d) -> p h d", h=4), in0=ST[rows, :].rearrange("p (h d) -> p h d", h=4),
                                                                 in1=bcast(Edec[rows, :], [[1, 4], [0, 64]]), op=ALU.mult), r=["ST", "Edec"], w=["tmpS"])
                P.op("dve", lambda e, rows=rows, g=g: e.tensor_tensor(out=ST[rows, :], in0=tmpS[rows, :], in1=pS[rows, g * 256:(g + 1) * 256], op=ALU.add),
                     r=["tmpS", "pS"], w=["ST"])
            P.op("act", lambda e: e.activation(out=STb[:, :], in_=ST[:, :], func=AF.Copy), r=["ST"], w=["STb"])
        for g in (range(2) if "d" not in SKIP else []):
            for c2 in range(2):
                rows = slice(g * 64, (g + 1) * 64)
                idx = g * 2 + c2
                P.op("pe", lambda e, g=g, c2=c2, idx=idx: e.matmul(pA[:, idx * 64:(idx + 1) * 64], lhsT=ST[:, c2 * 128:(c2 + 1) * 128],
                                                                   rhs=cst[:, g * 64:(g + 1) * 64], start=True, stop=True), r=["ST", "cst"], w=["pA0", "pA1"])
        if "e" not in SKIP:
            P.op("dve", lambda e: e.tensor_copy(out=stT[:, :, :].rearrange("p a n -> p (a n)"), in_=pA[:, 0:256]), r=["pA0", "pA1"], w=["stT"])
            P.dma("sp", p_ssm.rearrange("(a q) p n -> (q p) a n", q=2), stT[:, :, :], r=["stT"])

        s = 0
        if not DO_S:
            P.emit_phase()
            return
        for i in range(4):
            P.dma("sp", xsh[i][s][0:64, :], hist_s[16 * i:16 * i + 64, :], r=["hist", "hist2"], w=[("xsh", i, s)])
        P.dma("sp", zt[s][0:64, :], projd[SEQ:NTOK, 0:512], w=[("zt", s)])
        P.dma("sp", dtr[s][0:64, :], projd[SEQ:NTOK, O_DT:O_DT + 8], w=[("dtr", s)])
        conv_silu(P, 64, xsh, s, cw, cb, acc, acc2, tP, tD, xa)
        softplus_dt(P, 64, dtr, s, dtb, aneg, dt, dA)
        kX = ("xa", s)
        P.dma("sp", xs_d[:, :], xa[s][0:64, 0:512], r=[kX], w=["xs_d"])
        P.dma("sp", bc_d[:, :], xa[s][0:64, 512:768], r=[kX], w=["bc_d"])
        P.dma("sp", dts_d[:, 0:8], dt[s][0:64, :], r=[("dt", s)], w=["dts_d"])
        P.dma("sp", dts_d[:, 8:16], dA[s][0:64, :], r=[("dA", s)], w=["dts_d"])
        with ExitStack() as st2:
            def sb2(n, sh, d=F32):
                return st2.enter_context(nc.sbuf_tensor("sd2_" + n, sh, d))
            S = sb2("S", [128, 64, 64]); T1 = sb2("T1", [128, 64, 64])
            X = sb2("X", [128, 4, 64]); Bt = sb2("Bt", [128, 4, 64]); Ct = sb2("Ct", [128, 4, 64])
            dtA = sb2("dtA", [128, 2, 4]); ea = sb2("ea", [128, 4]); Xdt = sb2("Xdt", [128, 4, 64]); Yv = sb2("Yv", [128, 4, 64])
            P.dma("sp", S[:, :, :].rearrange("p a b -> p (a b)"), state_ssm.rearrange("s h p n -> (s h) (p n)"), w=["S"])
            P.dma("sp", X[:, :, :], xs_d.rearrange("(t s) (h d) -> (s h) t d", t=4, h=8), r=["xs_d"], w=["X"])
            for hh in range(4):
                for g in range(2):
                    sB = AP(bc_d.tensor, bc_d.offset + g * 64, [[256, 16], [16 * 256, 4], [1, 64]])
                    sC = AP(bc_d.tensor, bc_d.offset + 128 + g * 64, [[256, 16], [16 * 256, 4], [1, 64]])
                    P.dma("sp", Bt[4 * g + hh:128:8, :, :], sB, r=["bc_d"], w=["Bt"])
                    P.dma("sp", Ct[4 * g + hh:128:8, :, :], sC, r=["bc_d"], w=["Ct"])
            for seq in range(16):
                for j in range(2):
                    P.dma("sp", dtA[seq * 8:(seq + 1) * 8, j, :], AP(dts_d.tensor, dts_d.offset + seq * 16 + j * 8, [[1, 8], [16 * 16, 4]]),
                          r=["dts_d"], w=["dtA"], allow_slow_non_contiguous=True)
            P.op("act", lambda e: e.activation(out=ea[:, :], in_=dtA[:, 1, :], func=AF.Exp), r=["dtA"], w=["ea"])
            P.op("dve", lambda e: e.tensor_tensor(out=Xdt[:, :, :], in0=X[:, :, :], in1=bcast(dtA[:, 0, :], [[1, 4], [0, 64]]), op=ALU.mult), r=["X", "dtA"], w=["Xdt"])
            for t in range(TS):
                P.op("dve", lambda e, t=t: e.tensor_scalar(out=S[:, :, :], in0=S[:, :, :], scalar1=ea[:, t:t + 1], scalar2=None, op0=ALU.mult), r=["S", "ea"], w=["S"])
                P.op("pool", lambda e, t=t: e.tensor_tensor(out=T1[:, :, :], in0=bcast(Xdt[:, t, :], [[1, 64], [0, 64]]), in1=bcast(Bt[:, t, :], [[0, 64], [1, 64]]), op=ALU.mult),
                     r=["Xdt", "Bt"], w=["T1"])
                P.op("dve", lambda e: e.tensor_tensor(out=S[:, :, :], in0=S[:, :, :], in1=T1[:, :, :], op=ALU.add), r=["S", "T1"], w=["S"])
                P.op("pool", lambda e, t=t: e.tensor_tensor(out=T1[:, :, :], in0=S[:, :, :], in1=bcast(Ct[:, t, :], [[0, 64], [1, 64]]), op=ALU.mult), r=["S", "Ct"], w=["T1"])
                P.op("dve", lambda e, t=t: e.tensor_reduce(out=Yv[:, t, :], in_=T1[:, :, :], axis=AX.X, op=ALU.add), r=["T1"], w=["Yv"])
            P.dma("sp", s_ssm.rearrange("s h p n -> (s h) (p n)"), S[:, :, :].rearrange("p a b -> p (a b)"), r=["S"])
            P.dma("sp", ys_d.rearrange("(t s) (h d) -> (s h) t d", t=4, h=8), Yv[:, :, :], r=["Yv"], w=["ys_d"])
            ky = ("yt", 0)
            P.dma("sp", yt[0][0:64, :], ys_d[:, :], r=["ys_d"], w=[ky])
            gate_norm_store(P, 64, yt[0], ky, xa[s][0:64, 0:512], kX, zt[s], ("zt", s), dsk, nw, xd, sz, junk, ss, rstd, eps, mixd[SEQ:NTOK, 0:512])
            P.emit_phase()


def outproj_phase(kk, P, mixd, h1, h2, w_out, ln_g, ln_b, ident_d):
    nc = kk.nc
    with ExitStack() as st:
        def sbt(n, s, d=F32):
            return st.enter_context(nc.sbuf_tensor("op_" + n, s, d))
        Wo = sbt("Wo", [128, 8, D], BF16)
        ident = sbt("ident", [128, 128])
        gt = sbt("gt", [128, D]); bt = sbt("bt", [128, D])
        mt = [sbt("mt%d" % s, [128, D]) for s in range(2)]
        h1t = [sbt("h1t%d" % s, [128, D]) for s in range(2)]
        mT = [sbt("mT%d" % s, [128, 8, 128], BF16) for s in range(2)]
        zt = [sbt("zt%d" % s, [128, D]) for s in range(2)]
        ot = [sbt("ot%d" % s, [128, D]) for s in range(2)]
        sb = dict(stats=sbt("stats", [128, 2, 6]), mv=sbt("mv", [128, 2]), rstd=sbt("rstd", [128, 1]),
                  nb=sbt("nb", [128, 1]), eps=sbt("eps", [128, 1]))
        pT = [st.enter_context(nc.psum_tensor("op_pT%d" % i, [128, 512], F32)) for i in range(2)]
        pO = [[st.enter_context(nc.psum_tensor("op_pO%d%d" % (s, h), [128, 512], F32)) for h in range(2)] for s in range(2)]
        P.op("pool", lambda e: e.memset(sb["eps"][:, :], LN_EPS), w=["eps"])
        P.dma("sp", ident[:, :], ident_d[:, :], w=["ident"])
        P.dma("sp", gt[:, :], ln_g[0:1, :].partition_broadcast(128), w=["op_g"])
        P.dma("sp", bt[:, :], ln_b[0:1, :].partition_broadcast(128), w=["op_b"])
        for k in range(8):
            P.dma("pool", Wo[:, k, :], w_out[k * 128:(k + 1) * 128, :], w=["Wo"])
        for t, (r0, nr) in enumerate(TILES):
            s = t % 2
            P.dma("sp", mt[s][:nr, :], mixd[r0:r0 + nr, :], w=[("mt", s)])
            P.dma("sp", h1t[s][:nr, :], h1[r0:r0 + nr, :], w=[("h1t", s)])
            transpose_tile(P, mt[s], nr, ("mt", s), pT, lambda kh, s=s, nr=nr: mT[s][:, kh * 4:(kh + 1) * 4, :nr], ("mT", s), ident)
            for half in range(2):
                for k in range(8):
                    P.op("pe", lambda e, s=s, nr=nr, half=half, k=k: e.matmul(pO[s][half][:nr, :], lhsT=mT[s][:, k, :nr],
                                                                               rhs=Wo[:, k, half * 512:(half + 1) * 512], start=(k == 0), stop=(k == 7)),
                         r=[("mT", s), "Wo"], w=[("pO", s, half)])
                P.op("dve", lambda e, s=s, nr=nr, half=half: e.scalar_tensor_tensor(
                    out=zt[s][:nr, half * 512:(half + 1) * 512], in0=h1t[s][:nr, half * 512:(half + 1) * 512], scalar=ALPHA,
                    in1=pO[s][half][:nr, :], op0=ALU.mult, op1=ALU.add), r=[("h1t", s), ("pO", s, half)], w=[("zt", s)])
            layer_norm_tile(P, sb, zt[s], nr, gt, bt, ot[s], ("zt", s), ("ot", s), "op_")
            P.dma("sp", h2[r0:r0 + nr, :], ot[s][:nr, :], r=[("ot", s)], w=[("h2", t)])
        P.emit_phase()


NEG = -30000.0
GELU_C = 0.044715
GELU_S = 2.0 * 0.7978845608028654


def nsa_prompt_phase(kk, P, projd, kv_out, mixd, cw, consts_d, consts2_d, seltab_d):
    nc = kk.nc
    NQT = int(os.environ.get("NSA_NQT", "32"))
    with ExitStack() as st:
        def sbt(n, s, d=F32):
            return st.enter_context(nc.sbuf_tensor("na_" + n, s, d))
        cst = sbt("cst", [128, 514])
        ident = cst[:, 0:128]
        rmask = cst[:, 512:514]
        c2 = sbt("c2", [128, 1024])
        cval, causb, wbias, cvalid = c2[:, 0:255], c2[:, 255:383], c2[:, 383:1023], c2[:, 1023:1024]
        identb = sbt("identb", [128, 128], BF16)
        rm8 = sbt("rm8", [128, 2])
        kTs = sbt("kTs", [128, SEQ], BF16); kTw = sbt("kTw", [128, SEQ], BF16)
        Vs = sbt("Vs", [128, 32, 128], BF16); Vw = sbt("Vw", [128, 32, 128], BF16)
        KT2 = sbt("KT2", [128, 4, 2048], BF16)
        W1 = [sbt("W1%d" % x, [128, 16, 128], BF16) for x in range(2)]
        pec = [sbt("pec%d" % x, [128, 16], BF16) for x in range(2)]
        perow = [sbt("perow%d" % x, [16, 128]) for x in range(2)]
        w2p = [[sbt("w2p%d%d" % (x, g), [128, 128], BF16) for g in range(2)] for x in range(2)]
        peterm = [sbt("peterm%d" % x, [128, 1]) for x in range(2)]
        gl = [[sbt("gl%d%d" % (x, g), [128, 256], BF16) for g in range(2)] for x in range(2)]
        gx = sbt("gx", [128, 256]); gu = sbt("gu", [128, 256])
        kcT = sbt("kcT", [128, 256], BF16); vc = sbt("vc", [128, 2, 128], BF16)
        kvt = [sbt("kvt%d" % s, [128, 768]) for s in range(2)]
        pair = [sbt("pair%d" % s, [128, 4, 128]) for s in range(2)]
        qt = [sbt("qt%d" % s, [128, 512]) for s in range(2)]
        gt_ = [sbt("gate%d" % s, [128, 24]) for s in range(2)]
        sg = [sbt("sg%d" % s, [128, 24]) for s in range(2)]
        qTh = [sbt("qTh%d" % s, [128, 8, 128], BF16) for s in range(2)]
        bias_c = [sbt("biasc%d" % s, [128, 255]) for s in range(2)]
        Ssb = [sbt("Ssb%d" % s, [128, SEQ]) for s in range(2)]
        Pbf = [sbt("Pbf%d" % s, [128, SEQ], BF16) for s in range(2)]
        PTsb = [sbt("PTsb%d" % s, [128, 32, 128], BF16) for s in range(2)]
        Pf = [sbt("Pf%d" % s, [128, 255]) for s in range(2)]
        pacc = [sbt("pacc%d" % g, [128, 260]) for g in range(2)]
        imp = sbt("imp", [128, 64]); cand = sbt("cand", [128, 64]); cand2 = sbt("cand2", [128, 64])
        m8 = sbt("m8", [128, 16]); selb = [sbt("selb%d" % g, [128, 64]) for g in range(2)]
        seltab = [sbt("seltab%d" % s, [128, 128]) for s in range(2)]
        mx = [sbt("mx%d" % s, [128, 1]) for s in range(4)]
        rsum = [sbt("rsum%d" % s, [128, 1]) for s in range(4)]
        gs = [sbt("gs%d" % s, [128, 1]) for s in range(4)]
        onsa = [sbt("onsa%d" % s, [128, 512]) for s in range(2)]
        pS = [st.enter_context(nc.psum_tensor("na_pS%d" % i, [128, 512], F32)) for i in range(2)]
        pPT = [st.enter_context(nc.psum_tensor("na_pPT%d" % i, [128, 1024], BF16)) for i in range(2)]
        pO = [st.enter_context(nc.psum_tensor("na_pO%d" % i, [128, 512], F32)) for i in range(3)]
        pT = st.enter_context(nc.psum_tensor("na_pT", [128, 512], F32))

        P.dma("sp", cst[:, :], consts_d[:, 0:514], w=["cst"])
        P.dma("sp", c2[:, :], consts2_d[:, :], w=["c2"])
        P.op("dve", lambda e: e.tensor_copy(out=identb[:, :], in_=ident), r=["cst"], w=["identb"])
        P.op("dve", lambda e: e.tensor_scalar(out=rm8[:, :], in0=rmask, scalar1=0.125, scalar2=None, op0=ALU.mult), r=["cst"], w=["rm8"])
        for x, pre in enumerate(("cmp_k_", "cmp_v_") if "w" not in os.environ.get("NSA_SKIP", "") else ()):
            w1 = cw[pre + "w1"]
            for j in range(16):
                P.dma("pool", W1[x][:, j, :], w1[j * 128:(j + 1) * 128, :], w=[("W1", x)])
            P.dma("sp", perow[x][:, :], cw[pre + "pe"].rearrange("(j p) o -> j (p o)", p=128), w=[("perow", x)])
            P.op("pe", lambda e, x=x: e.transpose(out=pT[:, 0:16], in_=perow[x][:, :], identity=cst[0:16, 0:16]), r=[("perow", x), "cst"], w=["pT"])
            P.op("act", lambda e, x=x: e.activation(out=pec[x][:, :], in_=pT[:, 0:16], func=AF.Copy), r=["pT"], w=[("pec", x)])
            for g in range(2):
                P.op("pool", lambda e, x=x, g=g: e.memset(w2p[x][g][:, :], 0.0), w=[("w2p", x, g)])
                P.dma("pool", w2p[x][g][:, g * 64:(g + 1) * 64], cw[pre + "w2"][:, :], w=[("w2p", x, g)])
        for g in range(2):
            P.op("pool", lambda e, g=g: e.memset(pacc[g][:, :], 0.0), w=[("pacc", g)])

        for t in (range(32) if "k" not in os.environ.get("NSA_SKIP", "") else []):
            s = t % 2
            P.dma("sp", kvt[s][:, :], kv_out[t * 128:(t + 1) * 128, :], w=[("kvt", s)])
            KVL = int(os.environ.get("NSA_KV", "9"))
            if KVL >= 1:
                P.op("pe", lambda e, s=s: e.transpose(out=pT[:, 0:128], in_=kvt[s][:, 256:384], identity=ident), r=[("kvt", s), "cst"], w=["pT"])
                P.op("pe", lambda e, s=s: e.transpose(out=pT[:, 128:256], in_=kvt[s][:, 512:640], identity=ident), r=[("kvt", s), "cst"], w=["pT"])
            if KVL >= 2:
                P.op("act", lambda e, t=t: e.activation(out=kTs[:, t * 128:(t + 1) * 128], in_=pT[:, 0:128], func=AF.Copy), r=["pT"], w=["kTs"])
            if KVL >= 3:
                P.op("dve", lambda e, t=t: e.tensor_copy(out=kTw[:, t * 128:(t + 1) * 128], in_=pT[:, 128:256]), r=["pT"], w=["kTw"])
            if KVL >= 4:
                P.op("pool", lambda e, s=s, t=t: e.tensor_copy(out=Vs[:, t, :], in_=kvt[s][:, 384:512]), r=[("kvt", s)], w=["Vs"])
                P.op("pool", lambda e, s=s, t=t: e.tensor_copy(out=Vw[:, t, :], in_=kvt[s][:, 640:768]), r=[("kvt", s)], w=["Vw"])

        for b in (range(16) if "r" not in os.environ.get("NSA_SKIP", "") else []):
            s = b % 2
            for x in range(2):
                for g in range(2):
                    src = AP(kv_out.tensor, kv_out.offset + b * 256 * 768 + x * 128 + g * 64, [[1536, 128], [768, 2], [1, 64]])
                    P.dma("sp", pair[s][:, x * 2 + g, :].rearrange("p (e d) -> p e d", e=2), src, w=[("pair", s)])
            for xg in range(4):
                P.op("pe", lambda e, s=s, xg=xg: e.transpose(out=pT[:, xg * 128:(xg + 1) * 128], in_=pair[s][:, xg, :], identity=ident),
                     r=[("pair", s), "cst"], w=["pT"])
            P.op("act", lambda e, b=b: e.activation(out=KT2[:, :, b * 128:(b + 1) * 128], in_=pT[:, :].rearrange("p (a m) -> p a m", a=4), func=AF.Copy),
                 r=["pT"], w=["KT2"])
        NSKIP = os.environ.get("NSA_SKIP", "")
        for x in (range(2) if "z" not in NSKIP else []):
            for j in (range(16) if "p" not in NSKIP else []):
                P.op("pe", lambda e, x=x, j=j: e.matmul(pO[0][:, 0:1], lhsT=W1[x][:, j, :], rhs=pec[x][:, j:j + 1], start=(j == 0), stop=(j == 15)),
                     r=[("W1", x), ("pec", x)], w=["pO0"])
            P.op("dve", lambda e, x=x: e.tensor_copy(out=peterm[x][:, :], in_=pO[0][:, 0:1]), r=["pO0"], w=[("peterm", x)])
            for g in range(2):
                b = g % 2
                for j in (range(16) if "c" not in NSKIP else []):
                    P.op("pe", lambda e, x=x, g=g, j=j, b=b: e.matmul(pS[b][:, 0:255], lhsT=W1[x][:, j, :], rhs=KT2[:, x * 2 + g, j:j + 8 * 254 + 1:8],
                                                                       start=(j == 0), stop=(j == 15)), r=[("W1", x), "KT2"], w=[("pS", b)])
                P.op("act", lambda e, x=x, b=b: e.activation(out=gx[:, 0:255], in_=pS[b][:, 0:255], func=AF.Identity, bias=peterm[x][:, 0:1], scale=1.0),
                     r=[("pS", b), ("peterm", x)], w=["gx"])
                P.op("dve", lambda e: e.tensor_tensor(out=gu[:, 0:255], in0=gx[:, 0:255], in1=gx[:, 0:255], op=ALU.mult), r=["gx"], w=["gu"])
                P.op("dve", lambda e: e.tensor_scalar(out=gu[:, 0:255], in0=gu[:, 0:255], scalar1=GELU_C, scalar2=1.0, op0=ALU.mult, op1=ALU.add), r=["gu"], w=["gu"])
                P.op("dve", lambda e: e.tensor_tensor(out=gu[:, 0:255], in0=gu[:, 0:255], in1=gx[:, 0:255], op=ALU.mult), r=["gu", "gx"], w=["gu"])
                P.op("act", lambda e: e.activation(out=gu[:, 0:255], in_=gu[:, 0:255], func=AF.Sigmoid, scale=GELU_S), r=["gu"], w=["gu"])
                P.op("dve", lambda e, x=x, g=g: e.tensor_tensor(out=gl[x][g][:, 0:255], in0=gu[:, 0:255], in1=gx[:, 0:255], op=ALU.mult), r=["gu", "gx"], w=[("gl", x, g)])
        for g in (range(2) if "z" not in NSKIP else []):
            P.op("pe", lambda e, g=g: e.matmul(pS[0][:, 0:255], lhsT=w2p[0][g][:, :], rhs=gl[0][g][:, 0:255], start=(g == 0), stop=(g == 1)),
                 r=[("w2p", 0, g), ("gl", 0, g)], w=[("pS", 0)])
        P.op("act", lambda e: e.activation(out=kcT[:, 0:255], in_=pS[0][:, 0:255], func=AF.Copy), r=[("pS", 0)], w=["kcT"])
        for ct, (c0, cn) in enumerate(((0, 128), (128, 127)) if "v" not in NSKIP else ()):
            for g in range(2):
                P.op("pe", lambda e, g=g, c0=c0, cn=cn, ct=ct: e.matmul(pS[1][:cn, ct * 128:(ct + 1) * 128], lhsT=gl[1][g][:, c0:c0 + cn], rhs=w2p[1][g][:, :],
                                                                         start=(g == 0), stop=(g == 1)), r=[("w2p", 1, g), ("gl", 1, g)], w=[("pS", 1)])
            P.op("act", lambda e, cn=cn, ct=ct: e.activation(out=vc[:cn, ct, :], in_=pS[1][:cn, ct * 128:(ct + 1) * 128], func=AF.Copy), r=[("pS", 1)], w=["vc"])

        cnt = {"a": 0}

        def attend(i, h, br, qs, kT, kkey, V, vkey, tile_lo, ntile, mode, first):
            g = h // 4
            a = cnt["a"]; cnt["a"] += 1
            b = a % 2
            m4 = a % 4
            nk = ntile * 128 if mode != "cmp" else 255
            kS, kP, kPT = ("Ssb", b), ("Pbf", b), ("PTsb", b)
            nch = (nk + 511) // 512
            for ch in range(nch):
                k0 = ch * 512
                w = min(512, nk - k0)
                pb = (a + ch) % 2
                kcol = tile_lo * 128 + k0
                P.op("pe", lambda e, pb=pb, w=w, kcol=kcol: e.matmul(pS[pb][:, 0:w], lhsT=qTh[qs][:, h, :], rhs=kT[:, kcol:kcol + w], start=True, stop=True),
                     r=[("qTh", qs), kkey], w=[("pS", pb)])
                if mode == "cmp":
                    P.op("dve", lambda e, pb=pb, w=w, k0=k0: e.tensor_tensor(out=Ssb[b][:, k0:k0 + w], in0=pS[pb][:, 0:w], in1=bias_c[qs][:, 0:w], op=ALU.add),
                         r=[("pS", pb), ("biasc", qs)], w=[kS])
                elif mode == "win":
                    woff = (5 - ntile) * 128 + k0
                    P.op("dve", lambda e, pb=pb, w=w, k0=k0, woff=woff: e.tensor_tensor(out=Ssb[b][:, k0:k0 + w], in0=pS[pb][:, 0:w], in1=wbias[:, woff:woff + w], op=ALU.add),
                         r=[("pS", pb), "c2"], w=[kS])
                elif mode == "sel" and i >= 8:
                    nb = w // 64
                    blk0 = (tile_lo * 128 + k0) // 64
                    P.op("dve", lambda e, pb=pb, w=w, k0=k0, nb=nb, blk0=blk0: e.tensor_tensor(
                        out=Ssb[b][:, k0:k0 + w].rearrange("p (n d) -> p n d", d=64), in0=pS[pb][:, 0:w].rearrange("p (n d) -> p n d", d=64),
                        in1=bcast(selb[g][:, blk0:blk0 + nb], [[1, nb], [0, 64]]), op=ALU.add), r=[("pS", pb), ("selb", g)], w=[kS])
                else:
                    P.op("act", lambda e, pb=pb, w=w, k0=k0: e.activation(out=Ssb[b][:, k0:k0 + w], in_=pS[pb][:, 0:w], func=AF.Copy), r=[("pS", pb)], w=[kS])
            if mode == "sel":
                d0 = (ntile - 1) * 128
                P.op("pool", lambda e, d0=d0: e.tensor_tensor(out=Ssb[b][:, d0:d0 + 128], in0=Ssb[b][:, d0:d0 + 128], in1=causb, op=ALU.add), r=[kS, "c2"], w=[kS])
            P.op("dve", lambda e: e.tensor_reduce(out=mx[m4][:, :], in_=Ssb[b][:, 0:nk], axis=AX.X, op=ALU.max), r=[kS], w=[("mx", m4)])
            P.op("dve", lambda e: e.tensor_scalar(out=mx[m4][:, :], in0=mx[m4][:, :], scalar1=-1.0, scalar2=None, op0=ALU.mult), r=[("mx", m4)], w=[("mx", m4)])
            P.op("pool", lambda e: e.memset(rsum[m4][:, :], 0.0), w=[("rsum", m4)])
            if mode == "cmp":
                pf = Pf[b]
                P.op("act", lambda e: e.activation(out=pf[:, 0:255], in_=Ssb[b][:, 0:255], func=AF.Exp, bias=mx[m4][:, 0:1], scale=1.0, accum_out=rsum[m4][:, :]),
                     r=[kS, ("mx", m4), ("rsum", m4)], w=[("Pf", b), ("rsum", m4)])
                P.op("pool", lambda e: e.tensor_copy(out=Pbf[b][:, 0:255], in_=pf[:, 0:255]), r=[("Pf", b)], w=[kP])
            else:
                P.op("act", lambda e: e.activation(out=Pbf[b][:, 0:nk], in_=Ssb[b][:, 0:nk], func=AF.Exp, bias=mx[m4][:, 0:1], scale=1.0, accum_out=rsum[m4][:, :]),
                     r=[kS, ("mx", m4), ("rsum", m4)], w=[kP, ("rsum", m4)])
            P.op("dve", lambda e: e.reciprocal(out=rsum[m4][:, :], in_=rsum[m4][:, :]), r=[("rsum", m4)], w=[("rsum", m4)])
            if mode == "cmp" and i >= 8:
                r = h % 4
                if r == 0:
                    P.op("dve", lambda e: e.tensor_scalar(out=pacc[g][:, 1:256], in0=Pf[b][:, 0:255], scalar1=rsum[m4][:, 0:1], scalar2=None, op0=ALU.mult),
                         r=[("Pf", b), ("rsum", m4)], w=[("pacc", g)])
                else:
                    P.op("dve", lambda e: e.scalar_tensor_tensor(out=pacc[g][:, 1:256], in0=Pf[b][:, 0:255], scalar=rsum[m4][:, 0:1], in1=pacc[g][:, 1:256],
                                                                 op0=ALU.mult, op1=ALU.add), r=[("Pf", b), ("rsum", m4), ("pacc", g)], w=[("pacc", g)])
            tiles = [(j * 128, min(128, nk - j * 128)) for j in range((nk + 127) // 128)]
            for j0 in range(0, len(tiles), 8):
                grp = tiles[j0:j0 + 8]
                pb = (a + j0 // 8) % 2
                for jj, (c0, cn) in enumerate(grp):
                    P.op("pe", lambda e, pb=pb, jj=jj, c0=c0, cn=cn: e.transpose(out=pPT[pb][:cn, jj * 128:(jj + 1) * 128], in_=Pbf[b][:, c0:c0 + cn], identity=identb[:, :]),
                         r=[kP, "identb"], w=[("pPT", pb)])
                ng = len(grp)
                eng = "act" if (j0 // 8) % 2 == 0 else "dve"
                if all(cn == 128 for _, cn in grp):
                    parts = [(pPT[pb][:, 0:ng * 128].rearrange("p (j t) -> p j t", t=128), PTsb[b][:, j0:j0 + ng, :])]
                else:
                    parts = [(pPT[pb][:cn, jj * 128:(jj + 1) * 128], PTsb[b][:cn, j0 + jj, :]) for jj, (_, cn) in enumerate(grp)]
                for (srcv, dstv) in parts:
                    if eng == "act":
                        P.op("act", lambda e, srcv=srcv, dstv=dstv: e.activation(out=dstv, in_=srcv, func=AF.Copy), r=[("pPT", pb)], w=[kPT])
                    else:
                        P.op("dve", lambda e, srcv=srcv, dstv=dstv: e.tensor_copy(out=dstv, in_=srcv), r=[("pPT", pb)], w=[kPT])
            po = pO[br]
            for j, (c0, cn) in enumerate(tiles):
                if mode == "cmp":
                    rhs = V[:cn, j, g * 64:(g + 1) * 64]
                else:
                    rhs = V[:cn, tile_lo + j, g * 64:(g + 1) * 64]
                P.op("pe", lambda e, j=j, cn=cn, rhs=rhs: e.matmul(po[:, h * 64:(h + 1) * 64], lhsT=PTsb[b][:cn, j, :], rhs=rhs, start=(j == 0), stop=(j == len(tiles) - 1)),
                     r=[kPT, vkey], w=[("pO", br)])
            P.op("dve", lambda e: e.tensor_tensor(out=gs[m4][:, :], in0=rsum[m4][:, :], in1=sg[qs][:, 3 * h + br:3 * h + br + 1], op=ALU.mult),
                 r=[("rsum", m4), ("sg", qs)], w=[("gs", m4)])
            if mode == "cmp" and i == 0:
                P.op("dve", lambda e: e.tensor_tensor(out=gs[m4][:, :], in0=gs[m4][:, :], in1=cvalid, op=ALU.mult), r=[("gs", m4), "c2"], w=[("gs", m4)])
            ko = ("onsa", qs)
            if first:
                P.op("dve", lambda e: e.tensor_scalar(out=onsa[qs][:, h * 64:(h + 1) * 64], in0=po[:, h * 64:(h + 1) * 64], scalar1=gs[m4][:, 0:1], scalar2=None, op0=ALU.mult),
                     r=[("pO", br), ("gs", m4)], w=[ko])
            else:
                P.op("dve", lambda e: e.scalar_tensor_tensor(out=onsa[qs][:, h * 64:(h + 1) * 64], in0=po[:, h * 64:(h + 1) * 64], scalar=gs[m4][:, 0:1],
                                                             in1=onsa[qs][:, h * 64:(h + 1) * 64], op0=ALU.mult, op1=ALU.add), r=[("pO", br), ("gs", m4), ko], w=[ko])

        for i in range(NQT):
            qs = i % 2
            r0 = i * 128
            P.dma("sp", qt[qs][:, :], projd[r0:r0 + 128, O_Q:O_Q + 512], w=[("qt", qs)])
            P.dma("sp", gt_[qs][:, :], projd[r0:r0 + 128, O_GATE:O_GATE + 24], w=[("gate", qs)])
            P.op("act", lambda e, qs=qs: e.activation(out=sg[qs][:, :], in_=gt_[qs][:, :], func=AF.Sigmoid), r=[("gate", qs)], w=[("sg", qs)])
            if i >= 8:
                P.dma("sp", seltab[qs][:, :], seltab_d[i, :, :], w=[("seltab", qs)])
            wins = [(0, [(0, 0)]), (64, [(1, 0)]), (128, [(2, 0)]), (192, [(3, 0), (4, 1)]), (256, [(5, 1)]), (320, [(6, 1)]), (384, [(7, 1)])]
            for batch in (wins[0:4], wins[4:7]):
                for slot, (c0, heads) in enumerate(batch):
                    P.op("pe", lambda e, qs=qs, c0=c0, slot=slot: e.transpose(out=pT[:, slot * 128:(slot + 1) * 128], in_=qt[qs][:, c0:c0 + 128], identity=ident),
                         r=[("qt", qs), "cst"], w=["pT"])
                for slot, (c0, heads) in enumerate(batch):
                    for (h, g) in heads:
                        P.op("act", lambda e, qs=qs, h=h, g=g, slot=slot: e.activation(out=qTh[qs][:, h, :], in_=pT[:, slot * 128:(slot + 1) * 128], func=AF.Copy, scale=rm8[:, g:g + 1]),
                             r=["pT", "rm8"], w=[("qTh", qs)])
            P.op("dve", lambda e, qs=qs, r0=r0: e.tensor_scalar(out=bias_c[qs][:, :], in0=cval, scalar1=float(r0), scalar2=0.0, op0=ALU.add, op1=ALU.is_ge),
                 r=["c2"], w=[("biasc", qs)])
            P.op("dve", lambda e, qs=qs: e.tensor_scalar(out=bias_c[qs][:, :], in0=bias_c[qs][:, :], scalar1=-1.0, scalar2=-NEG, op0=ALU.add, op1=ALU.mult),
                 r=[("biasc", qs)], w=[("biasc", qs)])
            for h in range(8):
                attend(i, h, 0, qs, kcT, "kcT", vc, "vc", 0, 2, "cmp", True)
            if i >= 8:
                for g in range(2):
                    kpa = ("pacc", g)
                    P.op("dve", lambda e, g=g: e.tensor_tensor(out=imp[:, :], in0=pacc[g][:, 0:253:4], in1=pacc[g][:, 1:254:4], op=ALU.add), r=[kpa], w=["imp"])
                    for o in (2, 3, 4):
                        P.op("dve", lambda e, g=g, o=o: e.tensor_tensor(out=imp[:, :], in0=imp[:, :], in1=pacc[g][:, o:o + 253:4], op=ALU.add), r=[kpa, "imp"], w=["imp"])
                    P.op("dve", lambda e, qs=qs: e.tensor_tensor(out=cand[:, :], in0=imp[:, :], in1=seltab[qs][:, 0:64], op=ALU.mult), r=["imp", ("seltab", qs)], w=["cand"])
                    P.op("dve", lambda e, qs=qs: e.scalar_tensor_tensor(out=cand[:, :], in0=seltab[qs][:, 0:64], scalar=-1.0, in1=cand[:, :], op0=ALU.add, op1=ALU.add),
                         r=["cand", ("seltab", qs)], w=["cand"])
                    P.op("dve", lambda e: e.max(out=m8[:, 0:8], in_=cand[:, :]), r=["cand"], w=["m8"])
                    P.op("dve", lambda e: e.match_replace(out=cand2[:, :], in_to_replace=m8[:, 0:8], in_values=cand[:, :], imm_value=-2.0), r=["cand", "m8"], w=["cand2"])
                    P.op("dve", lambda e: e.max(out=m8[:, 8:16], in_=cand2[:, :]), r=["cand2"], w=["m8"])
                    P.op("dve", lambda e, g=g: e.tensor_scalar(out=selb[g][:, :], in0=cand[:, :], scalar1=m8[:, 12:13], scalar2=None, op0=ALU.is_ge), r=["cand", "m8"], w=[("selb", g)])
                    P.op("dve", lambda e, g=g, qs=qs: e.tensor_tensor(out=selb[g][:, :], in0=selb[g][:, :], in1=seltab[qs][:, 64:128], op=ALU.max), r=[("selb", g), ("seltab", qs)], w=[("selb", g)])
                    P.op("dve", lambda e, g=g: e.tensor_scalar(out=selb[g][:, :], in0=selb[g][:, :], scalar1=-1.0, scalar2=-NEG, op0=ALU.add, op1=ALU.mult), r=[("selb", g)], w=[("selb", g)])
            for h in range(8):
                attend(i, h, 1, qs, kTs, "kTs", Vs, "Vs", 0, i + 1, "sel", False)
            lo = max(0, i - 4)
            for h in range(8):
                attend(i, h, 2, qs, kTw, "kTw", Vw, "Vw", lo, i - lo + 1, "win", False)
            P.dma("sp", mixd[r0:r0 + 128, 512:1024], onsa[qs][:, :], r=[("onsa", qs)])
        P.emit_phase()


def nsa_sample_phase(kk, P, projd, kv_out, mixd, cw, consts_d, consts3_d, caches, ckw, cvw, page_table):
    nc = kk.nc
    NS = int(os.environ.get("NSA_NSEQ", str(NSEQ_S)))
    NPG = 64
    NK = NPG * 128 + 4
    with ExitStack() as st:
        def sbt(n, s, d=F32):
            return st.enter_context(nc.sbuf_tensor("ns_" + n, s, d))
        cst = sbt("cst", [128, 514]); ident = cst[:, 0:128]
        c3 = sbt("c3", [128, 1024])
        Amat, A2, bias4, wbias, candm, forced = c3[0:64, 0:8], c3[0:8, 8:72], c3[0:64, 72:76], c3[0:64, 76:592], c3[0:8, 592:720], c3[0:8, 720:848]
        identb = sbt("identb", [128, 128], BF16)
        idx = sbt("idx", [128, NSEQ_S * NPG], I32); ptb = idx; ptf = sbt("ptf", [128, NSEQ_S * NPG])
        kTs = sbt("kTs", [128, NK + 4], BF16); Vs = sbt("Vs", [128, NPG + 1, 128], BF16)
        kTc = sbt("kTc", [128, NPG * 128], BF16)
        W1g = [[sbt("W1g%d%d" % (x, g), [128, 32, 128], BF16) for g in range(2)] for x in range(2)]
        W1 = [sbt("W1%d" % x, [128, 16, 128], BF16) for x in range(2)]
        pec = [sbt("pec%d" % x, [128, 16], BF16) for x in range(2)]
        perow = [sbt("perow%d" % x, [16, 128]) for x in range(2)]
        w2p = [[sbt("w2p%d%d" % (x, g), [128, 128], BF16) for g in range(2)] for x in range(2)]
        peterm = [sbt("peterm%d" % x, [128, 1]) for x in range(2)]
        gl = [[sbt("gl%d%d" % (x, g), [128, 512], BF16) for g in range(2)] for x in range(2)]
        gx = sbt("gx", [128, 512]); gu = sbt("gu", [128, 512])
        kcT = sbt("kcT", [128, 512], BF16); vc = sbt("vc", [128, 4, 128], BF16)
        pg = [sbt("pg%d" % i, [128, 4, 128]) for i in range(4)]
        newT = sbt("newT", [128, 2, 64], BF16)
        stile = sbt("stile", [64, 768])
        Q64 = sbt("Q64", [64, NSEQ_S, 128]); G64 = sbt("G64", [64, NSEQ_S, 3]); SG64 = sbt("SG64", [64, NSEQ_S, 3])
        qT64 = sbt("qT64", [128, 64], BF16)
        kTw = sbt("kTw", [128, 520], BF16); Vw = sbt("Vw", [128, 5, 128], BF16)
        wtile = sbt("wtile", [128, 4, 256])
        Ssb = sbt("Ssb", [64, NK + 4]); Pbf = sbt("Pbf", [64, NK + 4], BF16); PTsb = sbt("PTsb", [128, NPG + 1, 64], BF16)
        Pf = sbt("Pf", [64, 512]); pacc8 = sbt("pacc8", [8, 520]); imp = sbt("imp", [8, 128]); cand = sbt("cand", [8, 128]); cand2 = sbt("cand2", [8, 128])
        m8 = sbt("m8", [8, 16]); selb8 = sbt("selb8", [8, 132]); selb64 = sbt("selb64", [64, 132])
        mx = sbt("mx", [64, 1]); rsum = sbt("rsum", [64, 1]); gs = sbt("gs", [64, 1]); osum = sbt("osum", [64, 64])
        pS = [st.enter_context(nc.psum_tensor("ns_pS%d" % i, [128, 512], F32)) for i in range(2)]
        pPT = [st.enter_context(nc.psum_tensor("ns_pPT%d" % i, [128, 1024], BF16)) for i in range(2)]
        pO = st.enter_context(nc.psum_tensor("ns_pO", [128, 512], F32))
        pT = [st.enter_context(nc.psum_tensor("ns_pT%d" % i, [128, 512], F32)) for i in range(2)]
        pI = st.enter_context(nc.psum_tensor("ns_pI", [128, 512], F32))

        P.dma("sp", cst[:, :], consts_d[:, 0:514], w=["cst"])
        P.dma("sp", c3[:, :], consts3_d[:, :], w=["c3"])
        P.dma("sp", ptb[:, :], page_table.rearrange("s (o j) -> o (s j)", o=1).partition_broadcast(128), w=["idx"])
        P.op("dve", lambda e: e.tensor_copy(out=ptf[:, :], in_=ptb[:, :]), r=["idx"], w=["ptf"])
        P.op("dve", lambda e: e.tensor_scalar(out=ptf[:, :], in0=ptf[:, :], scalar1=128.0, scalar2=c3[:, 848:849], op0=ALU.mult, op1=ALU.add), r=["ptf", "c3"], w=["ptf"])
        P.op("dve", lambda e: e.tensor_copy(out=idx[:, :], in_=ptf[:, :]), r=["ptf"], w=["idx"])
        P.op("dve", lambda e: e.tensor_copy(out=identb[:, :], in_=ident), r=["cst"], w=["identb"])
        for x, pre in enumerate(("cmp_k_", "cmp_v_")):
            w1 = cw[pre + "w1"]
            for j in range(16):
                P.dma("pool", W1[x][:, j, :], w1[j * 128:(j + 1) * 128, :], w=[("W1", x)])
            for g in range(2):
                P.op("pool", lambda e, x=x, g=g: e.memset(W1g[x][g][:, :, :], 0.0), w=[("W1g", x, g)])
                P.dma("pool", W1g[x][g][g * 64:(g + 1) * 64, :, :], w1.rearrange("(s d) h -> d s h", d=64), w=[("W1g", x, g)])
                P.op("pool", lambda e, x=x, g=g: e.memset(w2p[x][g][:, :], 0.0), w=[("w2p", x, g)])
                P.dma("pool", w2p[x][g][:, g * 64:(g + 1) * 64], cw[pre + "w2"][:, :], w=[("w2p", x, g)])
            P.dma("sp", perow[x][:, :], cw[pre + "pe"].rearrange("(j p) o -> j (p o)", p=128), w=[("perow", x)])
            P.op("pe", lambda e, x=x: e.transpose(out=pT[0][:, 0:16], in_=perow[x][:, :], identity=cst[0:16, 0:16]), r=[("perow", x), "cst"], w=[("pT", 0)])
            P.op("act", lambda e, x=x: e.activation(out=pec[x][:, :], in_=pT[0][:, 0:16], func=AF.Copy), r=[("pT", 0)], w=[("pec", x)])
            for j in range(16):
                P.op("pe", lambda e, x=x, j=j: e.matmul(pO[:, 0:1], lhsT=W1[x][:, j, :], rhs=pec[x][:, j:j + 1], start=(j == 0), stop=(j == 15)),
                     r=[("W1", x), ("pec", x)], w=["pO"])
            P.op("dve", lambda e, x=x: e.tensor_copy(out=peterm[x][:, :], in_=pO[:, 0:1]), r=["pO"], w=[("peterm", x)])
        P.op("pool", lambda e: e.memset(pacc8[:, :], 0.0), w=["pacc8"])
        P.op("pool", lambda e: e.memset(selb8[:, :], 0.0), w=["selb8"])
        P.dma("sp", stile[:, :], kv_out[SEQ:NTOK, :], w=["stile"])
        P.op("pe", lambda e: e.transpose(out=pT[0][:, 0:64], in_=stile[:, 256:384], identity=cst[0:64, 0:64]), r=["stile", "cst"], w=[("pT", 0)])
        P.op("pe", lambda e: e.transpose(out=pT[0][:, 64:128], in_=stile[:, 512:640], identity=cst[0:64, 0:64]), r=["stile", "cst"], w=[("pT", 0)])
        P.op("act", lambda e: e.activation(out=newT[:, :, :].rearrange("p a c -> p (a c)"), in_=pT[0][:, 0:128], func=AF.Copy), r=[("pT", 0)], w=["newT"])
        P.op("pool", lambda e: e.memset(Q64[:, :, :], 0.0), w=["Q64"])
        P.op("pool", lambda e: e.memset(G64[:, :, :], 0.0), w=["G64"])
        for g in range(2):
            for r in range(4):
                h = 4 * g + r
                row0 = g * 32 + r * 4
                srcq = AP(projd.tensor, projd.offset + SEQ * DIN + O_Q + h * 64, [[16 * DIN, 4], [DIN, NSEQ_S], [1, 64]])
                P.dma("sp", Q64[row0:row0 + 4, :, g * 64:(g + 1) * 64], srcq, w=["Q64"])
                srcg = AP(projd.tensor, projd.offset + SEQ * DIN + O_GATE + 3 * h, [[16 * DIN, 4], [DIN, NSEQ_S], [1, 3]])
                P.dma("sp", G64[row0:row0 + 4, :, :], srcg, w=["G64"])
        P.op("act", lambda e: e.activation(out=SG64[:, :, :], in_=G64[:, :, :], func=AF.Sigmoid), r=["G64"], w=["SG64"])

        cnt = {"g": 0}

        def gather(seq, cache, grp):
            slot = cnt["g"] % 4
            cnt["g"] += 1
            rows = cache.rearrange("n t f -> (n t) f")
            for a_ in range(4):
                j = seq * NPG + grp * 4 + a_
                P.op("pool", lambda e, slot=slot, a_=a_, j=j, rows=rows: e.indirect_dma_start(
                    out=pg[slot][:, a_, :], out_offset=None, in_=rows, in_offset=bass.IndirectOffsetOnAxis(ap=idx[:, j:j + 1], axis=0)),
                    r=["idx"], w=[("pg", slot, a_)], dma=True)
            return slot, 0

        def wait_pages(eng, slot, need):
            return None

        def attend_s(seq, br, kT, kkey, nk, Vfn, vkey, bias_fn, first):
            nch = (nk + 511) // 512
            for ch in range(nch):
                k0 = ch * 512
                w = min(512, nk - k0)
                pb = ch % 2
                P.op("pe", lambda e, pb=pb, w=w, k0=k0: e.matmul(pS[pb][0:64, 0:w], lhsT=qT64[:, :], rhs=kT[:, k0:k0 + w], start=True, stop=True),
                     r=["qT64", kkey], w=[("pS", pb)])
                bias_fn(ch, k0, w, pb)
            P.op("dve", lambda e: e.tensor_reduce(out=mx[:, :], in_=Ssb[:, 0:nk], axis=AX.X, op=ALU.max), r=["Ssb"], w=["mx"])
            P.op("dve", lambda e: e.tensor_scalar(out=mx[:, :], in0=mx[:, :], scalar1=-1.0, scalar2=None, op0=ALU.mult), r=["mx"], w=["mx"])
            P.op("pool", lambda e: e.memset(rsum[:, :], 0.0), w=["rsum"])
            if br == 0:
                P.op("act", lambda e: e.activation(out=Pf[:, 0:nk], in_=Ssb[:, 0:nk], func=AF.Exp, bias=mx[:, 0:1], scale=1.0, accum_out=rsum[:, :]),
                     r=["Ssb", "mx", "rsum"], w=["Pf", "rsum"])
                P.op("pool", lambda e: e.tensor_copy(out=Pbf[:, 0:nk], in_=Pf[:, 0:nk]), r=["Pf"], w=["Pbf"])
            else:
                P.op("act", lambda e: e.activation(out=Pbf[:, 0:nk], in_=Ssb[:, 0:nk], func=AF.Exp, bias=mx[:, 0:1], scale=1.0, accum_out=rsum[:, :]),
                     r=["Ssb", "mx", "rsum"], w=["Pbf", "rsum"])
            P.op("dve", lambda e: e.reciprocal(out=rsum[:, :], in_=rsum[:, :]), r=["rsum"], w=["rsum"])
            tiles = [(j * 128, min(128, nk - j * 128)) for j in range((nk + 127) // 128)]
            for j0 in range(0, len(tiles), 16):
                grp = tiles[j0:j0 + 16]
                pb = (j0 // 16) % 2
                for jj, (c0, cn) in enumerate(grp):
                    P.op("pe", lambda e, pb=pb, jj=jj, c0=c0, cn=cn: e.transpose(out=pPT[pb][:cn, jj * 64:(jj + 1) * 64], in_=Pbf[:, c0:c0 + cn], identity=identb[0:64, 0:64]),
                         r=["Pbf", "identb"], w=[("pPT", pb)])
                if all(cn == 128 for _, cn in grp):
                    parts = [(pPT[pb][:, 0:len(grp) * 64].rearrange("p (j t) -> p j t", t=64), PTsb[:, j0:j0 + len(grp), :])]
                else:
                    parts = [(pPT[pb][:cn, jj * 64:(jj + 1) * 64], PTsb[:cn, j0 + jj, :]) for jj, (_, cn) in enumerate(grp)]
                eng = "act" if (j0 // 16) % 2 == 0 else "dve"
                for (srcv, dstv) in parts:
                    if eng == "act":
                        P.op("act", lambda e, srcv=srcv, dstv=dstv: e.activation(out=dstv, in_=srcv, func=AF.Copy), r=[("pPT", pb)], w=["PTsb"])
                    else:
                        P.op("dve", lambda e, srcv=srcv, dstv=dstv: e.tensor_copy(out=dstv, in_=srcv), r=[("pPT", pb)], w=["PTsb"])
            for j, (c0, cn) in enumerate(tiles):
                P.op("pe", lambda e, j=j, cn=cn: e.matmul(pO[0:64, 0:128], lhsT=PTsb[:cn, j, :], rhs=Vfn(j, cn), start=(j == 0), stop=(j == len(tiles) - 1)),
                     r=["PTsb", vkey], w=["pO"])
            P.op("dve", lambda e: e.tensor_tensor(out=gs[:, :], in0=rsum[:, :], in1=SG64[:, seq, br:br + 1], op=ALU.mult), r=["rsum", "SG64"], w=["gs"])
            for g in range(2):
                rows = slice(g * 32, g * 32 + 16)
                if first:
                    P.op("dve", lambda e, rows=rows, g=g: e.tensor_scalar(out=osum[rows, :], in0=pO[rows, g * 64:(g + 1) * 64], scalar1=gs[rows, 0:1], scalar2=None, op0=ALU.mult),
                         r=["pO", "gs"], w=["osum"])
                else:
                    P.op("dve", lambda e, rows=rows, g=g: e.scalar_tensor_tensor(out=osum[rows, :], in0=pO[rows, g * 64:(g + 1) * 64], scalar=gs[rows, 0:1], in1=osum[rows, :],
                                                                                 op0=ALU.mult, op1=ALU.add), r=["pO", "gs", "osum"], w=["osum"])

        for seq in range(NS):
            P.op("pe", lambda e, seq=seq: e.transpose(out=pT[1][:, 0:64], in_=Q64[:, seq, :], identity=cst[0:64, 0:64]), r=["Q64", "cst"], w=[("pT", 1)])
            P.op("act", lambda e: e.activation(out=qT64[:, :], in_=pT[1][:, 0:64], func=AF.Copy, scale=0.125), r=[("pT", 1)], w=["qT64"])
            def gather_cache(cache, kind):
                for grp in range(NPG // 4):
                    slot, need = gather(seq, cache, grp)
                    if kind == "vs":
                        wait_pages("pool", slot, need)
                        P.op("pool", lambda e, slot=slot, grp=grp: e.tensor_copy(out=Vs[:, grp * 4:(grp + 1) * 4, :], in_=pg[slot][:, :, :]),
                             r=[("pg", slot, a_) for a_ in range(4)], w=["Vs"])
                    else:
                        wait_pages("pe", slot, need)
                        tb = grp % 2
                        for a in range(4):
                            P.op("pe", lambda e, slot=slot, a=a, tb=tb: e.transpose(out=pT[tb][:, a * 128:(a + 1) * 128], in_=pg[slot][:, a, :], identity=ident),
                                 r=[("pg", slot, a), "cst"], w=[("pT", tb)])
                        if kind == "ks":
                            dst = kTs[:, grp * 512:(grp + 1) * 512]
                            dkey = "kTs"
                        else:
                            dst = kTc[:, grp * 512:(grp + 1) * 512]
                            dkey = "kTc"
                        if grp % 2 == 0:
                            P.op("act", lambda e, dst=dst, tb=tb: e.activation(out=dst, in_=pT[tb][:, :], func=AF.Copy), r=[("pT", tb)], w=[dkey])
                        else:
                            P.op("dve", lambda e, dst=dst, tb=tb: e.tensor_copy(out=dst, in_=pT[tb][:, :]), r=[("pT", tb)], w=[dkey])
            gather_cache(caches[2], "ks")
            gather_cache(caches[3], "vs")
            P.op("act", lambda e, seq=seq: e.activation(out=kTs[:, NPG * 128:NPG * 128 + 4], in_=newT[:, 0, seq:64:16], func=AF.Copy), r=["newT"], w=["kTs"])
            P.dma("pool", Vs[0:4, NPG, :], kv_out[SEQ + seq:NTOK:16, 384:512], w=["Vs"])
            P.dma("sp", wtile[:, :, 0:128], AP(ckw.tensor, ckw.offset + seq * 512 * 128, [[128, 128], [128 * 128, 4], [1, 128]]), w=["wtileK"])
            for a in range(4):
                P.op("pe", lambda e, a=a: e.transpose(out=pT[0][:, a * 128:(a + 1) * 128], in_=wtile[:, a, 0:128], identity=ident), r=["wtileK", "cst"], w=[("pT", 0)])
            P.op("act", lambda e: e.activation(out=kTw[:, 0:512], in_=pT[0][:, :], func=AF.Copy), r=[("pT", 0)], w=["kTw"])
            P.op("act", lambda e, seq=seq: e.activation(out=kTw[:, 512:516], in_=newT[:, 1, seq:64:16], func=AF.Copy), r=["newT"], w=["kTw"])
            P.dma("pool", Vw[:, 0:4, :], AP(cvw.tensor, cvw.offset + seq * 512 * 128, [[128, 128], [128 * 128, 4], [1, 128]]), w=["Vw"])
            P.dma("pool", Vw[0:4, 4, :], kv_out[SEQ + seq:NTOK:16, 640:768], w=["Vw"])
            for x in range(2):
                gather_cache(caches[x], "kc")
                for g in range(2):
                    b = g % 2
                    for sp_ in range(32):
                        P.op("pe", lambda e, x=x, g=g, sp_=sp_, b=b: e.matmul(pS[b][:, 0:511], lhsT=W1g[x][g][:, sp_, :], rhs=kTc[:, sp_:sp_ + 16 * 510 + 1:16],
                                                                              start=(sp_ == 0), stop=(sp_ == 31)), r=[("W1g", x, g), "kTc"], w=[("pS", b)])
                    P.op("act", lambda e, x=x, b=b: e.activation(out=gx[:, 0:511], in_=pS[b][:, 0:511], func=AF.Identity, bias=peterm[x][:, 0:1], scale=1.0),
                         r=[("pS", b), ("peterm", x)], w=["gx"])
                    P.op("dve", lambda e: e.tensor_tensor(out=gu[:, 0:511], in0=gx[:, 0:511], in1=gx[:, 0:511], op=ALU.mult), r=["gx"], w=["gu"])
                    P.op("dve", lambda e: e.tensor_scalar(out=gu[:, 0:511], in0=gu[:, 0:511], scalar1=GELU_C, scalar2=1.0, op0=ALU.mult, op1=ALU.add), r=["gu"], w=["gu"])
                    P.op("dve", lambda e: e.tensor_tensor(out=gu[:, 0:511], in0=gu[:, 0:511], in1=gx[:, 0:511], op=ALU.mult), r=["gu", "gx"], w=["gu"])
                    P.op("act", lambda e: e.activation(out=gu[:, 0:511], in_=gu[:, 0:511], func=AF.Sigmoid, scale=GELU_S), r=["gu"], w=["gu"])
                    P.op("dve", lambda e, x=x, g=g: e.tensor_tensor(out=gl[x][g][:, 0:511], in0=gu[:, 0:511], in1=gx[:, 0:511], op=ALU.mult), r=["gu", "gx"], w=[("gl", x, g)])
            for g in range(2):
                P.op("pe", lambda e, g=g: e.matmul(pS[0][:, 0:511], lhsT=w2p[0][g][:, :], rhs=gl[0][g][:, 0:511], start=(g == 0), stop=(g == 1)),
                     r=[("w2p", 0, g), ("gl", 0, g)], w=[("pS", 0)])
            P.op("act", lambda e: e.activation(out=kcT[:, 0:511], in_=pS[0][:, 0:511], func=AF.Copy), r=[("pS", 0)], w=["kcT"])
            for ct in range(4):
                c0 = ct * 128
                cn = min(128, 511 - c0)
                for g in range(2):
                    P.op("pe", lambda e, g=g, c0=c0, cn=cn, ct=ct: e.matmul(pS[1][:cn, ct * 128:(ct + 1) * 128], lhsT=gl[1][g][:, c0:c0 + cn], rhs=w2p[1][g][:, :],
                                                                             start=(g == 0), stop=(g == 1)), r=[("w2p", 1, g), ("gl", 1, g)], w=[("pS", 1)])
                P.op("act", lambda e, cn=cn, ct=ct: e.activation(out=vc[:cn, ct, :], in_=pS[1][:cn, ct * 128:(ct + 1) * 128], func=AF.Copy), r=[("pS", 1)], w=["vc"])
            def bias_cmp(ch, k0, w, pb):
                P.op("act", lambda e: e.activation(out=Ssb[:, k0:k0 + w], in_=pS[pb][0:64, 0:w], func=AF.Copy), r=[("pS", pb)], w=["Ssb"])
            attend_s(seq, 0, kcT, "kcT", 511, lambda j, cn: vc[:cn, j, :], "vc", bias_cmp, True)
            P.op("dve", lambda e: e.tensor_scalar(out=Pf[:, 0:511], in0=Pf[:, 0:511], scalar1=rsum[:, 0:1], scalar2=None, op0=ALU.mult), r=["Pf", "rsum"], w=["Pf"])
            P.op("pe", lambda e: e.matmul(pI[0:8, 0:511], lhsT=Amat, rhs=Pf[:, 0:511], start=True, stop=True), r=["Pf", "c3"], w=["pI"])
            P.op("dve", lambda e: e.tensor_copy(out=pacc8[:, 1:512], in_=pI[0:8, 0:511]), r=["pI"], w=["pacc8"])
            P.op("dve", lambda e: e.tensor_tensor(out=imp[:, :], in0=pacc8[:, 0:509:4], in1=pacc8[:, 1:510:4], op=ALU.add), r=["pacc8"], w=["imp"])
            for o in (2, 3, 4):
                P.op("dve", lambda e, o=o: e.tensor_tensor(out=imp[:, :], in0=imp[:, :], in1=pacc8[:, o:o + 509:4], op=ALU.add), r=["pacc8", "imp"], w=["imp"])
            P.op("dve", lambda e: e.tensor_tensor(out=cand[:, :], in0=imp[:, :], in1=candm, op=ALU.mult), r=["imp", "c3"], w=["cand"])
            P.op("dve", lambda e: e.scalar_tensor_tensor(out=cand[:, :], in0=candm, scalar=-1.0, in1=cand[:, :], op0=ALU.add, op1=ALU.add), r=["cand", "c3"], w=["cand"])
            P.op("dve", lambda e: e.max(out=m8[:, 0:8], in_=cand[:, :]), r=["cand"], w=["m8"])
            P.op("dve", lambda e: e.match_replace(out=cand2[:, :], in_to_replace=m8[:, 0:8], in_values=cand[:, :], imm_value=-2.0), r=["cand", "m8"], w=["cand2"])
            P.op("dve", lambda e: e.max(out=m8[:, 8:16], in_=cand2[:, :]), r=["cand2"], w=["m8"])
            P.op("dve", lambda e: e.tensor_scalar(out=selb8[:, 0:128], in0=cand[:, :], scalar1=m8[:, 12:13], scalar2=None, op0=ALU.is_ge), r=["cand", "m8"], w=["selb8"])
            P.op("dve", lambda e: e.tensor_tensor(out=selb8[:, 0:128], in0=selb8[:, 0:128], in1=forced, op=ALU.max), r=["selb8", "c3"], w=["selb8"])
            P.op("dve", lambda e: e.tensor_scalar(out=selb8[:, 0:128], in0=selb8[:, 0:128], scalar1=-1.0, scalar2=-NEG, op0=ALU.add, op1=ALU.mult), r=["selb8"], w=["selb8"])
            P.op("pe", lambda e: e.matmul(pI[0:64, 0:132], lhsT=A2, rhs=selb8[:, 0:132], start=True, stop=True), r=["selb8", "c3"], w=["pI"])
            P.op("dve", lambda e: e.tensor_copy(out=selb64[:, :], in_=pI[0:64, 0:132]), r=["pI"], w=["selb64"])
            def bias_sel(ch, k0, w, pb):
                if w == 512:
                    P.op("dve", lambda e: e.tensor_tensor(out=Ssb[:, k0:k0 + 512].rearrange("p (n d) -> p n d", d=64), in0=pS[pb][0:64, 0:512].rearrange("p (n d) -> p n d", d=64),
                                                          in1=bcast(selb64[:, ch * 8:ch * 8 + 8], [[1, 8], [0, 64]]), op=ALU.add), r=[("pS", pb), "selb64"], w=["Ssb"])
                else:
                    P.op("dve", lambda e: e.tensor_tensor(out=Ssb[:, k0:k0 + w], in0=pS[pb][0:64, 0:w], in1=bias4, op=ALU.add), r=[("pS", pb), "c3"], w=["Ssb"])
            attend_s(seq, 1, kTs, "kTs", NK, lambda j, cn: Vs[:cn, j, :], "Vs", bias_sel, False)
            def bias_win(ch, k0, w, pb):
                P.op("dve", lambda e: e.tensor_tensor(out=Ssb[:, k0:k0 + w], in0=pS[pb][0:64, 0:w], in1=wbias[:, k0:k0 + w], op=ALU.add), r=[("pS", pb), "c3"], w=["Ssb"])
            attend_s(seq, 2, kTw, "kTw", 516, lambda j, cn: Vw[:cn, j, :], "Vw", bias_win, False)
            for g in range(2):
                for r in range(4):
                    h = 4 * g + r
                    row0 = g * 32 + r * 4
                    dst = AP(mixd.tensor, mixd.offset + (SEQ + seq) * D + 512 + h * 64, [[16 * D, 4], [1, 64]])
                    P.dma("sp", dst, osum[row0:row0 + 4, :], r=["osum"])
        P.emit_phase()


def build_program():
    kk = K()
    nc = kk.nc
    x = kk.din("x", [NTOK, D])
    consts_d = kk.din("consts", [128, 514])
    ident_d = consts_d[:, 0:128]
    state_ssm = kk.din("state_ssm", [NSEQ_S, 8, 64, 64])
    state_conv = kk.din("state_conv", [NSEQ_S, 3, 768])
    conv_w = kk.din("conv_w", [4, 768])
    conv_b = kk.din("conv_b", [1, 768])
    dt_bias = kk.din("dt_bias", [1, 8])
    a_log = kk.din("a_log", [1, 8])
    d_skip = kk.din("d_skip", [1, 8])
    ssm_norm_w = kk.din("ssm_norm_w", [1, 512])
    consts2_d = kk.din("consts2", [128, 1024])
    seltab_d = kk.din("seltab", [32, 128, 128])
    consts3_d = kk.din("consts3", [128, 1024])
    page_table = kk.din("page_table", [NSEQ_S, 64], I32)
    caches = [kk.din(n, [N_POOL, 128, 128]) for n in ("cache_k_cmp", "cache_v_cmp", "cache_k_slc", "cache_v_slc")]
    cw = {}
    for pre in ("cmp_k_", "cmp_v_"):
        cw[pre + "w1"] = kk.din(pre + "w1", [2048, 128])
        cw[pre + "w2"] = kk.din(pre + "w2", [128, 64])
        cw[pre + "pe"] = kk.din(pre + "pe", [2048, 1])
    cos_d = kk.din("cos", [NTOK, 32])
    sin_d = kk.din("sin", [NTOK, 32])
    w_in = kk.din("w_in", [D, DIN])
    b_gate = kk.din("b_gate", [1, 24])
    f1g = kk.din("ffn1_w_gate", [D, DFF])
    f1u = kk.din("ffn1_w_up", [D, DFF])
    f1d = kk.din("ffn1_w_down", [DFF, D])
    f2g = kk.din("ffn2_w_gate", [D, DFF])
    f2u = kk.din("ffn2_w_up", [D, DFF])
    f2d = kk.din("ffn2_w_down", [DFF, D])
    w_out = kk.din("w_out", [D, D])
    ln2g = kk.din("ln2_g", [1, D]); ln2b = kk.din("ln2_b", [1, D])
    ln3g = kk.din("ln3_g", [1, D]); ln3b = kk.din("ln3_b", [1, D])
    ln1g = kk.din("ln1_g", [1, D])
    ln1b = kk.din("ln1_b", [1, D])
    ckw = kk.din("cache_k_win", [NSEQ_S, 512, 128])
    cvw = kk.din("cache_v_win", [NSEQ_S, 512, 128])

    kv_out = kk.dout("kv_out", [NTOK, 768])
    p_win = kk.dout("p_win", [2, 512, 128])
    s_win = kk.dout("s_win", [2, NSEQ_S, 512, 128])
    p_conv = kk.dout("p_conv", [3, 768])
    s_conv = kk.dout("s_conv", [NSEQ_S, 3, 768])
    p_ssm = kk.dout("p_ssm", [8, 64, 64])
    s_ssm = kk.dout("s_ssm", [NSEQ_S, 8, 64, 64])
    mixd = kk.dscr("mixd", [NTOK, D])
    y_out = kk.dout("y_out", [NTOK, D])
    h2 = kk.dscr("h2", [NTOK, D])
    h1 = kk.dscr("h1", [NTOK, D])
    projd = kk.dscr("projd", [NTOK, DIN])

    with ExitStack() as st:
        P = Prog(nc, st)
        import os
        PH = os.environ.get("KPHASES", "ffn1,inproj,ssd,nsa,nsas,outproj,ffn2,tail").split(",")
        if "ffn1" in PH:
            ffn_phase(kk, P, "f1_", f1g, f1u, f1d, ln1g, ln1b, x, h1, ident_d)
        if "inproj" in PH:
            inproj_phase(kk, P, h1, projd, kv_out, w_in, b_gate, cos_d, sin_d, ident_d)
        if "ssd" in PH:
            ssd_phase(kk, P, projd, mixd, p_ssm, s_ssm, state_ssm, state_conv, conv_w, conv_b, dt_bias, a_log, d_skip, ssm_norm_w, consts_d)
        if "nsa" in PH:
            nsa_prompt_phase(kk, P, projd, kv_out, mixd, cw, consts_d, consts2_d, seltab_d)
        if "nsas" in PH:
            nsa_sample_phase(kk, P, projd, kv_out, mixd, cw, consts_d, consts3_d, caches, ckw, cvw, page_table)
        if "outproj" in PH:
            outproj_phase(kk, P, mixd, h1, h2, w_out, ln2g, ln2b, ident_d)
        if "ffn2" in PH:
            ffn_phase(kk, P, "f2_", f2g, f2u, f2d, ln3g, ln3b, h2, y_out, ident_d)
        for i in range(2):
            c0 = 512 + 128 * i
            P.dma("sp", p_win[i, :, :], kv_out[SEQ - 512:SEQ, c0:c0 + 128])
            cw = (ckw, cvw)[i]
            P.dma("act", s_win[i, :, 0:508, :], cw[:, 4:512, :])
            P.dma("sp", s_win[i, :, 508:512, :], kv_out[SEQ:NTOK, c0:c0 + 128].rearrange("(t s) f -> s t f", t=TS))
        P.dma("sp", p_conv[:, :], projd[SEQ - 3:SEQ, O_XBC:O_XBC + 768])
        P.dma("sp", s_conv[:, :, :], projd[SEQ:NTOK, O_XBC:O_XBC + 768].rearrange("(t s) f -> s t f", t=TS)[:, 1:4, :])
        P.emit_phase()
        kk.n_inst = P.n_inst
    return kk


_CACHE = {}


def _rope_tables():
    half = 32
    inv = (10000.0 ** (-np.arange(half, dtype=np.float32) / half)).astype(np.float32)
    pos = np.concatenate([np.arange(SEQ), np.repeat(8192 + np.arange(TS), NSEQ_S)]).astype(np.float32)
    ang = (pos[:, None] * inv[None, :]).astype(np.float32)
    return np.cos(ang).astype(np.float32), np.sin(ang).astype(np.float32)


def kernel(**inp):
    if "kk" not in _CACHE:
        _CACHE["kk"] = build_program()
    kk = _CACHE["kk"]
    f32 = np.float32
    cos, sin = _rope_tables()
    ii = np.arange(128)
    consts = np.concatenate([np.eye(128, dtype=f32), (ii[:, None] <= ii[None, :]).astype(f32),
                             (ii[None, :] < ii[:, None]).astype(f32), np.ones((128, 128), f32),
                             (ii[:, None] < 64).astype(f32), (ii[:, None] >= 64).astype(f32)], axis=1)
    tl = ii[:, None].astype(np.int64)
    cvalm = (tl - 16 * np.arange(255)[None, :] - 31).astype(f32)
    causb = np.where(ii[None, :] <= ii[:, None], 0.0, -30000.0).astype(f32)
    jj = np.arange(640)[None, :]
    delta = (jj // 128 - 4) * 128 + (jj % 128) - tl
    wbias = np.where((delta <= 0) & (delta > -512), 0.0, -30000.0).astype(f32)
    cvalid = (tl >= 31).astype(f32)
    consts2 = np.concatenate([cvalm, causb, wbias, cvalid], axis=1).astype(f32)
    qpos = (np.arange(32)[:, None] * 128 + np.arange(128)[None, :])
    cur = (qpos // 64)[:, :, None]
    blk = np.arange(64)[None, None, :]
    candm = ((blk >= 1) & (blk <= cur - 2)).astype(f32)
    forced = (((blk == 0) | (blk == cur) | (blk == cur - 1))).astype(f32)
    seltab = np.concatenate([candm, forced], axis=2).astype(f32)
    c3 = np.zeros((128, 1024), f32)
    rows = np.arange(64)
    rg, rr, rt = rows // 32, (rows % 32) // 4, rows % 4
    used = (rows % 32) < 16
    for rw in rows[used]:
        c3[rw, rg[rw] * 4 + rt[rw]] = 1.0
        c3[rg[rw] * 4 + rt[rw], 8 + rw] = 1.0
    c3[0:64, 72:76] = np.where(np.arange(4)[None, :] <= rt[:, None], 0.0, -30000.0)
    wr = np.arange(516)[None, :]
    wvalid = np.where(wr < 512, wr > rt[:, None], (wr - 512) <= rt[:, None])
    c3[0:64, 76:592] = np.where(wvalid, 0.0, -30000.0)
    c3[0:8, 592:720] = ((np.arange(128) >= 1) & (np.arange(128) <= 126)).astype(f32)[None, :]
    c3[0:8, 720:848] = ((np.arange(128) == 0) | (np.arange(128) == 127)).astype(f32)[None, :]
    c3[:, 848] = np.arange(128, dtype=f32)
    shared = {"consts": consts, "cos": cos, "sin": sin, "consts2": consts2, "seltab": seltab, "consts3": c3}
    for name in ("cache_k_cmp", "cache_v_cmp", "cache_k_slc", "cache_v_slc"):
        if name in kk.ins:
            shared[name] = np.asarray(inp[name])[0].reshape(-1, 128, 128)
    for name in ("w_in", "b_gate", "ffn1_w_gate", "ffn1_w_up", "ffn1_w_down", "ln1_g", "ln1_b",
                 "conv_w", "conv_b", "dt_bias", "a_log", "d_skip", "ssm_norm_w",
                 "ffn2_w_gate", "ffn2_w_up", "ffn2_w_down", "w_out", "ln2_g", "ln2_b", "ln3_g", "ln3_b",
                 "cmp_k_w1", "cmp_k_w2", "cmp_k_pe", "cmp_v_w1", "cmp_v_w2", "cmp_v_pe"):
        if name in kk.ins:
            a = np.asarray(inp[name])
            shared[name] = np.ascontiguousarray(a[0].reshape(kk.ins[name].shape))
    in_maps = []
    for c in range(NCORES):
        m = dict(shared)
        xs = np.asarray(inp["x_sample"])[c * NSEQ_S:(c + 1) * NSEQ_S].transpose(1, 0, 2).reshape(NSEQ_S * TS, D)
        m["x"] = np.ascontiguousarray(np.concatenate([np.asarray(inp["x_prompt"])[c], xs], axis=0))
        m["page_table"] = np.ascontiguousarray(np.asarray(inp["page_table"])[c * NSEQ_S:(c + 1) * NSEQ_S]).astype(np.int32)
        m["state_ssm"] = np.ascontiguousarray(np.asarray(inp["state_ssm"])[0, c * NSEQ_S:(c + 1) * NSEQ_S])
        m["state_conv"] = np.ascontiguousarray(np.asarray(inp["state_conv"])[0, c * NSEQ_S:(c + 1) * NSEQ_S])
        m["cache_k_win"] = np.ascontiguousarray(np.asarray(inp["cache_k_win"])[0, c * NSEQ_S:(c + 1) * NSEQ_S].reshape(NSEQ_S, 512, 128))
        m["cache_v_win"] = np.ascontiguousarray(np.asarray(inp["cache_v_win"])[0, c * NSEQ_S:(c + 1) * NSEQ_S].reshape(NSEQ_S, 512, 128))
        in_maps.append({k: v for k, v in m.items() if k in kk.ins})
    res = run_bass_kernel_spmd(kk.nc, in_maps, core_ids=list(range(NCORES)))
    R = res.results
    _CACHE["last"] = R

    def cat(name, sl=None):
        return np.stack([np.asarray(r[name]) if sl is None else np.asarray(r[name])[sl] for r in R], axis=0)

    kv = cat("kv_out")
    kvp = kv[:, :SEQ].reshape(NCORES, SEQ, 6, 2, 64)
    kvs = kv[:, SEQ:].reshape(NCORES, TS, NSEQ_S, 6, 2, 64).transpose(0, 2, 1, 3, 4, 5).reshape(NCORES * NSEQ_S, TS, 6, 2, 64)
    outs = []
    yo = cat("y_out")
    y_p = np.ascontiguousarray(yo[:, :SEQ])
    y_s = np.ascontiguousarray(yo[:, SEQ:].reshape(NCORES, TS, NSEQ_S, D).transpose(0, 2, 1, 3).reshape(NCORES * NSEQ_S, TS, D))
    outs += [y_p, y_s]
    for i in range(4):
        outs.append(np.ascontiguousarray(kvp[:, :, i])[None])
        outs.append(np.ascontiguousarray(kvs[:, :, i])[None])
    pw = cat("p_win")
    sw = cat("s_win")
    for i in range(2):
        outs.append(np.ascontiguousarray(pw[:, i]).reshape(1, 8, 512, 2, 64))
        outs.append(np.ascontiguousarray(sw[:, i]).reshape(1, 128, 512, 2, 64))
    outs.append(cat("p_ssm")[None])
    outs.append(cat("s_ssm").reshape(1, 128, 8, 64, 64))
    outs.append(cat("p_conv")[None])
    outs.append(cat("s_conv").reshape(1, 128, 3, 768))
    return tuple(outs)
```

```python
import os
import numpy as np
from contextlib import ExitStack
import concourse.bass as bass
import concourse.mybir as mybir
from concourse.bass_utils import run_bass_kernel_spmd

F32 = mybir.dt.float32
BF16 = mybir.dt.bfloat16
I32 = mybir.dt.int32
AF = mybir.ActivationFunctionType
ALU = mybir.AluOpType
AX = mybir.AxisListType
AP = bass.AP

NCORES = 8
D = 1024
DFF = 2816
NJ = DFF // 128
SEQ = 4096
NSEQ_S = 16
TS = 4
NTOK = SEQ + NSEQ_S * TS
DIN = 2592
N_POOL = int(os.environ.get('KN_POOL', '10240'))
ALPHA = 2.0 ** 0.25
LN_EPS = 1e-5
O_Z, O_XBC, O_DT, O_Q, O_KV, O_GATE = 0, 512, 1280, 1288, 1800, 2568

N_DMA_SEMS = 40
SYNC_SAME_ENGINE = True


PSUM_NAMES = {"pT", "pGU", "pD", "pP", "pA", "pM", "pC", "pY", "pY0", "pS", "pO", "pPT"}
PSUM_ALIAS = {"pA0": "pA", "pA1": "pA", "pO0": ("pO", 0)}


def _norm_key(k):
    if isinstance(k, str) and k in PSUM_ALIAS:
        return PSUM_ALIAS[k]
    if isinstance(k, tuple) and k[0] == "pTq":
        return "pT"
    return k


def _is_psum_key(k):
    n = k[0] if isinstance(k, tuple) else k
    return isinstance(n, str) and n in PSUM_NAMES


class _Op:
    __slots__ = ("eng", "fn", "dma", "deps", "signals", "sem", "val", "prewait")

    def __init__(self, eng, fn, dma):
        self.eng = eng
        self.fn = fn
        self.dma = dma
        self.deps = []
        self.signals = False
        self.sem = None
        self.val = None
        self.prewait = None


class Prog:
    ENGS = ("pe", "act", "dve", "pool", "sp")

    def __init__(self, nc, stack):
        self.nc = nc
        self.esem = {e: stack.enter_context(nc.semaphore("es_" + e)) for e in ("pe", "act", "dve", "pool")}
        self.ecount = {e: 0 for e in self.esem}
        self.dsems = [stack.enter_context(nc.semaphore("ds%d" % i)) for i in range(N_DMA_SEMS)]
        self.dcount = [0] * N_DMA_SEMS
        self.dlast = [None] * N_DMA_SEMS
        self.ndma = 0
        self.waited = {e: {} for e in self.ENGS}
        self.n_inst = 0
        self._reset_phase()

    def _reset_phase(self):
        self.ops = []
        self.last_w = {}
        self.readers = {}

    def op(self, eng, fn, r=(), w=(), dma=False):
        r = [_norm_key(k) for k in r]
        w = [_norm_key(k) for k in w]
        w = w + [k for k in r if _is_psum_key(k) and k not in w]
        o = _Op(eng, fn, dma)
        deps = []
        for k in r:
            lw = self.last_w.get(k)
            if lw is not None:
                deps.append(lw)
        for k in w:
            lw = self.last_w.get(k)
            if lw is not None:
                deps.append(lw)
            deps.extend(self.readers.get(k, ()))
        seen = set()
        for d in deps:
            if id(d) in seen:
                continue
            seen.add(id(d))
            if (not d.dma) and d.eng == eng and (not dma) and (eng == "pe" or not SYNC_SAME_ENGINE):
                continue
            o.deps.append(d)
            d.signals = True
        if dma:
            o.signals = True
            s = self.ndma % N_DMA_SEMS
            self.ndma += 1
            o.prewait = self.dlast[s]
            self.dcount[s] += 1
            o.sem = self.dsems[s]
            o.val = 16 * self.dcount[s]
            self.dlast[s] = o
        for k in r:
            self.readers.setdefault(k, []).append(o)
        for k in w:
            self.last_w[k] = o
            self.readers[k] = []
        self.ops.append(o)
        return o

    def dma(self, eng, out, in_, r=(), w=(), **kw):
        return self.op(eng, lambda e: e.dma_start(out=out, in_=in_, **kw), r=r, w=w, dma=True)

    def emit_phase(self):
        nc = self.nc
        by_eng = {e: [o for o in self.ops if o.eng == e] for e in self.ENGS}
        for e in self.esem:
            lst = [o for o in by_eng[e] if not o.dma]
            if lst:
                lst[-1].signals = True
        for o in self.ops:
            if not o.dma and o.signals:
                self.ecount[o.eng] += 1
                o.sem = self.esem[o.eng]
                o.val = self.ecount[o.eng]
        targets = [(self.esem[e], self.ecount[e]) for e in self.esem if self.ecount[e] > 0]
        targets += [(self.dsems[i], 16 * self.dcount[i]) for i in range(N_DMA_SEMS) if self.dcount[i] > 0]
        prog = self

        def emit_engine(ename, eng):
            waited = prog.waited[ename]

            def wait(sem, val):
                key = sem.num
                if waited.get(key, 0) >= val:
                    return
                waited[key] = val
                eng.wait_ge(sem, val)
                prog.n_inst += 1

            for o in by_eng[ename]:
                if o.prewait is not None:
                    wait(o.prewait.sem, o.prewait.val)
                for d in o.deps:
                    wait(d.sem, d.val)
                inst = o.fn(eng)
                prog.n_inst += 1
                if o.signals:
                    inst.then_inc(o.sem, 16 if o.dma else 1)
            for (s, v) in targets:
                wait(s, v)

        with nc.Block() as block:
            @block.sync
            def _(e):
                emit_engine("sp", e)

            @block.tensor
            def _(e):
                emit_engine("pe", e)

            @block.scalar
            def _(e):
                emit_engine("act", e)

            @block.vector
            def _(e):
                emit_engine("dve", e)

            @block.gpsimd
            def _(e):
                emit_engine("pool", e)
        self._reset_phase()


def bcast(ap, dims):
    return AP(ap.tensor, ap.offset, [list(ap.ap[0])] + [list(d) for d in dims])


TILES = [(i * 128, 128) for i in range(32)] + [(SEQ, 64)]
GROUPS = [[TILES[2 * g], TILES[2 * g + 1]] for g in range(16)] + [[TILES[32]]]


class K:
    def __init__(self):
        self.nc = bass.Bass("TRN2", target_bir_lowering=False)
        self.ins = {}
        self.outs = {}

    def din(self, name, shape, dt=F32):
        t = self.nc.dram_tensor(name, list(shape), dt, kind="ExternalInput").ap()
        self.ins[name] = t
        return t

    def dout(self, name, shape, dt=F32):
        t = self.nc.dram_tensor(name, list(shape), dt, kind="ExternalOutput").ap()
        self.outs[name] = t
        return t

    def dscr(self, name, shape, dt=F32):
        return self.nc.dram_tensor(name, list(shape), dt, kind="Internal").ap()


def layer_norm_tile(P, sb, z, nr, gt, bt, out, key_z, key_out, tag):
    stats, mv, rstd, nb = sb["stats"], sb["mv"], sb["rstd"], sb["nb"]
    for c in range(2):
        P.op("dve", lambda e, c=c: e.bn_stats(out=stats[:nr, c, :], in_=z[:nr, c * 512:(c + 1) * 512]),
             r=[key_z], w=["stats"])
    P.op("dve", lambda e: e.bn_aggr(out=mv[:nr, :], in_=stats[:nr, :, :]), r=["stats"], w=["mv"])
    P.op("act", lambda e: e.activation(out=rstd[:nr, :], in_=mv[:nr, 1:2], func=AF.Sqrt, bias=sb["eps"][:nr, :], scale=1.0),
         r=["mv"], w=["rstd"])
    P.op("dve", lambda e: e.reciprocal(out=rstd[:nr, :], in_=rstd[:nr, :]), r=["rstd"], w=["rstd"])
    P.op("dve", lambda e: e.scalar_tensor_tensor(out=nb[:nr, :], in0=mv[:nr, 0:1], scalar=-1.0, in1=rstd[:nr, :],
                                                 op0=ALU.mult, op1=ALU.mult), r=["mv", "rstd"], w=["nb"])
    P.op("act", lambda e: e.activation(out=out[:nr, :], in_=z[:nr, :], func=AF.Identity, bias=nb[:nr, 0:1], scale=rstd[:nr, 0:1]),
         r=[key_z, "nb", "rstd"], w=[key_out])
    P.op("pool", lambda e: e.tensor_tensor(out=out[:nr, :], in0=out[:nr, :], in1=gt[:nr, :], op=ALU.mult),
         r=[key_out, tag + "g"], w=[key_out])
    P.op("pool", lambda e: e.tensor_tensor(out=out[:nr, :], in0=out[:nr, :], in1=bt[:nr, :], op=ALU.add),
         r=[key_out, tag + "b"], w=[key_out])


def transpose_tile(P, src, nr, key_src, pT, dst_fn, key_dst, ident, evac_engs=("act", "dve")):
    for kh in range(2):
        for kk in range(4):
            k = kh * 4 + kk
            P.op("pe", lambda e, kk=kk, k=k, kh=kh: e.transpose(out=pT[kh][:, kk * 128:kk * 128 + nr],
                                                                 in_=src[:nr, k * 128:(k + 1) * 128], identity=ident[:nr, :nr]),
                 r=[key_src, "ident"], w=[("pT", kh)])
        eng = evac_engs[kh % len(evac_engs)]
        src_ap = pT[kh][:, :].rearrange("p (k t) -> p k t", k=4)[:, :, :nr]
        if eng == "act":
            P.op("act", lambda e, kh=kh, src_ap=src_ap: e.activation(out=dst_fn(kh), in_=src_ap, func=AF.Copy),
                 r=[("pT", kh)], w=[key_dst])
        else:
            P.op(eng, lambda e, kh=kh, src_ap=src_ap: e.tensor_copy(out=dst_fn(kh), in_=src_ap),
                 r=[("pT", kh)], w=[key_dst])


def ffn_phase(kk, P, tag, w_gate, w_up, w_down, ln_g, ln_b, src, dst, ident_d, prologue=None):
    nc = kk.nc
    with ExitStack() as st:
        def sbt(n, s, d=F32):
            return st.enter_context(nc.sbuf_tensor(tag + n, s, d))
        Wg = sbt("Wg", [128, 8, DFF], BF16)
        Wu = sbt("Wu", [128, 8, DFF], BF16)
        Wd = sbt("Wd", [128, NJ, D], BF16)
        ident = sbt("ident", [128, 128])
        gt = sbt("gt", [128, D])
        bt = sbt("bt", [128, D])
        xs = [[sbt("xs%d%d" % (s, t), [128, D]) for t in range(2)] for s in range(2)]
        xT = [sbt("xT%d" % s, [128, 8, 256], BF16) for s in range(2)]
        sg = [sbt("sg%d" % s, [128, 256]) for s in range(2)]
        hT = [sbt("hT%d" % s, [128, 256], BF16) for s in range(3)]
        zt = [sbt("zt%d" % s, [128, D]) for s in range(2)]
        ot = [sbt("ot%d" % s, [128, D]) for s in range(2)]
        sb = dict(stats=sbt("stats", [128, 2, 6]), mv=sbt("mv", [128, 2]), rstd=sbt("rstd", [128, 1]),
                  nb=sbt("nb", [128, 1]), eps=sbt("eps", [128, 1]))
        pT = [st.enter_context(nc.psum_tensor(tag + "pT%d" % i, [128, 512], F32)) for i in range(2)]
        pGU = [st.enter_context(nc.psum_tensor(tag + "pGU%d" % i, [128, 512], F32)) for i in range(2)]
        pD = [[st.enter_context(nc.psum_tensor(tag + "pD%d%d" % (t, h), [128, 512], F32)) for h in range(2)] for t in range(2)]
        extra = prologue.alloc(st) if prologue is not None else None

        P.op("pool", lambda e: e.memset(sb["eps"][:, :], LN_EPS), w=["eps"])
        P.dma("sp", ident[:, :], ident_d[:, :], w=["ident"])
        P.dma("sp", gt[:, :], ln_g[0:1, :].partition_broadcast(128), w=[tag + "g"])
        P.dma("sp", bt[:, :], ln_b[0:1, :].partition_broadcast(128), w=[tag + "b"])
        if prologue is not None:
            prologue.load_consts(P, extra)
        for k in range(8):
            P.dma("pool", Wg[:, k, :], w_gate[k * 128:(k + 1) * 128, :], w=[("Wg", k)])
            P.dma("pool", Wu[:, k, :], w_up[k * 128:(k + 1) * 128, :], w=[("Wu", k)])
        for j in range(NJ):
            P.dma("pool", Wd[:, j, :], w_down[j * 128:(j + 1) * 128, :], w=[("Wd", j)])

        def load_group(G):
            slot = G % 2
            for ti, (r0, nr) in enumerate(GROUPS[G]):
                key = ("xs", slot, ti)
                if prologue is None:
                    P.dma("sp", xs[slot][ti][:nr, :], src[r0:r0 + nr, :], w=[key])
                else:
                    prologue.emit(P, extra, slot, ti, r0, nr, xs[slot][ti], key, pT, ident)
                transpose_tile(P, xs[slot][ti], nr, key, pT,
                               lambda kh, slot=slot, ti=ti, nr=nr: xT[slot][:, kh * 4:(kh + 1) * 4, ti * 128:ti * 128 + nr],
                               ("xT", slot), ident)
                P.op("pool", lambda e, slot=slot, ti=ti, nr=nr: e.tensor_scalar(out=xs[slot][ti][:nr, :], in0=xs[slot][ti][:nr, :],
                                                                                 scalar1=ALPHA, scalar2=None, op0=ALU.mult),
                     r=[key], w=[key])

        def gu(G, j):
            slot = G % 2
            ntok = sum(nr for _, nr in GROUPS[G])
            b = j % 2
            for (W, off, wk) in ((Wg, 0, "Wg"), (Wu, 256, "Wu")):
                for k in range(8):
                    P.op("pe", lambda e, W=W, off=off, k=k: e.matmul(pGU[b][:, off:off + ntok], lhsT=W[:, k, j * 128:(j + 1) * 128],
                                                                      rhs=xT[slot][:, k, :ntok], start=(k == 0), stop=(k == 7)),
                         r=[("xT", slot), (wk, k)], w=[("pGU", b)])
            P.op("act", lambda e: e.activation(out=sg[b][:, :ntok], in_=pGU[b][:, 0:ntok], func=AF.Silu),
                 r=[("pGU", b)], w=[("sg", b)])
            h3 = j % 3
            P.op("dve", lambda e: e.tensor_tensor(out=hT[h3][:, :ntok], in0=sg[b][:, :ntok], in1=pGU[b][:, 256:256 + ntok], op=ALU.mult),
                 r=[("sg", b), ("pGU", b)], w=[("hT", h3)])

        def down(G, j):
            h3 = j % 3
            for ti, (r0, nr) in enumerate(GROUPS[G]):
                for half in range(2):
                    P.op("pe", lambda e, ti=ti, nr=nr, half=half: e.matmul(pD[ti][half][:nr, :], lhsT=hT[h3][:, ti * 128:ti * 128 + nr],
                                                                            rhs=Wd[:, j, half * 512:(half + 1) * 512],
                                                                            start=(j == 0), stop=(j == NJ - 1)),
                         r=[("hT", h3), ("Wd", j)], w=[("pD", ti, half)])

        def epilogue(G):
            slot = G % 2
            for ti, (r0, nr) in enumerate(GROUPS[G]):
                key = ("xs", slot, ti)
                zs = (G * 2 + ti) % 2
                for half in range(2):
                    P.op("dve", lambda e, ti=ti, nr=nr, half=half, zs=zs: e.scalar_tensor_tensor(
                        out=zt[zs][:nr, half * 512:(half + 1) * 512], in0=pD[ti][half][:nr, :], scalar=0.5,
                        in1=xs[slot][ti][:nr, half * 512:(half + 1) * 512], op0=ALU.mult, op1=ALU.add),
                        r=[("pD", ti, half), key], w=[("zt", zs)])
                layer_norm_tile(P, sb, zt[zs], nr, gt, bt, ot[zs], ("zt", zs), ("ot", zs), tag)
                P.dma("sp", dst[r0:r0 + nr, :], ot[zs][:nr, :], r=[("ot", zs)], w=[("dst", r0)])

        nG = len(GROUPS)
        load_group(0)
        for G in range(nG):
            for j in range(NJ):
                gu(G, j)
                if j > 0:
                    down(G, j - 1)
                if j == 8 and G + 1 < nG:
                    load_group(G + 1)
            down(G, NJ - 1)
            epilogue(G)
        P.emit_phase()


def inproj_phase(kk, P, h1, projd, kv_out, w_in, b_gate, cos_d, sin_d, ident_d):
    nc = kk.nc
    with ExitStack() as st:
        def sbt(n, s, d=F32):
            return st.enter_context(nc.sbuf_tensor("ip_" + n, s, d))
        Win = sbt("Win", [128, 8, DIN], BF16)
        ident = sbt("ident", [128, 128])
        cosT = sbt("cos", [128, 33, 32])
        sinT = sbt("sin", [128, 33, 32])
        bg = sbt("bg", [128, 24])
        ht = [sbt("ht%d" % s, [128, D]) for s in range(2)]
        hT = [sbt("hT%d" % s, [128, 8, 128], BF16) for s in range(2)]
        proj = [sbt("proj%d" % s, [128, DIN]) for s in range(2)]
        tmps = {"dve": [sbt("tmpa%d" % i, [128, 8, 32]) for i in range(4)],
                "pool": [sbt("tmpb%d" % i, [128, 8, 32]) for i in range(4)]}
        pT = [st.enter_context(nc.psum_tensor("ip_pT%d" % i, [128, 512], F32)) for i in range(2)]
        pP = [st.enter_context(nc.psum_tensor("ip_pP%d" % i, [128, 512], F32)) for i in range(6)]
        P.dma("sp", ident[:, :], ident_d[:, :], w=["ident"])
        P.dma("sp", cosT[:, 0:32, :], cos_d[0:SEQ, :].rearrange("(t p) d -> p t d", p=128), w=["cos"])
        P.dma("sp", sinT[:, 0:32, :], sin_d[0:SEQ, :].rearrange("(t p) d -> p t d", p=128), w=["sin"])
        P.dma("sp", cosT[0:64, 32, :], cos_d[SEQ:NTOK, :], w=["cos"])
        P.dma("sp", sinT[0:64, 32, :], sin_d[SEQ:NTOK, :], w=["sin"])
        P.dma("sp", bg[:, :], b_gate[0:1, :].partition_broadcast(128), w=["bg"])
        for k in range(8):
            P.dma("pool", Win[:, k, :], w_in[k * 128:(k + 1) * 128, :], w=[("Win", k)])
        chunks = [(0, 512), (512, 512), (1024, 264), (1288, 512), (1800, 512), (2312, 280)]
        for t, (r0, nr) in enumerate(TILES):
            s = t % 2
            P.dma("sp", ht[s][:nr, :], h1[r0:r0 + nr, :], w=[("ht", s)])
            transpose_tile(P, ht[s], nr, ("ht", s), pT, lambda kh, s=s, nr=nr: hT[s][:, kh * 4:(kh + 1) * 4, :nr], ("hT", s), ident)
            for ci, (c0, wd) in enumerate(chunks):
                for k in range(8):
                    P.op("pe", lambda e, ci=ci, c0=c0, wd=wd, k=k, s=s, nr=nr: e.matmul(
                        pP[ci][:nr, :wd], lhsT=hT[s][:, k, :nr], rhs=Win[:, k, c0:c0 + wd], start=(k == 0), stop=(k == 7)),
                        r=[("hT", s), ("Win", k)], w=[("pP", ci)])
                if ci % 2 == 0:
                    P.op("act", lambda e, ci=ci, c0=c0, wd=wd, s=s, nr=nr: e.activation(out=proj[s][:nr, c0:c0 + wd], in_=pP[ci][:nr, :wd], func=AF.Copy),
                         r=[("pP", ci)], w=[("proj", s, ci)])
                else:
                    P.op("dve", lambda e, ci=ci, c0=c0, wd=wd, s=s, nr=nr: e.tensor_copy(out=proj[s][:nr, c0:c0 + wd], in_=pP[ci][:nr, :wd]),
                         r=[("pP", ci)], w=[("proj", s, ci)])
            groups = [(O_Q, 8, 3), (O_KV, 2, 4), (O_KV + 256, 2, 4), (O_KV + 512, 2, 5)]
            for gi, (c0, H, ci) in enumerate(groups):
                eng = "dve" if gi % 2 == 0 else "pool"
                X = proj[s][:nr, c0:c0 + H * 64].rearrange("p (h d) -> p h d", h=H)
                x1 = X[:, :, 0:32]
                x2 = X[:, :, 32:64]
                cb = bcast(cosT[:nr, t, :], [[0, H], [1, 32]])
                sn = bcast(sinT[:nr, t, :], [[0, H], [1, 32]])
                key = ("proj", s, ci)
                t1, t2, t3, t4 = [tmps[eng][i][:nr, 0:H, :] for i in range(4)]
                tk = ("tmp", eng)
                P.op(eng, lambda e, t1=t1, x1=x1, cb=cb: e.tensor_tensor(out=t1, in0=x1, in1=cb, op=ALU.mult), r=[key, "cos"], w=[tk])
                P.op(eng, lambda e, t2=t2, x2=x2, sn=sn: e.tensor_tensor(out=t2, in0=x2, in1=sn, op=ALU.mult), r=[key, "sin"], w=[tk])
                P.op(eng, lambda e, t3=t3, x2=x2, cb=cb: e.tensor_tensor(out=t3, in0=x2, in1=cb, op=ALU.mult), r=[key, "cos"], w=[tk])
                P.op(eng, lambda e, t4=t4, x1=x1, sn=sn: e.tensor_tensor(out=t4, in0=x1, in1=sn, op=ALU.mult), r=[key, "sin"], w=[tk])
                P.op(eng, lambda e, t1=t1, t2=t2, x1=x1: e.tensor_tensor(out=x1, in0=t1, in1=t2, op=ALU.subtract), r=[tk], w=[key])
                P.op(eng, lambda e, t3=t3, t4=t4, x2=x2: e.tensor_tensor(out=x2, in0=t3, in1=t4, op=ALU.add), r=[tk], w=[key])
            P.op("dve", lambda e, s=s, nr=nr: e.tensor_tensor(out=proj[s][:nr, O_GATE:O_GATE + 24], in0=proj[s][:nr, O_GATE:O_GATE + 24],
                                                              in1=bg[:nr, :], op=ALU.add), r=[("proj", s, 5), "bg"], w=[("proj", s, 5)])
            allk = [("proj", s, ci) for ci in range(6)]
            P.dma("sp", projd[r0:r0 + nr, :], proj[s][:nr, :], r=allk, w=[("projd", t)])
            P.dma("sp", kv_out[r0:r0 + nr, :], proj[s][:nr, O_KV:O_KV + 768], r=allk, w=[("kvo", t)])
        P.emit_phase()


def conv_silu(P, nr, xsh, s, cw, cb, acc, acc2, tP, tD, xa):
    kA, kB, kX = ("acc", s), ("acc2", s), ("xa", s)
    P.op("pool", lambda e: e.tensor_tensor(out=acc[s][:nr, :], in0=xsh[0][s][:nr, :], in1=cw[:nr, 0, :], op=ALU.mult), r=[("xsh", 0, s), "cw"], w=[kA])
    P.op("pool", lambda e: e.tensor_tensor(out=tP[:nr, :], in0=xsh[1][s][:nr, :], in1=cw[:nr, 1, :], op=ALU.mult), r=[("xsh", 1, s), "cw"], w=["tP"])
    P.op("pool", lambda e: e.tensor_tensor(out=acc[s][:nr, :], in0=acc[s][:nr, :], in1=tP[:nr, :], op=ALU.add), r=[kA, "tP"], w=[kA])
    P.op("dve", lambda e: e.tensor_tensor(out=acc2[s][:nr, :], in0=xsh[2][s][:nr, :], in1=cw[:nr, 2, :], op=ALU.mult), r=[("xsh", 2, s), "cw"], w=[kB])
    P.op("dve", lambda e: e.tensor_tensor(out=tD[:nr, :], in0=xsh[3][s][:nr, :], in1=cw[:nr, 3, :], op=ALU.mult), r=[("xsh", 3, s), "cw"], w=["tD"])
    P.op("dve", lambda e: e.tensor_tensor(out=acc2[s][:nr, :], in0=acc2[s][:nr, :], in1=tD[:nr, :], op=ALU.add), r=[kB, "tD"], w=[kB])
    P.op("dve", lambda e: e.tensor_tensor(out=acc2[s][:nr, :], in0=acc2[s][:nr, :], in1=cb[:nr, :], op=ALU.add), r=[kB, "cb"], w=[kB])
    P.op("dve", lambda e: e.tensor_tensor(out=acc2[s][:nr, :], in0=acc2[s][:nr, :], in1=acc[s][:nr, :], op=ALU.add), r=[kB, kA], w=[kB])
    P.op("act", lambda e: e.activation(out=xa[s][:nr, :], in_=acc2[s][:nr, :], func=AF.Silu), r=[kB], w=[kX])


def softplus_dt(P, nr, dtr, s, dtb, aneg, dt, dA):
    P.op("dve", lambda e: e.tensor_tensor(out=dtr[s][:nr, :], in0=dtr[s][:nr, :], in1=dtb[:nr, :], op=ALU.add), r=[("dtr", s), "dtb"], w=[("dtr", s)])
    P.op("act", lambda e: e.activation(out=dtr[s][:nr, :], in_=dtr[s][:nr, :], func=AF.Exp), r=[("dtr", s)], w=[("dtr", s)])
    P.op("act", lambda e: e.activation(out=dt[s][:nr, :], in_=dtr[s][:nr, :], func=AF.Ln, bias=1.0, scale=1.0), r=[("dtr", s)], w=[("dt", s)])
    P.op("dve", lambda e: e.tensor_tensor(out=dA[s][:nr, :], in0=dt[s][:nr, :], in1=aneg[:nr, :], op=ALU.mult), r=[("dt", s), "aneg"], w=[("dA", s)])


def gate_norm_store(P, nr, yt, ky, xa_x, kx, zt, kz, dsk, nw, xd, sz, junk, ss, rstd, eps, dst_ap):
    P.op("pool", lambda e: e.tensor_tensor(out=xd[:nr, :].rearrange("p (h d) -> p h d", h=8), in0=xa_x.rearrange("p (h d) -> p h d", h=8),
                                           in1=bcast(dsk[:nr, :], [[1, 8], [0, 64]]), op=ALU.mult), r=[kx, "dsk"], w=["xd"])
    P.op("pool", lambda e: e.tensor_tensor(out=yt[:nr, :], in0=yt[:nr, :], in1=xd[:nr, :], op=ALU.add), r=[ky, "xd"], w=[ky])
    P.op("act", lambda e: e.activation(out=sz[:nr, :], in_=zt[:nr, :], func=AF.Silu), r=[kz], w=["sz"])
    P.op("pool", lambda e: e.tensor_tensor(out=yt[:nr, :], in0=yt[:nr, :], in1=sz[:nr, :], op=ALU.mult), r=[ky, "sz"], w=[ky])
    P.op("pool", lambda e: e.memset(ss[:nr, :], 0.0), w=["ss"])
    P.op("act", lambda e: e.activation(out=junk[:nr, :], in_=yt[:nr, :], func=AF.Square, accum_out=ss[:nr, :]), r=[ky, "ss"], w=["junk", "ss"])
    P.op("act", lambda e: e.activation(out=rstd[:nr, :], in_=ss[:nr, :], func=AF.Sqrt, bias=eps[:nr, :], scale=1.0 / 512.0), r=["ss", "eps"], w=["rstd"])
    P.op("dve", lambda e: e.reciprocal(out=rstd[:nr, :], in_=rstd[:nr, :]), r=["rstd"], w=["rstd"])
    P.op("dve", lambda e: e.scalar_tensor_tensor(out=yt[:nr, :], in0=yt[:nr, :], scalar=rstd[:nr, 0:1], in1=nw[:nr, :], op0=ALU.mult, op1=ALU.mult),
         r=[ky, "rstd", "nw"], w=[ky])
    P.dma("sp", dst_ap, yt[:nr, :], r=[ky])


def ssd_phase(kk, P, projd, mixd, p_ssm, s_ssm, state_ssm, state_conv, conv_w, conv_b, dt_bias, a_log, d_skip, ssm_norm_w, consts_d):
    nc = kk.nc
    hist_s = kk.dscr("hist_s", [112, 768])
    xs_d = kk.dscr("xs_d", [64, 512])
    bc_d = kk.dscr("bc_d", [64, 256])
    dts_d = kk.dscr("dts_d", [64, 16])
    ys_d = kk.dscr("ys_d", [64, 512])
    XBC = slice(O_XBC, O_XBC + 768)
    with ExitStack() as st:
        def sbt(n, s, d=F32):
            return st.enter_context(nc.sbuf_tensor("sd_" + n, s, d))
        cst = sbt("cst", [128, 514])
        ident, U, Lt, ones = cst[:, 0:128], cst[:, 128:256], cst[:, 256:384], cst[:, 384:512]
        rmask = cst[:, 512:514]
        cw = sbt("cw", [128, 4, 768]); cb = sbt("cb", [128, 768])
        dtb = sbt("dtb", [128, 8]); aneg = sbt("aneg", [128, 8]); dsk = sbt("dsk", [128, 8]); nw = sbt("nw", [128, 512])
        eps = sbt("eps", [128, 1])
        xsh = [[sbt("xsh%d%d" % (i, s), [128, 768]) for s in range(2)] for i in range(4)]
        zt = [sbt("zt%d" % s, [128, 512]) for s in range(2)]
        acc = [sbt("acc%d" % s, [128, 768]) for s in range(2)]
        acc2 = [sbt("acc2%d" % s, [128, 768]) for s in range(2)]
        xa = [sbt("xa%d" % s, [128, 768]) for s in range(2)]
        tP = sbt("tP", [128, 768]); tD = sbt("tD", [128, 768])
        dtr = [sbt("dtr%d" % s, [128, 8]) for s in range(2)]
        dt = [sbt("dt%d" % s, [128, 8]) for s in range(2)]
        dA = [sbt("dA%d" % s, [128, 8]) for s in range(2)]
        xdt = [sbt("xdt%d" % s, [128, 512], BF16) for s in range(2)]
        Bbf = [sbt("Bbf%d" % s, [128, 128], BF16) for s in range(2)]
        BCT = [[sbt("BCT%d%d" % (s, g), [128, 256], BF16) for g in range(2)] for s in range(2)]
        LdA = [sbt("LdA%d" % s, [128, 8, 128]) for s in range(2)]
        dec = [sbt("dec%d" % s, [128, 8, 128]) for s in range(2)]
        CBm = [sbt("CBm%d" % s, [128, 2, 128]) for s in range(2)]
        WT = [sbt("WT%d" % s, [128, 8, 128], BF16) for s in range(2)]
        expcum = [sbt("expcum%d" % s, [128, 8]) for s in range(2)]
        Edec = sbt("Edec", [128, 4]); dtt = sbt("dtt", [128, 8]); xdtt = sbt("xdtt", [128, 512], BF16)
        ST = sbt("ST", [128, 256]); STb = sbt("STb", [128, 256], BF16); tmpS = sbt("tmpS", [128, 256])
        yt = [sbt("yt%d" % s, [128, 512]) for s in range(2)]
        xd = sbt("xd", [128, 512]); sz = sbt("sz", [128, 512]); junk = sbt("junk", [128, 512])
        ss = sbt("ss", [128, 1]); rstd = sbt("rstd", [128, 1])
        stT = sbt("stT", [128, 4, 64])
        pA = st.enter_context(nc.psum_tensor("sd_pA", [128, 512], F32))
        pM = [st.enter_context(nc.psum_tensor("sd_pM%d" % i, [128, 512], F32)) for i in range(2)]
        pY = st.enter_context(nc.psum_tensor("sd_pY", [128, 512], F32))
        pY0 = st.enter_context(nc.psum_tensor("sd_pY0", [128, 512], F32))
        pC = st.enter_context(nc.psum_tensor("sd_pC", [128, 512], F32))
        pS = st.enter_context(nc.psum_tensor("sd_pS", [128, 512], F32))

        P.dma("sp", cst[:, :], consts_d[:, 0:514], w=["cst"])
        SKIP = os.environ.get("SSD_SKIP", "")
        if "b" not in SKIP:
            P.dma("sp", cw[:, :, :].rearrange("p a b -> p (a b)"), conv_w.rearrange("(o a) b -> o (a b)", o=1).partition_broadcast(128), w=["cw"])
        P.dma("sp", cb[:, :], conv_b[0:1, :].partition_broadcast(128), w=["cb"])
        P.dma("sp", dtb[:, :], dt_bias[0:1, :].partition_broadcast(128), w=["dtb"])
        P.dma("sp", aneg[:, :], a_log[0:1, :].partition_broadcast(128), w=["aneg"])
        P.dma("sp", dsk[:, :], d_skip[0:1, :].partition_broadcast(128), w=["dsk"])
        P.dma("sp", nw[:, :], ssm_norm_w[0:1, :].partition_broadcast(128), w=["nw"])
        P.op("pool", lambda e: e.memset(eps[:, :], 1e-5), w=["eps"])
        P.op("act", lambda e: e.activation(out=aneg[:, :], in_=aneg[:, :], func=AF.Exp), r=["aneg"], w=["aneg"])
        P.op("dve", lambda e: e.tensor_scalar(out=aneg[:, :], in0=aneg[:, :], scalar1=-1.0, scalar2=None, op0=ALU.mult), r=["aneg"], w=["aneg"])
        P.op("pool", lambda e: e.memset(ST[:, :], 0.0), w=["ST"])
        P.op("pool", lambda e: e.memset(STb[:, :], 0.0), w=["STb"])
        if "c" not in SKIP:
            P.dma("act", hist_s[0:48, :].rearrange("(j s) f -> s j f", j=3), state_conv[:, :, :], w=["hist"])
            P.dma("act", hist_s[48:112, :], projd[SEQ:NTOK, XBC], w=["hist2"])

        NCH = int(os.environ.get("SSD_NCH", "32"))
        DO_S = os.environ.get("SSD_SAMPLE", "1") == "1"
        for c in range(NCH):
            s = c % 2
            r0 = c * 128
            for i in range(4):
                key = ("xsh", i, s)
                off = 3 - i
                if c == 0 and off > 0:
                    P.op("pool", lambda e, i=i, s=s: e.memset(xsh[i][s][:, :], 0.0), w=[key])
                    P.dma("sp", xsh[i][s][off:128, :], projd[0:128 - off, XBC], w=[key])
                else:
                    P.dma("sp", xsh[i][s][:, :], projd[r0 - off:r0 - off + 128, XBC], w=[key])
            P.dma("sp", zt[s][:, :], projd[r0:r0 + 128, 0:512], w=[("zt", s)])
            P.dma("sp", dtr[s][:, :], projd[r0:r0 + 128, O_DT:O_DT + 8], w=[("dtr", s)])
            conv_silu(P, 128, xsh, s, cw, cb, acc, acc2, tP, tD, xa)
            softplus_dt(P, 128, dtr, s, dtb, aneg, dt, dA)
            kX = ("xa", s)
            xv = xa[s][:, 0:512].rearrange("p (h d) -> p h d", h=8)
            P.op("dve", lambda e, s=s, xv=xv: e.tensor_tensor(out=xdt[s][:, :].rearrange("p (h d) -> p h d", h=8), in0=xv,
                                                              in1=bcast(dt[s][:, :], [[1, 8], [0, 64]]), op=ALU.mult), r=[kX, ("dt", s)], w=[("xdt", s)])
            P.op("pool", lambda e, s=s: e.tensor_copy(out=Bbf[s][:, :], in_=xa[s][:, 512:640]), r=[kX], w=[("Bbf", s)])
            P.op("pe", lambda e, s=s: e.transpose(out=pA[:, 0:128], in_=xa[s][:, 512:640], identity=ident), r=[kX, "cst"], w=["pA0"])
            P.op("pe", lambda e, s=s: e.transpose(out=pA[:, 128:256], in_=xa[s][:, 640:768], identity=ident), r=[kX, "cst"], w=["pA0"])
            for g in range(2):
                P.op("act", lambda e, s=s, g=g: e.activation(out=BCT[s][g][:, :], in_=pA[:, 0:256], func=AF.Copy, scale=rmask[:, g:g + 1]),
                     r=["pA0", "cst"], w=[("BCT", s)])
            for g in range(2):
                P.op("pe", lambda e, s=s, g=g: e.matmul(pA[:, 256 + g * 128:256 + (g + 1) * 128], lhsT=BCT[s][g][:, 0:128],
                                                         rhs=BCT[s][g][:, 128:256], start=True, stop=True), r=[("BCT", s)], w=["pA1"])
            P.op("dve", lambda e, s=s: e.tensor_tensor(out=CBm[s][:, :, :], in0=pA[:, 256:512].rearrange("p (g l) -> p g l", g=2),
                                                       in1=bcast(U, [[0, 2], [1, 128]]), op=ALU.mult), r=["pA1", "cst"], w=[("CBm", s)])
            P.op("dve", lambda e, s=s: e.tensor_tensor(out=LdA[s][:, :, :], in0=bcast(Lt, [[0, 8], [1, 128]]),
                                                       in1=bcast(dA[s][:, :], [[1, 8], [0, 128]]), op=ALU.mult), r=[("dA", s), "cst"], w=[("LdA", s)])
            for h in range(8):
                P.op("pe", lambda e, s=s, h=h: e.matmul(pM[h // 4][:, (h % 4) * 128:(h % 4 + 1) * 128], lhsT=LdA[s][:, h, :], rhs=U,
                                                         start=True, stop=True), r=[("LdA", s), "cst"], w=[("pM", h // 4)])
            for b in range(2):
                P.op("act", lambda e, s=s, b=b: e.activation(out=dec[s][:, b * 4:(b + 1) * 4, :].rearrange("p h l -> p (h l)"), in_=pM[b][:, :], func=AF.Exp),
                     r=[("pM", b)], w=[("dec", s)])
            P.op("dve", lambda e, s=s: e.tensor_tensor(out=WT[s][:, :, :].rearrange("p (g h) l -> p g h l", g=2),
                                                       in0=dec[s][:, :, :].rearrange("p (g h) l -> p g h l", g=2),
                                                       in1=bcast(CBm[s][:, :, :], [[128, 2], [0, 4], [1, 128]]), op=ALU.mult),
                 r=[("dec", s), ("CBm", s)], w=[("WT", s)])
            P.op("pe", lambda e, s=s: e.matmul(pC[:, 0:8], lhsT=U, rhs=dA[s][:, :], start=True, stop=True), r=[("dA", s), "cst"], w=["pC"])
            P.op("pe", lambda e, s=s: e.matmul(pC[:, 8:16], lhsT=ones, rhs=dA[s][:, :], start=True, stop=True), r=[("dA", s), "cst"], w=["pC"])
            P.op("act", lambda e, s=s: e.activation(out=expcum[s][:, :], in_=pC[:, 0:8], func=AF.Exp), r=["pC"], w=[("expcum", s)])
            P.op("act", lambda e: e.activation(out=Edec[0:64, :], in_=pC[0:64, 8:12], func=AF.Exp), r=["pC"], w=["Edec"])
            if "h" not in SKIP:
                P.op("act", lambda e: e.activation(out=Edec[64:128, :], in_=pC[64:128, 12:16], func=AF.Exp), r=["pC"], w=["Edec"])
            for h in range(8):
                P.op("pe", lambda e, s=s, h=h: e.matmul(pY[:, h * 64:(h + 1) * 64], lhsT=WT[s][:, h, :], rhs=xdt[s][:, h * 64:(h + 1) * 64],
                                                         start=True, stop=True), r=[("WT", s), ("xdt", s)], w=["pY"])
            for g in range(2):
                P.op("pe", lambda e, s=s, g=g: e.matmul(pY0[:, g * 256:(g + 1) * 256], lhsT=BCT[s][g][:, 128:256],
                                                         rhs=STb[:, :], start=True, stop=True), r=[("BCT", s), "STb"], w=["pY0"])
            ky = ("yt", s)
            P.op("dve", lambda e, s=s: e.tensor_tensor(out=yt[s][:, :].rearrange("p (h d) -> p h d", h=8), in0=pY0[:, :].rearrange("p (h d) -> p h d", h=8),
                                                       in1=bcast(expcum[s][:, :], [[1, 8], [0, 64]]), op=ALU.mult), r=["pY0", ("expcum", s)], w=[ky])
            P.op("dve", lambda e, s=s: e.tensor_tensor(out=yt[s][:, :], in0=yt[s][:, :], in1=pY[:, :], op=ALU.add), r=[ky, "pY"], w=[ky])
            gate_norm_store(P, 128, yt[s], ky, xa[s][:, 0:512], kX, zt[s], ("zt", s), dsk, nw, xd, sz, junk, ss, rstd, eps, mixd[r0:r0 + 128, 0:512])
            P.op("dve", lambda e, s=s: e.tensor_tensor(out=dtt[:, :], in0=dt[s][:, :], in1=dec[s][:, :, 127], op=ALU.mult), r=[("dt", s), ("dec", s)], w=["dtt"])
            P.op("dve", lambda e, s=s, xv=xv: e.tensor_tensor(out=xdtt[:, :].rearrange("p (h d) -> p h d", h=8), in0=xv,
                                                              in1=bcast(dtt[:, :], [[1, 8], [0, 64]]), op=ALU.mult), r=[kX, "dtt"], w=["xdtt"])
            P.op("pe", lambda e, s=s: e.matmul(pS[:, :], lhsT=Bbf[s][:, :], rhs=xdtt[:, :], start=True, stop=True), r=[("Bbf", s), "xdtt"], w=["pS"])
            for g in range(1 if "h" in SKIP else 2):
                rows = slice(g * 64, (g + 1) * 64)
                P.op("dve", lambda e, rows=rows: e.tensor_tensor(out=tmpS[rows, :].rearrange("p (h d) -> p h d", h=4), in0=ST[rows, :].rearrange("p (h d) -> p h d", h=4),
                                                                 in1=bcast(Edec[rows, :], [[1, 4], [0, 64]]), op=ALU.mult), r=["ST", "Edec"], w=["tmpS"])
                P.op("dve", lambda e, rows=rows, g=g: e.tensor_tensor(out=ST[rows, :], in0=tmpS[rows, :], in1=pS[rows, g * 256:(g + 1) * 256], op=ALU.add),
                     r=["tmpS", "pS"], w=["ST"])
            P.op("act", lambda e: e.activation(out=STb[:, :], in_=ST[:, :], func=AF.Copy), r=["ST"], w=["STb"])
        for g in (range(2) if "d" not in SKIP else []):
            for c2 in range(2):
                rows = slice(g * 64, (g + 1) * 64)
                idx = g * 2 + c2
                P.op("pe", lambda e, g=g, c2=c2, idx=idx: e.matmul(pA[:, idx * 64:(idx + 1) * 64], lhsT=ST[:, c2 * 128:(c2 + 1) * 128],
                                                                   rhs=cst[:, g * 64:(g + 1) * 64], start=True, stop=True), r=["ST", "cst"], w=["pA0", "pA1"])
        if "e" not in SKIP:
            P.op("dve", lambda e: e.tensor_copy(out=stT[:, :, :].rearrange("p a n -> p (a n)"), in_=pA[:, 0:256]), r=["pA0", "pA1"], w=["stT"])
            P.dma("sp", p_ssm.rearrange("(a q) p n -> (q p) a n", q=2), stT[:, :, :], r=["stT"])

        s = 0
        if not DO_S:
            P.emit_phase()
            return
        for i in range(4):
            P.dma("sp", xsh[i][s][0:64, :], hist_s[16 * i:16 * i + 64, :], r=["hist", "hist2"], w=[("xsh", i, s)])
        P.dma("sp", zt[s][0:64, :], projd[SEQ:NTOK, 0:512], w=[("zt", s)])
        P.dma("sp", dtr[s][0:64, :], projd[SEQ:NTOK, O_DT:O_DT + 8], w=[("dtr", s)])
        conv_silu(P, 64, xsh, s, cw, cb, acc, acc2, tP, tD, xa)
        softplus_dt(P, 64, dtr, s, dtb, aneg, dt, dA)
        kX = ("xa", s)
        P.dma("sp", xs_d[:, :], xa[s][0:64, 0:512], r=[kX], w=["xs_d"])
        P.dma("sp", bc_d[:, :], xa[s][0:64, 512:768], r=[kX], w=["bc_d"])
        P.dma("sp", dts_d[:, 0:8], dt[s][0:64, :], r=[("dt", s)], w=["dts_d"])
        P.dma("sp", dts_d[:, 8:16], dA[s][0:64, :], r=[("dA", s)], w=["dts_d"])
        with ExitStack() as st2:
            def sb2(n, sh, d=F32):
                return st2.enter_context(nc.sbuf_tensor("sd2_" + n, sh, d))
            S = sb2("S", [128, 64, 64]); T1 = sb2("T1", [128, 64, 64])
            X = sb2("X", [128, 4, 64]); Bt = sb2("Bt", [128, 4, 64]); Ct = sb2("Ct", [128, 4, 64])
            dtA = sb2("dtA", [128, 2, 4]); ea = sb2("ea", [128, 4]); Xdt = sb2("Xdt", [128, 4, 64]); Yv = sb2("Yv", [128, 4, 64])
            P.dma("sp", S[:, :, :].rearrange("p a b -> p (a b)"), state_ssm.rearrange("s h p n -> (s h) (p n)"), w=["S"])
            P.dma("sp", X[:, :, :], xs_d.rearrange("(t s) (h d) -> (s h) t d", t=4, h=8), r=["xs_d"], w=["X"])
            for hh in range(4):
                for g in range(2):
                    sB = AP(bc_d.tensor, bc_d.offset + g * 64, [[256, 16], [16 * 256, 4], [1, 64]])
                    sC = AP(bc_d.tensor, bc_d.offset + 128 + g * 64, [[256, 16], [16 * 256, 4], [1, 64]])
                    P.dma("sp", Bt[4 * g + hh:128:8, :, :], sB, r=["bc_d"], w=["Bt"])
                    P.dma("sp", Ct[4 * g + hh:128:8, :, :], sC, r=["bc_d"], w=["Ct"])
            for seq in range(16):
                for j in range(2):
                    P.dma("sp", dtA[seq * 8:(seq + 1) * 8, j, :], AP(dts_d.tensor, dts_d.offset + seq * 16 + j * 8, [[1, 8], [16 * 16, 4]]),
                          r=["dts_d"], w=["dtA"], allow_slow_non_contiguous=True)
            P.op("act", lambda e: e.activation(out=ea[:, :], in_=dtA[:, 1, :], func=AF.Exp), r=["dtA"], w=["ea"])
            P.op("dve", lambda e: e.tensor_tensor(out=Xdt[:, :, :], in0=X[:, :, :], in1=bcast(dtA[:, 0, :], [[1, 4], [0, 64]]), op=ALU.mult), r=["X", "dtA"], w=["Xdt"])
            for t in range(TS):
                P.op("dve", lambda e, t=t: e.tensor_scalar(out=S[:, :, :], in0=S[:, :, :], scalar1=ea[:, t:t + 1], scalar2=None, op0=ALU.mult), r=["S", "ea"], w=["S"])
                P.op("pool", lambda e, t=t: e.tensor_tensor(out=T1[:, :, :], in0=bcast(Xdt[:, t, :], [[1, 64], [0, 64]]), in1=bcast(Bt[:, t, :], [[0, 64], [1, 64]]), op=ALU.mult),
                     r=["Xdt", "Bt"], w=["T1"])
                P.op("dve", lambda e: e.tensor_tensor(out=S[:, :, :], in0=S[:, :, :], in1=T1[:, :, :], op=ALU.add), r=["S", "T1"], w=["S"])
                P.op("pool", lambda e, t=t: e.tensor_tensor(out=T1[:, :, :], in0=S[:, :, :], in1=bcast(Ct[:, t, :], [[0, 64], [1, 64]]), op=ALU.mult), r=["S", "Ct"], w=["T1"])
                P.op("dve", lambda e, t=t: e.tensor_reduce(out=Yv[:, t, :], in_=T1[:, :, :], axis=AX.X, op=ALU.add), r=["T1"], w=["Yv"])
            P.dma("sp", s_ssm.rearrange("s h p n -> (s h) (p n)"), S[:, :, :].rearrange("p a b -> p (a b)"), r=["S"])
            P.dma("sp", ys_d.rearrange("(t s) (h d) -> (s h) t d", t=4, h=8), Yv[:, :, :], r=["Yv"], w=["ys_d"])
            ky = ("yt", 0)
            P.dma("sp", yt[0][0:64, :], ys_d[:, :], r=["ys_d"], w=[ky])
            gate_norm_store(P, 64, yt[0], ky, xa[s][0:64, 0:512], kX, zt[s], ("zt", s), dsk, nw, xd, sz, junk, ss, rstd, eps, mixd[SEQ:NTOK, 0:512])
            P.emit_phase()


def outproj_phase(kk, P, mixd, h1, h2, w_out, ln_g, ln_b, ident_d):
    nc = kk.nc
    with ExitStack() as st:
        def sbt(n, s, d=F32):
            return st.enter_context(nc.sbuf_tensor("op_" + n, s, d))
        Wo = sbt("Wo", [128, 8, D], BF16)
        ident = sbt("ident", [128, 128])
        gt = sbt("gt", [128, D]); bt = sbt("bt", [128, D])
        mt = [sbt("mt%d" % s, [128, D]) for s in range(2)]
        h1t = [sbt("h1t%d" % s, [128, D]) for s in range(2)]
        mT = [sbt("mT%d" % s, [128, 8, 128], BF16) for s in range(2)]
        zt = [sbt("zt%d" % s, [128, D]) for s in range(2)]
        ot = [sbt("ot%d" % s, [128, D]) for s in range(2)]
        sb = dict(stats=sbt("stats", [128, 2, 6]), mv=sbt("mv", [128, 2]), rstd=sbt("rstd", [128, 1]),
                  nb=sbt("nb", [128, 1]), eps=sbt("eps", [128, 1]))
        pT = [st.enter_context(nc.psum_tensor("op_pT%d" % i, [128, 512], F32)) for i in range(2)]
        pO = [[st.enter_context(nc.psum_tensor("op_pO%d%d" % (s, h), [128, 512], F32)) for h in range(2)] for s in range(2)]
        P.op("pool", lambda e: e.memset(sb["eps"][:, :], LN_EPS), w=["eps"])
        P.dma("sp", ident[:, :], ident_d[:, :], w=["ident"])
        P.dma("sp", gt[:, :], ln_g[0:1, :].partition_broadcast(128), w=["op_g"])
        P.dma("sp", bt[:, :], ln_b[0:1, :].partition_broadcast(128), w=["op_b"])
        for k in range(8):
            P.dma("pool", Wo[:, k, :], w_out[k * 128:(k + 1) * 128, :], w=[("Wo", k)])
        for t, (r0, nr) in enumerate(TILES):
            s = t % 2
            P.dma("sp", mt[s][:nr, :], mixd[r0:r0 + nr, :], w=[("mt", s)])
            P.dma("sp", h1t[s][:nr, :], h1[r0:r0 + nr, :], w=[("h1t", s)])
            transpose_tile(P, mt[s], nr, ("mt", s), pT, lambda kh, s=s, nr=nr: mT[s][:, kh * 4:(kh + 1) * 4, :nr], ("mT", s), ident)
            for half in range(2):
                for k in range(8):
                    P.op("pe", lambda e, s=s, nr=nr, half=half, k=k: e.matmul(pO[s][half][:nr, :], lhsT=mT[s][:, k, :nr],
                                                                               rhs=Wo[:, k, half * 512:(half + 1) * 512], start=(k == 0), stop=(k == 7)),
                         r=[("mT", s), ("Wo", k)], w=[("pO", s, half)])
                P.op("dve", lambda e, s=s, nr=nr, half=half: e.scalar_tensor_tensor(
                    out=zt[s][:nr, half * 512:(half + 1) * 512], in0=h1t[s][:nr, half * 512:(half + 1) * 512], scalar=ALPHA,
                    in1=pO[s][half][:nr, :], op0=ALU.mult, op1=ALU.add), r=[("h1t", s), ("pO", s, half)], w=[("zt", s)])
            layer_norm_tile(P, sb, zt[s], nr, gt, bt, ot[s], ("zt", s), ("ot", s), "op_")
            P.dma("sp", h2[r0:r0 + nr, :], ot[s][:nr, :], r=[("ot", s)], w=[("h2", t)])
        P.emit_phase()


NEG = -30000.0
GELU_C = 0.044715
GELU_S = 2.0 * 0.7978845608028654


def nsa_prompt_phase(kk, P, projd, kv_out, mixd, cw, consts_d, consts2_d, seltab_d):
    nc = kk.nc
    NQT = int(os.environ.get("NSA_NQT", "32"))
    with ExitStack() as st:
        def sbt(n, s, d=F32):
            return st.enter_context(nc.sbuf_tensor("na_" + n, s, d))
        cst = sbt("cst", [128, 514])
        ident = cst[:, 0:128]
        rmask = cst[:, 512:514]
        c2 = sbt("c2", [128, 1024])
        cval, causb, wbias, cvalid = c2[:, 0:255], c2[:, 255:383], c2[:, 383:1023], c2[:, 1023:1024]
        identb = sbt("identb", [128, 128], BF16)
        rm8 = sbt("rm8", [128, 2])
        kTs = sbt("kTs", [128, SEQ], BF16); kTw = sbt("kTw", [128, SEQ], BF16)
        Vs = sbt("Vs", [128, 32, 128], BF16); Vw = sbt("Vw", [128, 32, 128], BF16)
        KT2 = sbt("KT2", [128, 4, 2048], BF16)
        W1 = [sbt("W1%d" % x, [128, 16, 128], BF16) for x in range(2)]
        pec = [sbt("pec%d" % x, [128, 16], BF16) for x in range(2)]
        perow = [sbt("perow%d" % x, [16, 128]) for x in range(2)]
        w2p = [[sbt("w2p%d%d" % (x, g), [128, 128], BF16) for g in range(2)] for x in range(2)]
        peterm = [sbt("peterm%d" % x, [128, 1]) for x in range(2)]
        gl = [[sbt("gl%d%d" % (x, g), [128, 256], BF16) for g in range(2)] for x in range(2)]
        gx = sbt("gx", [128, 256]); gu = sbt("gu", [128, 256])
        kcT = sbt("kcT", [128, 256], BF16); vc = sbt("vc", [128, 2, 128], BF16)
        kvt = [sbt("kvt%d" % s, [128, 768]) for s in range(2)]
        pair = [sbt("pair%d" % s, [128, 4, 128]) for s in range(2)]
        qt = [sbt("qt%d" % s, [128, 512]) for s in range(2)]
        gt_ = [sbt("gate%d" % s, [128, 24]) for s in range(2)]
        sg = [sbt("sg%d" % s, [128, 24]) for s in range(2)]
        qTh = [sbt("qTh%d" % s, [128, 8, 128], BF16) for s in range(2)]
        bias_c = [sbt("biasc%d" % s, [128, 255]) for s in range(2)]
        Ssb = [sbt("Ssb%d" % s, [128, SEQ]) for s in range(2)]
        Pbf = [sbt("Pbf%d" % s, [128, SEQ], BF16) for s in range(2)]
        PTsb = [sbt("PTsb%d" % s, [128, 32, 128], BF16) for s in range(2)]
        Pf = [sbt("Pf%d" % s, [128, 255]) for s in range(2)]
        pacc = [sbt("pacc%d" % g, [128, 260]) for g in range(2)]
        imp = sbt("imp", [128, 64]); cand = sbt("cand", [128, 64]); cand2 = sbt("cand2", [128, 64])
        m8 = sbt("m8", [128, 16]); selb = [sbt("selb%d" % g, [128, 64]) for g in range(2)]
        seltab = [sbt("seltab%d" % s, [128, 128]) for s in range(2)]
        mx = [sbt("mx%d" % s, [128, 1]) for s in range(4)]
        rsum = [sbt("rsum%d" % s, [128, 1]) for s in range(4)]
        gs = [sbt("gs%d" % s, [128, 1]) for s in range(4)]
        onsa = [sbt("onsa%d" % s, [128, 512]) for s in range(2)]
        pS = [st.enter_context(nc.psum_tensor("na_pS%d" % i, [128, 512], F32)) for i in range(2)]
        pPT = [st.enter_context(nc.psum_tensor("na_pPT%d" % i, [128, 1024], BF16)) for i in range(2)]
        pO = [st.enter_context(nc.psum_tensor("na_pO%d" % i, [128, 512], F32)) for i in range(3)]
        pT = st.enter_context(nc.psum_tensor("na_pT", [128, 512], F32))

        P.dma("sp", cst[:, :], consts_d[:, 0:514], w=["cst"])
        P.dma("sp", c2[:, :], consts2_d[:, :], w=["c2"])
        P.op("dve", lambda e: e.tensor_copy(out=identb[:, :], in_=ident), r=["cst"], w=["identb"])
        P.op("dve", lambda e: e.tensor_scalar(out=rm8[:, :], in0=rmask, scalar1=0.125, scalar2=None, op0=ALU.mult), r=["cst"], w=["rm8"])
        for x, pre in enumerate(("cmp_k_", "cmp_v_") if "w" not in os.environ.get("NSA_SKIP", "") else ()):
            w1 = cw[pre + "w1"]
            for j in range(16):
                P.dma("pool", W1[x][:, j, :], w1[j * 128:(j + 1) * 128, :], w=[("W1", x)])
            P.dma("sp", perow[x][:, :], cw[pre + "pe"].rearrange("(j p) o -> j (p o)", p=128), w=[("perow", x)])
            P.op("pe", lambda e, x=x: e.transpose(out=pT[:, 0:16], in_=perow[x][:, :], identity=cst[0:16, 0:16]), r=[("perow", x), "cst"], w=["pT"])
            P.op("act", lambda e, x=x: e.activation(out=pec[x][:, :], in_=pT[:, 0:16], func=AF.Copy), r=["pT"], w=[("pec", x)])
            for g in range(2):
                P.op("pool", lambda e, x=x, g=g: e.memset(w2p[x][g][:, :], 0.0), w=[("w2p", x, g)])
                P.dma("pool", w2p[x][g][:, g * 64:(g + 1) * 64], cw[pre + "w2"][:, :], w=[("w2p", x, g)])
        for g in range(2):
            P.op("pool", lambda e, g=g: e.memset(pacc[g][:, :], 0.0), w=[("pacc", g)])

        for t in (range(32) if "k" not in os.environ.get("NSA_SKIP", "") else []):
            s = t % 2
            P.dma("sp", kvt[s][:, :], kv_out[t * 128:(t + 1) * 128, :], w=[("kvt", s)])
            KVL = int(os.environ.get("NSA_KV", "9"))
            if KVL >= 1:
                P.op("pe", lambda e, s=s: e.transpose(out=pT[:, 0:128], in_=kvt[s][:, 256:384], identity=ident), r=[("kvt", s), "cst"], w=["pT"])
                P.op("pe", lambda e, s=s: e.transpose(out=pT[:, 128:256], in_=kvt[s][:, 512:640], identity=ident), r=[("kvt", s), "cst"], w=["pT"])
            if KVL >= 2:
                P.op("act", lambda e, t=t: e.activation(out=kTs[:, t * 128:(t + 1) * 128], in_=pT[:, 0:128], func=AF.Copy), r=["pT"], w=["kTs"])
            if KVL >= 3:
                P.op("dve", lambda e, t=t: e.tensor_copy(out=kTw[:, t * 128:(t + 1) * 128], in_=pT[:, 128:256]), r=["pT"], w=["kTw"])
            if KVL >= 4:
                P.op("pool", lambda e, s=s, t=t: e.tensor_copy(out=Vs[:, t, :], in_=kvt[s][:, 384:512]), r=[("kvt", s)], w=["Vs"])
                P.op("pool", lambda e, s=s, t=t: e.tensor_copy(out=Vw[:, t, :], in_=kvt[s][:, 640:768]), r=[("kvt", s)], w=["Vw"])

        for b in (range(16) if "r" not in os.environ.get("NSA_SKIP", "") else []):
            s = b % 2
            for x in range(2):
                for g in range(2):
                    src = AP(kv_out.tensor, kv_out.offset + b * 256 * 768 + x * 128 + g * 64, [[1536, 128], [768, 2], [1, 64]])
                    P.dma("sp", pair[s][:, x * 2 + g, :].rearrange("p (e d) -> p e d", e=2), src, w=[("pair", s)])
            for xg in range(4):
                P.op("pe", lambda e, s=s, xg=xg: e.transpose(out=pT[:, xg * 128:(xg + 1) * 128], in_=pair[s][:, xg, :], identity=ident),
                     r=[("pair", s), "cst"], w=["pT"])
            P.op("act", lambda e, b=b: e.activation(out=KT2[:, :, b * 128:(b + 1) * 128], in_=pT[:, :].rearrange("p (a m) -> p a m", a=4), func=AF.Copy),
                 r=["pT"], w=["KT2"])
        NSKIP = os.environ.get("NSA_SKIP", "")
        for x in (range(2) if "z" not in NSKIP else []):
            for j in (range(16) if "p" not in NSKIP else []):
                P.op("pe", lambda e, x=x, j=j: e.matmul(pO[0][:, 0:1], lhsT=W1[x][:, j, :], rhs=pec[x][:, j:j + 1], start=(j == 0), stop=(j == 15)),
                     r=[("W1", x), ("pec", x)], w=["pO0"])
            P.op("dve", lambda e, x=x: e.tensor_copy(out=peterm[x][:, :], in_=pO[0][:, 0:1]), r=["pO0"], w=[("peterm", x)])
            for g in range(2):
                b = g % 2
                for j in (range(16) if "c" not in NSKIP else []):
                    P.op("pe", lambda e, x=x, g=g, j=j, b=b: e.matmul(pS[b][:, 0:255], lhsT=W1[x][:, j, :], rhs=KT2[:, x * 2 + g, j:j + 8 * 254 + 1:8],
                                                                       start=(j == 0), stop=(j == 15)), r=[("W1", x), "KT2"], w=[("pS", b)])
                P.op("act", lambda e, x=x, b=b: e.activation(out=gx[:, 0:255], in_=pS[b][:, 0:255], func=AF.Identity, bias=peterm[x][:, 0:1], scale=1.0),
                     r=[("pS", b), ("peterm", x)], w=["gx"])
                P.op("dve", lambda e: e.tensor_tensor(out=gu[:, 0:255], in0=gx[:, 0:255], in1=gx[:, 0:255], op=ALU.mult), r=["gx"], w=["gu"])
                P.op("dve", lambda e: e.tensor_scalar(out=gu[:, 0:255], in0=gu[:, 0:255], scalar1=GELU_C, scalar2=1.0, op0=ALU.mult, op1=ALU.add), r=["gu"], w=["gu"])
                P.op("dve", lambda e: e.tensor_tensor(out=gu[:, 0:255], in0=gu[:, 0:255], in1=gx[:, 0:255], op=ALU.mult), r=["gu", "gx"], w=["gu"])
                P.op("act", lambda e: e.activation(out=gu[:, 0:255], in_=gu[:, 0:255], func=AF.Sigmoid, scale=GELU_S), r=["gu"], w=["gu"])
                P.op("dve", lambda e, x=x, g=g: e.tensor_tensor(out=gl[x][g][:, 0:255], in0=gu[:, 0:255], in1=gx[:, 0:255], op=ALU.mult), r=["gu", "gx"], w=[("gl", x, g)])
        for g in (range(2) if "z" not in NSKIP else []):
            P.op("pe", lambda e, g=g: e.matmul(pS[0][:, 0:255], lhsT=w2p[0][g][:, :], rhs=gl[0][g][:, 0:255], start=(g == 0), stop=(g == 1)),
                 r=[("w2p", 0, g), ("gl", 0, g)], w=[("pS", 0)])
        P.op("act", lambda e: e.activation(out=kcT[:, 0:255], in_=pS[0][:, 0:255], func=AF.Copy), r=[("pS", 0)], w=["kcT"])
        for ct, (c0, cn) in enumerate(((0, 128), (128, 127)) if "v" not in NSKIP else ()):
            for g in range(2):
                P.op("pe", lambda e, g=g, c0=c0, cn=cn, ct=ct: e.matmul(pS[1][:cn, ct * 128:(ct + 1) * 128], lhsT=gl[1][g][:, c0:c0 + cn], rhs=w2p[1][g][:, :],
                                                                         start=(g == 0), stop=(g == 1)), r=[("w2p", 1, g), ("gl", 1, g)], w=[("pS", 1)])
            P.op("act", lambda e, cn=cn, ct=ct: e.activation(out=vc[:cn, ct, :], in_=pS[1][:cn, ct * 128:(ct + 1) * 128], func=AF.Copy), r=[("pS", 1)], w=["vc"])

        cnt = {"a": 0}

        def attend(i, h, br, qs, kT, kkey, V, vkey, tile_lo, ntile, mode, first):
            g = h // 4
            a = cnt["a"]; cnt["a"] += 1
            b = a % 2
            m4 = a % 4
            nk = ntile * 128 if mode != "cmp" else 255
            kS, kP, kPT = ("Ssb", b), ("Pbf", b), ("PTsb", b)
            nch = (nk + 511) // 512
            for ch in range(nch):
                k0 = ch * 512
                w = min(512, nk - k0)
                pb = (a + ch) % 2
                kcol = tile_lo * 128 + k0
                P.op("pe", lambda e, pb=pb, w=w, kcol=kcol: e.matmul(pS[pb][:, 0:w], lhsT=qTh[qs][:, h, :], rhs=kT[:, kcol:kcol + w], start=True, stop=True),
                     r=[("qTh", qs), kkey], w=[("pS", pb)])
                if mode == "cmp":
                    P.op("dve", lambda e, pb=pb, w=w, k0=k0: e.tensor_tensor(out=Ssb[b][:, k0:k0 + w], in0=pS[pb][:, 0:w], in1=bias_c[qs][:, 0:w], op=ALU.add),
                         r=[("pS", pb), ("biasc", qs)], w=[kS])
                elif mode == "win":
                    woff = (5 - ntile) * 128 + k0
                    P.op("dve", lambda e, pb=pb, w=w, k0=k0, woff=woff: e.tensor_tensor(out=Ssb[b][:, k0:k0 + w], in0=pS[pb][:, 0:w], in1=wbias[:, woff:woff + w], op=ALU.add),
                         r=[("pS", pb), "c2"], w=[kS])
                elif mode == "sel" and i >= 8:
                    nb = w // 64
                    blk0 = (tile_lo * 128 + k0) // 64
                    P.op("dve", lambda e, pb=pb, w=w, k0=k0, nb=nb, blk0=blk0: e.tensor_tensor(
                        out=Ssb[b][:, k0:k0 + w].rearrange("p (n d) -> p n d", d=64), in0=pS[pb][:, 0:w].rearrange("p (n d) -> p n d", d=64),
                        in1=bcast(selb[g][:, blk0:blk0 + nb], [[1, nb], [0, 64]]), op=ALU.add), r=[("pS", pb), ("selb", g)], w=[kS])
                else:
                    P.op("act", lambda e, pb=pb, w=w, k0=k0: e.activation(out=Ssb[b][:, k0:k0 + w], in_=pS[pb][:, 0:w], func=AF.Copy), r=[("pS", pb)], w=[kS])
            if mode == "sel":
                d0 = (ntile - 1) * 128
                P.op("pool", lambda e, d0=d0: e.tensor_tensor(out=Ssb[b][:, d0:d0 + 128], in0=Ssb[b][:, d0:d0 + 128], in1=causb, op=ALU.add), r=[kS, "c2"], w=[kS])
            P.op("dve", lambda e: e.tensor_reduce(out=mx[m4][:, :], in_=Ssb[b][:, 0:nk], axis=AX.X, op=ALU.max), r=[kS], w=[("mx", m4)])
            P.op("dve", lambda e: e.tensor_scalar(out=mx[m4][:, :], in0=mx[m4][:, :], scalar1=-1.0, scalar2=None, op0=ALU.mult), r=[("mx", m4)], w=[("mx", m4)])
            P.op("pool", lambda e: e.memset(rsum[m4][:, :], 0.0), w=[("rsum", m4)])
            if mode == "cmp":
                pf = Pf[b]
                P.op("act", lambda e: e.activation(out=pf[:, 0:255], in_=Ssb[b][:, 0:255], func=AF.Exp, bias=mx[m4][:, 0:1], scale=1.0, accum_out=rsum[m4][:, :]),
                     r=[kS, ("mx", m4), ("rsum", m4)], w=[("Pf", b), ("rsum", m4)])
                P.op("pool", lambda e: e.tensor_copy(out=Pbf[b][:, 0:255], in_=pf[:, 0:255]), r=[("Pf", b)], w=[kP])
            else:
                P.op("act", lambda e: e.activation(out=Pbf[b][:, 0:nk], in_=Ssb[b][:, 0:nk], func=AF.Exp, bias=mx[m4][:, 0:1], scale=1.0, accum_out=rsum[m4][:, :]),
                     r=[kS, ("mx", m4), ("rsum", m4)], w=[kP, ("rsum", m4)])
            P.op("dve", lambda e: e.reciprocal(out=rsum[m4][:, :], in_=rsum[m4][:, :]), r=[("rsum", m4)], w=[("rsum", m4)])
            if mode == "cmp" and i >= 8:
                r = h % 4
                if r == 0:
                    P.op("dve", lambda e: e.tensor_scalar(out=pacc[g][:, 1:256], in0=Pf[b][:, 0:255], scalar1=rsum[m4][:, 0:1], scalar2=None, op0=ALU.mult),
                         r=[("Pf", b), ("rsum", m4)], w=[("pacc", g)])
                else:
                    P.op("dve", lambda e: e.scalar_tensor_tensor(out=pacc[g][:, 1:256], in0=Pf[b][:, 0:255], scalar=rsum[m4][:, 0:1], in1=pacc[g][:, 1:256],
                                                                 op0=ALU.mult, op1=ALU.add), r=[("Pf", b), ("rsum", m4), ("pacc", g)], w=[("pacc", g)])
            tiles = [(j * 128, min(128, nk - j * 128)) for j in range((nk + 127) // 128)]
            for j0 in range(0, len(tiles), 8):
                grp = tiles[j0:j0 + 8]
                pb = (a + j0 // 8) % 2
                for jj, (c0, cn) in enumerate(grp):
                    P.op("pe", lambda e, pb=pb, jj=jj, c0=c0, cn=cn: e.transpose(out=pPT[pb][:cn, jj * 128:(jj + 1) * 128], in_=Pbf[b][:, c0:c0 + cn], identity=identb[:, :]),
                         r=[kP, "identb"], w=[("pPT", pb)])
                ng = len(grp)
                eng = "act" if (j0 // 8) % 2 == 0 else "dve"
                if all(cn == 128 for _, cn in grp):
                    parts = [(pPT[pb][:, 0:ng * 128].rearrange("p (j t) -> p j t", t=128), PTsb[b][:, j0:j0 + ng, :])]
                else:
                    parts = [(pPT[pb][:cn, jj * 128:(jj + 1) * 128], PTsb[b][:cn, j0 + jj, :]) for jj, (_, cn) in enumerate(grp)]
                for (srcv, dstv) in parts:
                    if eng == "act":
                        P.op("act", lambda e, srcv=srcv, dstv=dstv: e.activation(out=dstv, in_=srcv, func=AF.Copy), r=[("pPT", pb)], w=[kPT])
                    else:
                        P.op("dve", lambda e, srcv=srcv, dstv=dstv: e.tensor_copy(out=dstv, in_=srcv), r=[("pPT", pb)], w=[kPT])
            po = pO[br]
            for j, (c0, cn) in enumerate(tiles):
                if mode == "cmp":
                    rhs = V[:cn, j, g * 64:(g + 1) * 64]
                else:
                    rhs = V[:cn, tile_lo + j, g * 64:(g + 1) * 64]
                P.op("pe", lambda e, j=j, cn=cn, rhs=rhs: e.matmul(po[:, h * 64:(h + 1) * 64], lhsT=PTsb[b][:cn, j, :], rhs=rhs, start=(j == 0), stop=(j == len(tiles) - 1)),
                     r=[kPT, vkey], w=[("pO", br)])
            P.op("dve", lambda e: e.tensor_tensor(out=gs[m4][:, :], in0=rsum[m4][:, :], in1=sg[qs][:, 3 * h + br:3 * h + br + 1], op=ALU.mult),
                 r=[("rsum", m4), ("sg", qs)], w=[("gs", m4)])
            if mode == "cmp" and i == 0:
                P.op("dve", lambda e: e.tensor_tensor(out=gs[m4][:, :], in0=gs[m4][:, :], in1=cvalid, op=ALU.mult), r=[("gs", m4), "c2"], w=[("gs", m4)])
            ko = ("onsa", qs)
            if first:
                P.op("dve", lambda e: e.tensor_scalar(out=onsa[qs][:, h * 64:(h + 1) * 64], in0=po[:, h * 64:(h + 1) * 64], scalar1=gs[m4][:, 0:1], scalar2=None, op0=ALU.mult),
                     r=[("pO", br), ("gs", m4)], w=[ko])
            else:
                P.op("dve", lambda e: e.scalar_tensor_tensor(out=onsa[qs][:, h * 64:(h + 1) * 64], in0=po[:, h * 64:(h + 1) * 64], scalar=gs[m4][:, 0:1],
                                                             in1=onsa[qs][:, h * 64:(h + 1) * 64], op0=ALU.mult, op1=ALU.add), r=[("pO", br), ("gs", m4), ko], w=[ko])

        for i in range(NQT):
            qs = i % 2
            r0 = i * 128
            P.dma("sp", qt[qs][:, :], projd[r0:r0 + 128, O_Q:O_Q + 512], w=[("qt", qs)])
            P.dma("sp", gt_[qs][:, :], projd[r0:r0 + 128, O_GATE:O_GATE + 24], w=[("gate", qs)])
            P.op("act", lambda e, qs=qs: e.activation(out=sg[qs][:, :], in_=gt_[qs][:, :], func=AF.Sigmoid), r=[("gate", qs)], w=[("sg", qs)])
            if i >= 8:
                P.dma("sp", seltab[qs][:, :], seltab_d[i, :, :], w=[("seltab", qs)])
            wins = [(0, [(0, 0)]), (64, [(1, 0)]), (128, [(2, 0)]), (192, [(3, 0), (4, 1)]), (256, [(5, 1)]), (320, [(6, 1)]), (384, [(7, 1)])]
            for batch in (wins[0:4], wins[4:7]):
                for slot, (c0, heads) in enumerate(batch):
                    P.op("pe", lambda e, qs=qs, c0=c0, slot=slot: e.transpose(out=pT[:, slot * 128:(slot + 1) * 128], in_=qt[qs][:, c0:c0 + 128], identity=ident),
                         r=[("qt", qs), "cst"], w=["pT"])
                for slot, (c0, heads) in enumerate(batch):
                    for (h, g) in heads:
                        P.op("act", lambda e, qs=qs, h=h, g=g, slot=slot: e.activation(out=qTh[qs][:, h, :], in_=pT[:, slot * 128:(slot + 1) * 128], func=AF.Copy, scale=rm8[:, g:g + 1]),
                             r=["pT", "rm8"], w=[("qTh", qs)])
            P.op("dve", lambda e, qs=qs, r0=r0: e.tensor_scalar(out=bias_c[qs][:, :], in0=cval, scalar1=float(r0), scalar2=0.0, op0=ALU.add, op1=ALU.is_ge),
                 r=["c2"], w=[("biasc", qs)])
            P.op("dve", lambda e, qs=qs: e.tensor_scalar(out=bias_c[qs][:, :], in0=bias_c[qs][:, :], scalar1=-1.0, scalar2=-NEG, op0=ALU.add, op1=ALU.mult),
                 r=[("biasc", qs)], w=[("biasc", qs)])
            for h in range(8):
                attend(i, h, 0, qs, kcT, "kcT", vc, "vc", 0, 2, "cmp", True)
            if i >= 8:
                for g in range(2):
                    kpa = ("pacc", g)
                    P.op("dve", lambda e, g=g: e.tensor_tensor(out=imp[:, :], in0=pacc[g][:, 0:253:4], in1=pacc[g][:, 1:254:4], op=ALU.add), r=[kpa], w=["imp"])
                    for o in (2, 3, 4):
                        P.op("dve", lambda e, g=g, o=o: e.tensor_tensor(out=imp[:, :], in0=imp[:, :], in1=pacc[g][:, o:o + 253:4], op=ALU.add), r=[kpa, "imp"], w=["imp"])
                    P.op("dve", lambda e, qs=qs: e.tensor_tensor(out=cand[:, :], in0=imp[:, :], in1=seltab[qs][:, 0:64], op=ALU.mult), r=["imp", ("seltab", qs)], w=["cand"])
                    P.op("dve", lambda e, qs=qs: e.scalar_tensor_tensor(out=cand[:, :], in0=seltab[qs][:, 0:64], scalar=-1.0, in1=cand[:, :], op0=ALU.add, op1=ALU.add),
                         r=["cand", ("seltab", qs)], w=["cand"])
                    P.op("dve", lambda e: e.max(out=m8[:, 0:8], in_=cand[:, :]), r=["cand"], w=["m8"])
                    P.op("dve", lambda e: e.match_replace(out=cand2[:, :], in_to_replace=m8[:, 0:8], in_values=cand[:, :], imm_value=-2.0), r=["cand", "m8"], w=["cand2"])
                    P.op("dve", lambda e: e.max(out=m8[:, 8:16], in_=cand2[:, :]), r=["cand2"], w=["m8"])
                    P.op("dve", lambda e, g=g: e.tensor_scalar(out=selb[g][:, :], in0=cand[:, :], scalar1=m8[:, 12:13], scalar2=None, op0=ALU.is_ge), r=["cand", "m8"], w=[("selb", g)])
                    P.op("dve", lambda e, g=g, qs=qs: e.tensor_tensor(out=selb[g][:, :], in0=selb[g][:, :], in1=seltab[qs][:, 64:128], op=ALU.max), r=[("selb", g), ("seltab", qs)], w=[("selb", g)])
                    P.op("dve", lambda e, g=g: e.tensor_scalar(out=selb[g][:, :], in0=selb[g][:, :], scalar1=-1.0, scalar2=-NEG, op0=ALU.add, op1=ALU.mult), r=[("selb", g)], w=[("selb", g)])
            for h in range(8):
                attend(i, h, 1, qs, kTs, "kTs", Vs, "Vs", 0, i + 1, "sel", False)
            lo = max(0, i - 4)
            for h in range(8):
                attend(i, h, 2, qs, kTw, "kTw", Vw, "Vw", lo, i - lo + 1, "win", False)
            P.dma("sp", mixd[r0:r0 + 128, 512:1024], onsa[qs][:, :], r=[("onsa", qs)])
        P.emit_phase()


def nsa_sample_phase(kk, P, projd, kv_out, mixd, cw, consts_d, consts3_d, caches, ckw, cvw, page_table):
    nc = kk.nc
    NS = int(os.environ.get("NSA_NSEQ", str(NSEQ_S)))
    NPG = 64
    NK = NPG * 128 + 4
    with ExitStack() as st:
        def sbt(n, s, d=F32):
            return st.enter_context(nc.sbuf_tensor("ns_" + n, s, d))
        cst = sbt("cst", [128, 514]); ident = cst[:, 0:128]
        c3 = sbt("c3", [128, 1024])
        Amat, A2, bias4, wbias, candm, forced = c3[0:64, 0:8], c3[0:8, 8:72], c3[0:64, 72:76], c3[0:64, 76:592], c3[0:8, 592:720], c3[0:8, 720:848]
        identb = sbt("identb", [128, 128], BF16)
        idx = sbt("idx", [128, NSEQ_S * NPG], I32); ptb = idx; ptf = sbt("ptf", [128, NSEQ_S * NPG])
        kTs = sbt("kTs", [128, NK + 4], BF16); Vs = sbt("Vs", [128, NPG + 1, 128], BF16)
        kTc = sbt("kTc", [128, NPG * 128], BF16)
        W1g = [[sbt("W1g%d%d" % (x, g), [128, 32, 128], BF16) for g in range(2)] for x in range(2)]
        W1 = [sbt("W1%d" % x, [128, 16, 128], BF16) for x in range(2)]
        pec = [sbt("pec%d" % x, [128, 16], BF16) for x in range(2)]
        perow = [sbt("perow%d" % x, [16, 128]) for x in range(2)]
        w2p = [[sbt("w2p%d%d" % (x, g), [128, 128], BF16) for g in range(2)] for x in range(2)]
        peterm = [sbt("peterm%d" % x, [128, 1]) for x in range(2)]
        gl = [[sbt("gl%d%d" % (x, g), [128, 512], BF16) for g in range(2)] for x in range(2)]
        gx = sbt("gx", [128, 512]); gu = sbt("gu", [128, 512])
        kcT = sbt("kcT", [128, 512], BF16); vc = sbt("vc", [128, 4, 128], BF16)
        pg = [sbt("pg%d" % i, [128, 4, 128]) for i in range(4)]
        newT = sbt("newT", [128, 2, 64], BF16)
        stile = sbt("stile", [64, 768])
        Q64 = sbt("Q64", [64, NSEQ_S, 128]); G64 = sbt("G64", [64, NSEQ_S, 3]); SG64 = sbt("SG64", [64, NSEQ_S, 3])
        qT64 = sbt("qT64", [128, 64], BF16)
        kTw = sbt("kTw", [128, 520], BF16); Vw = sbt("Vw", [128, 5, 128], BF16)
        wtile = sbt("wtile", [128, 4, 256])
        Ssb = sbt("Ssb", [64, NK + 4]); Pbf = sbt("Pbf", [64, NK + 4], BF16); PTsb = sbt("PTsb", [128, NPG + 1, 64], BF16)
        Pf = sbt("Pf", [64, 512]); pacc8 = sbt("pacc8", [8, 520]); imp = sbt("imp", [8, 128]); cand = sbt("cand", [8, 128]); cand2 = sbt("cand2", [8, 128])
        m8 = sbt("m8", [8, 16]); selb8 = sbt("selb8", [8, 132]); selb64 = sbt("selb64", [64, 132])
        mx = sbt("mx", [64, 1]); rsum = sbt("rsum", [64, 1]); gs = sbt("gs", [64, 1]); osum = sbt("osum", [64, 64])
        pS = [st.enter_context(nc.psum_tensor("ns_pS%d" % i, [128, 512], F32)) for i in range(2)]
        pPT = [st.enter_context(nc.psum_tensor("ns_pPT%d" % i, [128, 1024], BF16)) for i in range(2)]
        pO = st.enter_context(nc.psum_tensor("ns_pO", [128, 512], F32))
        pT = [st.enter_context(nc.psum_tensor("ns_pT%d" % i, [128, 512], F32)) for i in range(2)]
        pI = st.enter_context(nc.psum_tensor("ns_pI", [128, 512], F32))

        P.dma("sp", cst[:, :], consts_d[:, 0:514], w=["cst"])
        P.dma("sp", c3[:, :], consts3_d[:, :], w=["c3"])
        P.dma("sp", ptb[:, :], page_table.rearrange("s (o j) -> o (s j)", o=1).partition_broadcast(128), w=["idx"])
        P.op("dve", lambda e: e.tensor_copy(out=ptf[:, :], in_=ptb[:, :]), r=["idx"], w=["ptf"])
        P.op("dve", lambda e: e.tensor_scalar(out=ptf[:, :], in0=ptf[:, :], scalar1=128.0, scalar2=c3[:, 848:849], op0=ALU.mult, op1=ALU.add), r=["ptf", "c3"], w=["ptf"])
        P.op("dve", lambda e: e.tensor_copy(out=idx[:, :], in_=ptf[:, :]), r=["ptf"], w=["idx"])
        P.op("dve", lambda e: e.tensor_copy(out=identb[:, :], in_=ident), r=["cst"], w=["identb"])
        for x, pre in enumerate(("cmp_k_", "cmp_v_")):
            w1 = cw[pre + "w1"]
            for j in range(16):
                P.dma("pool", W1[x][:, j, :], w1[j * 128:(j + 1) * 128, :], w=[("W1", x)])
            for g in range(2):
                P.op("pool", lambda e, x=x, g=g: e.memset(W1g[x][g][:, :, :], 0.0), w=[("W1g", x, g)])
                P.dma("pool", W1g[x][g][g * 64:(g + 1) * 64, :, :], w1.rearrange("(s d) h -> d s h", d=64), w=[("W1g", x, g)])
                P.op("pool", lambda e, x=x, g=g: e.memset(w2p[x][g][:, :], 0.0), w=[("w2p", x, g)])
                P.dma("pool", w2p[x][g][:, g * 64:(g + 1) * 64], cw[pre + "w2"][:, :], w=[("w2p", x, g)])
            P.dma("sp", perow[x][:, :], cw[pre + "pe"].rearrange("(j p) o -> j (p o)", p=128), w=[("perow", x)])
            P.op("pe", lambda e, x=x: e.transpose(out=pT[0][:, 0:16], in_=perow[x][:, :], identity=cst[0:16, 0:16]), r=[("perow", x), "cst"], w=[("pT", 0)])
            P.op("act", lambda e, x=x: e.activation(out=pec[x][:, :], in_=pT[0][:, 0:16], func=AF.Copy), r=[("pT", 0)], w=[("pec", x)])
            for j in range(16):
                P.op("pe", lambda e, x=x, j=j: e.matmul(pO[:, 0:1], lhsT=W1[x][:, j, :], rhs=pec[x][:, j:j + 1], start=(j == 0), stop=(j == 15)),
                     r=[("W1", x), ("pec", x)], w=["pO"])
            P.op("dve", lambda e, x=x: e.tensor_copy(out=peterm[x][:, :], in_=pO[:, 0:1]), r=["pO"], w=[("peterm", x)])
        P.op("pool", lambda e: e.memset(pacc8[:, :], 0.0), w=["pacc8"])
        P.op("pool", lambda e: e.memset(selb8[:, :], 0.0), w=["selb8"])
        P.dma("sp", stile[:, :], kv_out[SEQ:NTOK, :], w=["stile"])
        P.op("pe", lambda e: e.transpose(out=pT[0][:, 0:64], in_=stile[:, 256:384], identity=cst[0:64, 0:64]), r=["stile", "cst"], w=[("pT", 0)])
        P.op("pe", lambda e: e.transpose(out=pT[0][:, 64:128], in_=stile[:, 512:640], identity=cst[0:64, 0:64]), r=["stile", "cst"], w=[("pT", 0)])
        P.op("act", lambda e: e.activation(out=newT[:, :, :].rearrange("p a c -> p (a c)"), in_=pT[0][:, 0:128], func=AF.Copy), r=[("pT", 0)], w=["newT"])
        P.op("pool", lambda e: e.memset(Q64[:, :, :], 0.0), w=["Q64"])
        P.op("pool", lambda e: e.memset(G64[:, :, :], 0.0), w=["G64"])
        for g in range(2):
            for r in range(4):
                h = 4 * g + r
                row0 = g * 32 + r * 4
                srcq = AP(projd.tensor, projd.offset + SEQ * DIN + O_Q + h * 64, [[16 * DIN, 4], [DIN, NSEQ_S], [1, 64]])
                P.dma("sp", Q64[row0:row0 + 4, :, g * 64:(g + 1) * 64], srcq, w=["Q64"])
                srcg = AP(projd.tensor, projd.offset + SEQ * DIN + O_GATE + 3 * h, [[16 * DIN, 4], [DIN, NSEQ_S], [1, 3]])
                P.dma("sp", G64[row0:row0 + 4, :, :], srcg, w=["G64"])
        P.op("act", lambda e: e.activation(out=SG64[:, :, :], in_=G64[:, :, :], func=AF.Sigmoid), r=["G64"], w=["SG64"])

        cnt = {"g": 0}

        def gather(seq, cache, grp):
            slot = cnt["g"] % 4
            cnt["g"] += 1
            rows = cache.rearrange("n t f -> (n t) f")
            for a_ in range(4):
                j = seq * NPG + grp * 4 + a_
                P.op("pool", lambda e, slot=slot, a_=a_, j=j, rows=rows: e.indirect_dma_start(
                    out=pg[slot][:, a_, :], out_offset=None, in_=rows, in_offset=bass.IndirectOffsetOnAxis(ap=idx[:, j:j + 1], axis=0)),
                    r=["idx"], w=[("pg", slot, a_)], dma=True)
            return slot, 0

        def wait_pages(eng, slot, need):
            return None

        def attend_s(seq, br, kT, kkey, nk, Vfn, vkey, bias_fn, first):
            nch = (nk + 511) // 512
            for ch in range(nch):
                k0 = ch * 512
                w = min(512, nk - k0)
                pb = ch % 2
                P.op("pe", lambda e, pb=pb, w=w, k0=k0: e.matmul(pS[pb][0:64, 0:w], lhsT=qT64[:, :], rhs=kT[:, k0:k0 + w], start=True, stop=True),
                     r=["qT64", kkey], w=[("pS", pb)])
                bias_fn(ch, k0, w, pb)
            P.op("dve", lambda e: e.tensor_reduce(out=mx[:, :], in_=Ssb[:, 0:nk], axis=AX.X, op=ALU.max), r=["Ssb"], w=["mx"])
            P.op("dve", lambda e: e.tensor_scalar(out=mx[:, :], in0=mx[:, :], scalar1=-1.0, scalar2=None, op0=ALU.mult), r=["mx"], w=["mx"])
            P.op("dve", lambda e: e.memset(rsum[:, :], 0.0), w=["rsum"])
            if br == 0:
                P.op("act", lambda e: e.activation(out=Pf[:, 0:nk], in_=Ssb[:, 0:nk], func=AF.Exp, bias=mx[:, 0:1], scale=1.0, accum_out=rsum[:, :]),
                     r=["Ssb", "mx", "rsum"], w=["Pf", "rsum"])
                P.op("dve", lambda e: e.tensor_copy(out=Pbf[:, 0:nk], in_=Pf[:, 0:nk]), r=["Pf"], w=["Pbf"])
            else:
                P.op("act", lambda e: e.activation(out=Pbf[:, 0:nk], in_=Ssb[:, 0:nk], func=AF.Exp, bias=mx[:, 0:1], scale=1.0, accum_out=rsum[:, :]),
                     r=["Ssb", "mx", "rsum"], w=["Pbf", "rsum"])
            P.op("dve", lambda e: e.reciprocal(out=rsum[:, :], in_=rsum[:, :]), r=["rsum"], w=["rsum"])
            tiles = [(j * 128, min(128, nk - j * 128)) for j in range((nk + 127) // 128)]
            for j0 in range(0, len(tiles), 16):
                grp = tiles[j0:j0 + 16]
                pb = (j0 // 16) % 2
                for jj, (c0, cn) in enumerate(grp):
                    P.op("pe", lambda e, pb=pb, jj=jj, c0=c0, cn=cn: e.transpose(out=pPT[pb][:cn, jj * 64:(jj + 1) * 64], in_=Pbf[:, c0:c0 + cn], identity=identb[0:64, 0:64]),
                         r=["Pbf", "identb"], w=[("pPT", pb)])
                if all(cn == 128 for _, cn in grp):
                    parts = [(pPT[pb][:, 0:len(grp) * 64].rearrange("p (j t) -> p j t", t=64), PTsb[:, j0:j0 + len(grp), :])]
                else:
                    parts = [(pPT[pb][:cn, jj * 64:(jj + 1) * 64], PTsb[:cn, j0 + jj, :]) for jj, (_, cn) in enumerate(grp)]
                eng = "act" if (j0 // 16) % 2 == 0 else "dve"
                for (srcv, dstv) in parts:
                    if eng == "act":
                        P.op("act", lambda e, srcv=srcv, dstv=dstv: e.activation(out=dstv, in_=srcv, func=AF.Copy), r=[("pPT", pb)], w=["PTsb"])
                    else:
                        P.op("dve", lambda e, srcv=srcv, dstv=dstv: e.tensor_copy(out=dstv, in_=srcv), r=[("pPT", pb)], w=["PTsb"])
            for j, (c0, cn) in enumerate(tiles):
                P.op("pe", lambda e, j=j, cn=cn: e.matmul(pO[0:64, 0:128], lhsT=PTsb[:cn, j, :], rhs=Vfn(j, cn), start=(j == 0), stop=(j == len(tiles) - 1)),
                     r=["PTsb", vkey], w=["pO"])
            P.op("dve", lambda e: e.tensor_tensor(out=gs[:, :], in0=rsum[:, :], in1=SG64[:, seq, br:br + 1], op=ALU.mult), r=["rsum", "SG64"], w=["gs"])
            for g in range(2):
                rows = slice(g * 32, g * 32 + 16)
                if first:
                    P.op("dve", lambda e, rows=rows, g=g: e.tensor_scalar(out=osum[rows, :], in0=pO[rows, g * 64:(g + 1) * 64], scalar1=gs[rows, 0:1], scalar2=None, op0=ALU.mult),
                         r=["pO", "gs"], w=["osum"])
                else:
                    P.op("dve", lambda e, rows=rows, g=g: e.scalar_tensor_tensor(out=osum[rows, :], in0=pO[rows, g * 64:(g + 1) * 64], scalar=gs[rows, 0:1], in1=osum[rows, :],
                                                                                 op0=ALU.mult, op1=ALU.add), r=["pO", "gs", "osum"], w=["osum"])

        for seq in range(NS):
            P.op("pe", lambda e, seq=seq: e.transpose(out=pT[1][:, 0:64], in_=Q64[:, seq, :], identity=cst[0:64, 0:64]), r=["Q64", "cst"], w=[("pT", 1)])
            P.op("act", lambda e: e.activation(out=qT64[:, :], in_=pT[1][:, 0:64], func=AF.Copy, scale=0.125), r=[("pT", 1)], w=["qT64"])
            def gather_cache(cache, kind):
                for grp in range(NPG // 4):
                    slot, need = gather(seq, cache, grp)
                    if kind == "vs":
                        wait_pages("pool", slot, need)
                        P.op("act", lambda e, slot=slot, grp=grp: e.activation(out=Vs[:, grp * 4:(grp + 1) * 4, :], in_=pg[slot][:, :, :], func=AF.Copy),
                             r=[("pg", slot, a_) for a_ in range(4)], w=["Vs"])
                    else:
                        wait_pages("pe", slot, need)
                        tb = grp % 2
                        for a in range(4):
                            P.op("pe", lambda e, slot=slot, a=a, tb=tb: e.transpose(out=pT[tb][:, a * 128:(a + 1) * 128], in_=pg[slot][:, a, :], identity=ident),
                                 r=[("pg", slot, a), "cst"], w=[("pT", tb)])
                        if kind == "ks":
                            dst = kTs[:, grp * 512:(grp + 1) * 512]
                            dkey = "kTs"
                        else:
                            dst = kTc[:, grp * 512:(grp + 1) * 512]
                            dkey = "kTc"
                        if grp % 2 == 0:
                            P.op("act", lambda e, dst=dst, tb=tb: e.activation(out=dst, in_=pT[tb][:, :], func=AF.Copy), r=[("pT", tb)], w=[dkey])
                        else:
                            P.op("dve", lambda e, dst=dst, tb=tb: e.tensor_copy(out=dst, in_=pT[tb][:, :]), r=[("pT", tb)], w=[dkey])
            gather_cache(caches[2], "ks")
            gather_cache(caches[3], "vs")
            P.op("act", lambda e, seq=seq: e.activation(out=kTs[:, NPG * 128:NPG * 128 + 4], in_=newT[:, 0, seq:64:16], func=AF.Copy), r=["newT"], w=["kTs"])
            P.dma("pool", Vs[0:4, NPG, :], kv_out[SEQ + seq:NTOK:16, 384:512], w=["Vs"])
            P.dma("sp", wtile[:, :, 0:128], AP(ckw.tensor, ckw.offset + seq * 512 * 128, [[128, 128], [128 * 128, 4], [1, 128]]), w=["wtileK"])
            for a in range(4):
                P.op("pe", lambda e, a=a: e.transpose(out=pT[0][:, a * 128:(a + 1) * 128], in_=wtile[:, a, 0:128], identity=ident), r=["wtileK", "cst"], w=[("pT", 0)])
            P.op("act", lambda e: e.activation(out=kTw[:, 0:512], in_=pT[0][:, :], func=AF.Copy), r=[("pT", 0)], w=["kTw"])
            P.op("act", lambda e, seq=seq: e.activation(out=kTw[:, 512:516], in_=newT[:, 1, seq:64:16], func=AF.Copy), r=["newT"], w=["kTw"])
            P.dma("pool", Vw[:, 0:4, :], AP(cvw.tensor, cvw.offset + seq * 512 * 128, [[128, 128], [128 * 128, 4], [1, 128]]), w=["Vw"])
            P.dma("pool", Vw[0:4, 4, :], kv_out[SEQ + seq:NTOK:16, 640:768], w=["Vw"])
            for x in range(2):
                gather_cache(caches[x], "kc")
                for g in range(2):
                    b = g % 2
                    for sp_ in range(32):
                        P.op("pe", lambda e, x=x, g=g, sp_=sp_, b=b: e.matmul(pS[b][:, 0:511], lhsT=W1g[x][g][:, sp_, :], rhs=kTc[:, sp_:sp_ + 16 * 510 + 1:16],
                                                                              start=(sp_ == 0), stop=(sp_ == 31)), r=[("W1g", x, g), "kTc"], w=[("pS", b)])
                    P.op("act", lambda e, x=x, b=b: e.activation(out=gx[:, 0:511], in_=pS[b][:, 0:511], func=AF.Identity, bias=peterm[x][:, 0:1], scale=1.0),
                         r=[("pS", b), ("peterm", x)], w=["gx"])
                    P.op("dve", lambda e: e.tensor_tensor(out=gu[:, 0:511], in0=gx[:, 0:511], in1=gx[:, 0:511], op=ALU.mult), r=["gx"], w=["gu"])
                    P.op("dve", lambda e: e.tensor_scalar(out=gu[:, 0:511], in0=gu[:, 0:511], scalar1=GELU_C, scalar2=1.0, op0=ALU.mult, op1=ALU.add), r=["gu"], w=["gu"])
                    P.op("dve", lambda e: e.tensor_tensor(out=gu[:, 0:511], in0=gu[:, 0:511], in1=gx[:, 0:511], op=ALU.mult), r=["gu", "gx"], w=["gu"])
                    P.op("act", lambda e: e.activation(out=gu[:, 0:511], in_=gu[:, 0:511], func=AF.Sigmoid, scale=GELU_S), r=["gu"], w=["gu"])
                    P.op("dve", lambda e, x=x, g=g: e.tensor_tensor(out=gl[x][g][:, 0:511], in0=gu[:, 0:511], in1=gx[:, 0:511], op=ALU.mult), r=["gu", "gx"], w=[("gl", x, g)])
            for g in range(2):
                P.op("pe", lambda e, g=g: e.matmul(pS[0][:, 0:511], lhsT=w2p[0][g][:, :], rhs=gl[0][g][:, 0:511], start=(g == 0), stop=(g == 1)),
                     r=[("w2p", 0, g), ("gl", 0, g)], w=[("pS", 0)])
            P.op("act", lambda e: e.activation(out=kcT[:, 0:511], in_=pS[0][:, 0:511], func=AF.Copy), r=[("pS", 0)], w=["kcT"])
            for ct in range(4):
                c0 = ct * 128
                cn = min(128, 511 - c0)
                for g in range(2):
                    P.op("pe", lambda e, g=g, c0=c0, cn=cn, ct=ct: e.matmul(pS[1][:cn, ct * 128:(ct + 1) * 128], lhsT=gl[1][g][:, c0:c0 + cn], rhs=w2p[1][g][:, :],
                                                                             start=(g == 0), stop=(g == 1)), r=[("w2p", 1, g), ("gl", 1, g)], w=[("pS", 1)])
                P.op("act", lambda e, cn=cn, ct=ct: e.activation(out=vc[:cn, ct, :], in_=pS[1][:cn, ct * 128:(ct + 1) * 128], func=AF.Copy), r=[("pS", 1)], w=["vc"])
            def bias_cmp(ch, k0, w, pb):
                P.op("act", lambda e: e.activation(out=Ssb[:, k0:k0 + w], in_=pS[pb][0:64, 0:w], func=AF.Copy), r=[("pS", pb)], w=["Ssb"])
            attend_s(seq, 0, kcT, "kcT", 511, lambda j, cn: vc[:cn, j, :], "vc", bias_cmp, True)
            P.op("dve", lambda e: e.tensor_scalar(out=Pf[:, 0:511], in0=Pf[:, 0:511], scalar1=rsum[:, 0:1], scalar2=None, op0=ALU.mult), r=["Pf", "rsum"], w=["Pf"])
            P.op("pe", lambda e: e.matmul(pI[0:8, 0:511], lhsT=Amat, rhs=Pf[:, 0:511], start=True, stop=True), r=["Pf", "c3"], w=["pI"])
            P.op("dve", lambda e: e.tensor_copy(out=pacc8[:, 1:512], in_=pI[0:8, 0:511]), r=["pI"], w=["pacc8"])
            P.op("dve", lambda e: e.tensor_tensor(out=imp[:, :], in0=pacc8[:, 0:509:4], in1=pacc8[:, 1:510:4], op=ALU.add), r=["pacc8"], w=["imp"])
            for o in (2, 3, 4):
                P.op("dve", lambda e, o=o: e.tensor_tensor(out=imp[:, :], in0=imp[:, :], in1=pacc8[:, o:o + 509:4], op=ALU.add), r=["pacc8", "imp"], w=["imp"])
            P.op("dve", lambda e: e.tensor_tensor(out=cand[:, :], in0=imp[:, :], in1=candm, op=ALU.mult), r=["imp", "c3"], w=["cand"])
            P.op("dve", lambda e: e.scalar_tensor_tensor(out=cand[:, :], in0=candm, scalar=-1.0, in1=cand[:, :], op0=ALU.add, op1=ALU.add), r=["cand", "c3"], w=["cand"])
            P.op("dve", lambda e: e.max(out=m8[:, 0:8], in_=cand[:, :]), r=["cand"], w=["m8"])
            P.op("dve", lambda e: e.match_replace(out=cand2[:, :], in_to_replace=m8[:, 0:8], in_values=cand[:, :], imm_value=-2.0), r=["cand", "m8"], w=["cand2"])
            P.op("dve", lambda e: e.max(out=m8[:, 8:16], in_=cand2[:, :]), r=["cand2"], w=["m8"])
            P.op("dve", lambda e: e.tensor_scalar(out=selb8[:, 0:128], in0=cand[:, :], scalar1=m8[:, 12:13], scalar2=None, op0=ALU.is_ge), r=["cand", "m8"], w=["selb8"])
            P.op("dve", lambda e: e.tensor_tensor(out=selb8[:, 0:128], in0=selb8[:, 0:128], in1=forced, op=ALU.max), r=["selb8", "c3"], w=["selb8"])
            P.op("dve", lambda e: e.tensor_scalar(out=selb8[:, 0:128], in0=selb8[:, 0:128], scalar1=-1.0, scalar2=-NEG, op0=ALU.add, op1=ALU.mult), r=["selb8"], w=["selb8"])
            P.op("pe", lambda e: e.matmul(pI[0:64, 0:132], lhsT=A2, rhs=selb8[:, 0:132], start=True, stop=True), r=["selb8", "c3"], w=["pI"])
            P.op("dve", lambda e: e.tensor_copy(out=selb64[:, :], in_=pI[0:64, 0:132]), r=["pI"], w=["selb64"])
            def bias_sel(ch, k0, w, pb):
                if w == 512:
                    P.op("dve", lambda e: e.tensor_tensor(out=Ssb[:, k0:k0 + 512].rearrange("p (n d) -> p n d", d=64), in0=pS[pb][0:64, 0:512].rearrange("p (n d) -> p n d", d=64),
                                                          in1=bcast(selb64[:, ch * 8:ch * 8 + 8], [[1, 8], [0, 64]]), op=ALU.add), r=[("pS", pb), "selb64"], w=["Ssb"])
                else:
                    P.op("dve", lambda e: e.tensor_tensor(out=Ssb[:, k0:k0 + w], in0=pS[pb][0:64, 0:w], in1=bias4, op=ALU.add), r=[("pS", pb), "c3"], w=["Ssb"])
            attend_s(seq, 1, kTs, "kTs", NK, lambda j, cn: Vs[:cn, j, :], "Vs", bias_sel, False)
            def bias_win(ch, k0, w, pb):
                P.op("dve", lambda e: e.tensor_tensor(out=Ssb[:, k0:k0 + w], in0=pS[pb][0:64, 0:w], in1=wbias[:, k0:k0 + w], op=ALU.add), r=[("pS", pb), "c3"], w=["Ssb"])
            attend_s(seq, 2, kTw, "kTw", 516, lambda j, cn: Vw[:cn, j, :], "Vw", bias_win, False)
            for g in range(2):
                for r in range(4):
                    h = 4 * g + r
                    row0 = g * 32 + r * 4
                    dst = AP(mixd.tensor, mixd.offset + (SEQ + seq) * D + 512 + h * 64, [[16 * D, 4], [1, 64]])
                    P.dma("sp", dst, osum[row0:row0 + 4, :], r=["osum"])
        P.emit_phase()


def build_program():
    kk = K()
    nc = kk.nc
    x = kk.din("x", [NTOK, D])
    consts_d = kk.din("consts", [128, 514])
    ident_d = consts_d[:, 0:128]
    state_ssm = kk.din("state_ssm", [NSEQ_S, 8, 64, 64])
    state_conv = kk.din("state_conv", [NSEQ_S, 3, 768])
    conv_w = kk.din("conv_w", [4, 768])
    conv_b = kk.din("conv_b", [1, 768])
    dt_bias = kk.din("dt_bias", [1, 8])
    a_log = kk.din("a_log", [1, 8])
    d_skip = kk.din("d_skip", [1, 8])
    ssm_norm_w = kk.din("ssm_norm_w", [1, 512])
    consts2_d = kk.din("consts2", [128, 1024])
    seltab_d = kk.din("seltab", [32, 128, 128])
    consts3_d = kk.din("consts3", [128, 1024])
    page_table = kk.din("page_table", [NSEQ_S, 64], I32)
    caches = [kk.din(n, [N_POOL, 128, 128]) for n in ("cache_k_cmp", "cache_v_cmp", "cache_k_slc", "cache_v_slc")]
    cw = {}
    for pre in ("cmp_k_", "cmp_v_"):
        cw[pre + "w1"] = kk.din(pre + "w1", [2048, 128])
        cw[pre + "w2"] = kk.din(pre + "w2", [128, 64])
        cw[pre + "pe"] = kk.din(pre + "pe", [2048, 1])
    cos_d = kk.din("cos", [NTOK, 32])
    sin_d = kk.din("sin", [NTOK, 32])
    w_in = kk.din("w_in", [D, DIN])
    b_gate = kk.din("b_gate", [1, 24])
    f1g = kk.din("ffn1_w_gate", [D, DFF])
    f1u = kk.din("ffn1_w_up", [D, DFF])
    f1d = kk.din("ffn1_w_down", [DFF, D])
    f2g = kk.din("ffn2_w_gate", [D, DFF])
    f2u = kk.din("ffn2_w_up", [D, DFF])
    f2d = kk.din("ffn2_w_down", [DFF, D])
    w_out = kk.din("w_out", [D, D])
    ln2g = kk.din("ln2_g", [1, D]); ln2b = kk.din("ln2_b", [1, D])
    ln3g = kk.din("ln3_g", [1, D]); ln3b = kk.din("ln3_b", [1, D])
    ln1g = kk.din("ln1_g", [1, D])
    ln1b = kk.din("ln1_b", [1, D])
    ckw = kk.din("cache_k_win", [NSEQ_S, 512, 128])
    cvw = kk.din("cache_v_win", [NSEQ_S, 512, 128])

    kv_out = kk.dout("kv_out", [NTOK, 768])
    p_win = kk.dout("p_win", [2, 512, 128])
    s_win = kk.dout("s_win", [2, NSEQ_S, 512, 128])
    p_conv = kk.dout("p_conv", [3, 768])
    s_conv = kk.dout("s_conv", [NSEQ_S, 3, 768])
    p_ssm = kk.dout("p_ssm", [8, 64, 64])
    s_ssm = kk.dout("s_ssm", [NSEQ_S, 8, 64, 64])
    mixd = kk.dscr("mixd", [NTOK, D])
    y_out = kk.dout("y_out", [NTOK, D])
    h2 = kk.dscr("h2", [NTOK, D])
    h1 = kk.dscr("h1", [NTOK, D])
    projd = kk.dscr("projd", [NTOK, DIN])

    with ExitStack() as st:
        P = Prog(nc, st)
        import os
        PH = os.environ.get("KPHASES", "ffn1,inproj,ssd,nsa,nsas,outproj,ffn2,tail").split(",")
        if "ffn1" in PH:
            ffn_phase(kk, P, "f1_", f1g, f1u, f1d, ln1g, ln1b, x, h1, ident_d)
        if "inproj" in PH:
            inproj_phase(kk, P, h1, projd, kv_out, w_in, b_gate, cos_d, sin_d, ident_d)
        if "ssd" in PH:
            ssd_phase(kk, P, projd, mixd, p_ssm, s_ssm, state_ssm, state_conv, conv_w, conv_b, dt_bias, a_log, d_skip, ssm_norm_w, consts_d)
        if "nsa" in PH:
            nsa_prompt_phase(kk, P, projd, kv_out, mixd, cw, consts_d, consts2_d, seltab_d)
        if "nsas" in PH:
            nsa_sample_phase(kk, P, projd, kv_out, mixd, cw, consts_d, consts3_d, caches, ckw, cvw, page_table)
        if "outproj" in PH:
            outproj_phase(kk, P, mixd, h1, h2, w_out, ln2g, ln2b, ident_d)
        if "ffn2" in PH:
            ffn_phase(kk, P, "f2_", f2g, f2u, f2d, ln3g, ln3b, h2, y_out, ident_d)
        for i in range(2):
            c0 = 512 + 128 * i
            P.dma("sp", p_win[i, :, :], kv_out[SEQ - 512:SEQ, c0:c0 + 128])
            cw = (ckw, cvw)[i]
            P.dma("act", s_win[i, :, 0:508, :], cw[:, 4:512, :])
            P.dma("sp", s_win[i, :, 508:512, :], kv_out[SEQ:NTOK, c0:c0 + 128].rearrange("(t s) f -> s t f", t=TS))
        P.dma("sp", p_conv[:, :], projd[SEQ - 3:SEQ, O_XBC:O_XBC + 768])
        P.dma("sp", s_conv[:, :, :], projd[SEQ:NTOK, O_XBC:O_XBC + 768].rearrange("(t s) f -> s t f", t=TS)[:, 1:4, :])
        P.emit_phase()
        kk.n_inst = P.n_inst
    return kk


_CACHE = {}


def _rope_tables():
    half = 32
    inv = (10000.0 ** (-np.arange(half, dtype=np.float32) / half)).astype(np.float32)
    pos = np.concatenate([np.arange(SEQ), np.repeat(8192 + np.arange(TS), NSEQ_S)]).astype(np.float32)
    ang = (pos[:, None] * inv[None, :]).astype(np.float32)
    return np.cos(ang).astype(np.float32), np.sin(ang).astype(np.float32)


def kernel(**inp):
    if "kk" not in _CACHE:
        _CACHE["kk"] = build_program()
    kk = _CACHE["kk"]
    f32 = np.float32
    cos, sin = _rope_tables()
    ii = np.arange(128)
    consts = np.concatenate([np.eye(128, dtype=f32), (ii[:, None] <= ii[None, :]).astype(f32),
                             (ii[None, :] < ii[:, None]).astype(f32), np.ones((128, 128), f32),
                             (ii[:, None] < 64).astype(f32), (ii[:, None] >= 64).astype(f32)], axis=1)
    tl = ii[:, None].astype(np.int64)
    cvalm = (tl - 16 * np.arange(255)[None, :] - 31).astype(f32)
    causb = np.where(ii[None, :] <= ii[:, None], 0.0, -30000.0).astype(f32)
    jj = np.arange(640)[None, :]
    delta = (jj // 128 - 4) * 128 + (jj % 128) - tl
    wbias = np.where((delta <= 0) & (delta > -512), 0.0, -30000.0).astype(f32)
    cvalid = (tl >= 31).astype(f32)
    consts2 = np.concatenate([cvalm, causb, wbias, cvalid], axis=1).astype(f32)
    qpos = (np.arange(32)[:, None] * 128 + np.arange(128)[None, :])
    cur = (qpos // 64)[:, :, None]
    blk = np.arange(64)[None, None, :]
    candm = ((blk >= 1) & (blk <= cur - 2)).astype(f32)
    forced = (((blk == 0) | (blk == cur) | (blk == cur - 1))).astype(f32)
    seltab = np.concatenate([candm, forced], axis=2).astype(f32)
    c3 = np.zeros((128, 1024), f32)
    rows = np.arange(64)
    rg, rr, rt = rows // 32, (rows % 32) // 4, rows % 4
    used = (rows % 32) < 16
    for rw in rows[used]:
        c3[rw, rg[rw] * 4 + rt[rw]] = 1.0
        c3[rg[rw] * 4 + rt[rw], 8 + rw] = 1.0
    c3[0:64, 72:76] = np.where(np.arange(4)[None, :] <= rt[:, None], 0.0, -30000.0)
    wr = np.arange(516)[None, :]
    wvalid = np.where(wr < 512, wr > rt[:, None], (wr - 512) <= rt[:, None])
    c3[0:64, 76:592] = np.where(wvalid, 0.0, -30000.0)
    c3[0:8, 592:720] = ((np.arange(128) >= 1) & (np.arange(128) <= 126)).astype(f32)[None, :]
    c3[0:8, 720:848] = ((np.arange(128) == 0) | (np.arange(128) == 127)).astype(f32)[None, :]
    c3[:, 848] = np.arange(128, dtype=f32)
    shared = {"consts": consts, "cos": cos, "sin": sin, "consts2": consts2, "seltab": seltab, "consts3": c3}
    for name in ("cache_k_cmp", "cache_v_cmp", "cache_k_slc", "cache_v_slc"):
        if name in kk.ins:
            shared[name] = np.asarray(inp[name])[0].reshape(-1, 128, 128)
    for name in ("w_in", "b_gate", "ffn1_w_gate", "ffn1_w_up", "ffn1_w_down", "ln1_g", "ln1_b",
                 "conv_w", "conv_b", "dt_bias", "a_log", "d_skip", "ssm_norm_w",
                 "ffn2_w_gate", "ffn2_w_up", "ffn2_w_down", "w_out", "ln2_g", "ln2_b", "ln3_g", "ln3_b",
                 "cmp_k_w1", "cmp_k_w2", "cmp_k_pe", "cmp_v_w1", "cmp_v_w2", "cmp_v_pe"):
        if name in kk.ins:
            a = np.asarray(inp[name])
            shared[name] = np.ascontiguousarray(a[0].reshape(kk.ins[name].shape))
    in_maps = []
    for c in range(NCORES):
        m = dict(shared)
        xs = np.asarray(inp["x_sample"])[c * NSEQ_S:(c + 1) * NSEQ_S].transpose(1, 0, 2).reshape(NSEQ_S * TS, D)
        m["x"] = np.ascontiguousarray(np.concatenate([np.asarray(inp["x_prompt"])[c], xs], axis=0))
        m["page_table"] = np.ascontiguousarray(np.asarray(inp["page_table"])[c * NSEQ_S:(c + 1) * NSEQ_S]).astype(np.int32)
        m["state_ssm"] = np.ascontiguousarray(np.asarray(inp["state_ssm"])[0, c * NSEQ_S:(c + 1) * NSEQ_S])
        m["state_conv"] = np.ascontiguousarray(np.asarray(inp["state_conv"])[0, c * NSEQ_S:(c + 1) * NSEQ_S])
        m["cache_k_win"] = np.ascontiguousarray(np.asarray(inp["cache_k_win"])[0, c * NSEQ_S:(c + 1) * NSEQ_S].reshape(NSEQ_S, 512, 128))
        m["cache_v_win"] = np.ascontiguousarray(np.asarray(inp["cache_v_win"])[0, c * NSEQ_S:(c + 1) * NSEQ_S].reshape(NSEQ_S, 512, 128))
        in_maps.append({k: v for k, v in m.items() if k in kk.ins})
    res = run_bass_kernel_spmd(kk.nc, in_maps, core_ids=list(range(NCORES)))
    R = res.results
    _CACHE["last"] = R

    def cat(name, sl=None):
        return np.stack([np.asarray(r[name]) if sl is None else np.asarray(r[name])[sl] for r in R], axis=0)

    kv = cat("kv_out")
    kvp = kv[:, :SEQ].reshape(NCORES, SEQ, 6, 2, 64)
    kvs = kv[:, SEQ:].reshape(NCORES, TS, NSEQ_S, 6, 2, 64).transpose(0, 2, 1, 3, 4, 5).reshape(NCORES * NSEQ_S, TS, 6, 2, 64)
    outs = []
    yo = cat("y_out")
    y_p = np.ascontiguousarray(yo[:, :SEQ])
    y_s = np.ascontiguousarray(yo[:, SEQ:].reshape(NCORES, TS, NSEQ_S, D).transpose(0, 2, 1, 3).reshape(NCORES * NSEQ_S, TS, D))
    outs += [y_p, y_s]
    for i in range(4):
        outs.append(np.ascontiguousarray(kvp[:, :, i])[None])
        outs.append(np.ascontiguousarray(kvs[:, :, i])[None])
    pw = cat("p_win")
    sw = cat("s_win")
    for i in range(2):
        outs.append(np.ascontiguousarray(pw[:, i]).reshape(1, 8, 512, 2, 64))
        outs.append(np.ascontiguousarray(sw[:, i]).reshape(1, 128, 512, 2, 64))
    outs.append(cat("p_ssm")[None])
    outs.append(cat("s_ssm").reshape(1, 128, 8, 64, 64))
    outs.append(cat("p_conv")[None])
    outs.append(cat("s_conv").reshape(1, 128, 3, 768))
    return tuple(outs)
```

```python
import os
import numpy as np
from contextlib import ExitStack
import concourse.bass as bass
import concourse.mybir as mybir
from concourse.bass_utils import run_bass_kernel_spmd

F32 = mybir.dt.float32
BF16 = mybir.dt.bfloat16
I32 = mybir.dt.int32
AF = mybir.ActivationFunctionType
ALU = mybir.AluOpType
AX = mybir.AxisListType
AP = bass.AP

NCORES = 8
D = 1024
DFF = 2816
NJ = DFF // 128
SEQ = 4096
NSEQ_S = 16
TS = 4
NTOK = SEQ + NSEQ_S * TS
DIN = 2592
N_POOL = int(os.environ.get('KN_POOL', '10240'))
ALPHA = 2.0 ** 0.25
LN_EPS = 1e-5
O_Z, O_XBC, O_DT, O_Q, O_KV, O_GATE = 0, 512, 1280, 1288, 1800, 2568

N_DMA_SEMS = 40
SYNC_SAME_ENGINE = True


PSUM_NAMES = {"pT", "pGU", "pD", "pP", "pA", "pM", "pC", "pY", "pY0", "pS", "pO", "pPT"}
PSUM_ALIAS = {"pA0": "pA", "pA1": "pA", "pO0": ("pO", 0)}


def _norm_key(k):
    if isinstance(k, str) and k in PSUM_ALIAS:
        return PSUM_ALIAS[k]
    if isinstance(k, tuple) and k[0] == "pTq":
        return "pT"
    return k


def _is_psum_key(k):
    n = k[0] if isinstance(k, tuple) else k
    return isinstance(n, str) and n in PSUM_NAMES


class _Op:
    __slots__ = ("eng", "fn", "dma", "deps", "signals", "sem", "val", "prewait")

    def __init__(self, eng, fn, dma):
        self.eng = eng
        self.fn = fn
        self.dma = dma
        self.deps = []
        self.signals = False
        self.sem = None
        self.val = None
        self.prewait = None


class Prog:
    ENGS = ("pe", "act", "dve", "pool", "sp")

    def __init__(self, nc, stack):
        self.nc = nc
        self.esem = {e: stack.enter_context(nc.semaphore("es_" + e)) for e in ("pe", "act", "dve", "pool")}
        self.ecount = {e: 0 for e in self.esem}
        self.dsems = [stack.enter_context(nc.semaphore("ds%d" % i)) for i in range(N_DMA_SEMS)]
        self.dcount = [0] * N_DMA_SEMS
        self.dlast = [None] * N_DMA_SEMS
        self.ndma = 0
        self.waited = {e: {} for e in self.ENGS}
        self.n_inst = 0
        self._reset_phase()

    def _reset_phase(self):
        self.ops = []
        self.last_w = {}
        self.readers = {}

    def op(self, eng, fn, r=(), w=(), dma=False):
        r = [_norm_key(k) for k in r]
        w = [_norm_key(k) for k in w]
        w = w + [k for k in r if _is_psum_key(k) and k not in w]
        o = _Op(eng, fn, dma)
        deps = []
        for k in r:
            lw = self.last_w.get(k)
            if lw is not None:
                deps.append(lw)
        for k in w:
            lw = self.last_w.get(k)
            if lw is not None:
                deps.append(lw)
            deps.extend(self.readers.get(k, ()))
        seen = set()
        for d in deps:
            if id(d) in seen:
                continue
            seen.add(id(d))
            if (not d.dma) and d.eng == eng and (not dma) and (eng == "pe" or not SYNC_SAME_ENGINE):
                continue
            o.deps.append(d)
            d.signals = True
        if dma:
            o.signals = True
            s = self.ndma % N_DMA_SEMS
            self.ndma += 1
            o.prewait = self.dlast[s]
            self.dcount[s] += 1
            o.sem = self.dsems[s]
            o.val = 16 * self.dcount[s]
            self.dlast[s] = o
        for k in r:
            self.readers.setdefault(k, []).append(o)
        for k in w:
            self.last_w[k] = o
            self.readers[k] = []
        self.ops.append(o)
        return o

    def dma(self, eng, out, in_, r=(), w=(), **kw):
        return self.op(eng, lambda e: e.dma_start(out=out, in_=in_, **kw), r=r, w=w, dma=True)

    def emit_phase(self):
        nc = self.nc
        by_eng = {e: [o for o in self.ops if o.eng == e] for e in self.ENGS}
        for e in self.esem:
            lst = [o for o in by_eng[e] if not o.dma]
            if lst:
                lst[-1].signals = True
        for o in self.ops:
            if not o.dma and o.signals:
                self.ecount[o.eng] += 1
                o.sem = self.esem[o.eng]
                o.val = self.ecount[o.eng]
        targets = [(self.esem[e], self.ecount[e]) for e in self.esem if self.ecount[e] > 0]
        targets += [(self.dsems[i], 16 * self.dcount[i]) for i in range(N_DMA_SEMS) if self.dcount[i] > 0]
        prog = self

        def emit_engine(ename, eng):
            waited = prog.waited[ename]

            def wait(sem, val):
                key = sem.num
                if waited.get(key, 0) >= val:
                    return
                waited[key] = val
                eng.wait_ge(sem, val)
                prog.n_inst += 1

            for o in by_eng[ename]:
                if o.prewait is not None:
                    wait(o.prewait.sem, o.prewait.val)
                for d in o.deps:
                    wait(d.sem, d.val)
                inst = o.fn(eng)
                prog.n_inst += 1
                if o.signals:
                    inst.then_inc(o.sem, 16 if o.dma else 1)
            for (s, v) in targets:
                wait(s, v)

        with nc.Block() as block:
            @block.sync
            def _(e):
                emit_engine("sp", e)

            @block.tensor
            def _(e):
                emit_engine("pe", e)

            @block.scalar
            def _(e):
                emit_engine("act", e)

            @block.vector
            def _(e):
                emit_engine("dve", e)

            @block.gpsimd
            def _(e):
                emit_engine("pool", e)
        self._reset_phase()


def bcast(ap, dims):
    return AP(ap.tensor, ap.offset, [list(ap.ap[0])] + [list(d) for d in dims])


TILES = [(i * 128, 128) for i in range(32)] + [(SEQ, 64)]
GROUPS = [[TILES[2 * g], TILES[2 * g + 1]] for g in range(16)] + [[TILES[32]]]


class K:
    def __init__(self):
        self.nc = bass.Bass("TRN2", target_bir_lowering=False)
        self.ins = {}
        self.outs = {}

    def din(self, name, shape, dt=F32):
        t = self.nc.dram_tensor(name, list(shape), dt, kind="ExternalInput").ap()
        self.ins[name] = t
        return t

    def dout(self, name, shape, dt=F32):
        t = self.nc.dram_tensor(name, list(shape), dt, kind="ExternalOutput").ap()
        self.outs[name] = t
        return t

    def dscr(self, name, shape, dt=F32):
        return self.nc.dram_tensor(name, list(shape), dt, kind="Internal").ap()


def layer_norm_tile(P, sb, z, nr, gt, bt, out, key_z, key_out, tag):
    stats, mv, rstd, nb = sb["stats"], sb["mv"], sb["rstd"], sb["nb"]
    for c in range(2):
        P.op("dve", lambda e, c=c: e.bn_stats(out=stats[:nr, c, :], in_=z[:nr, c * 512:(c + 1) * 512]),
             r=[key_z], w=["stats"])
    P.op("dve", lambda e: e.bn_aggr(out=mv[:nr, :], in_=stats[:nr, :, :]), r=["stats"], w=["mv"])
    P.op("act", lambda e: e.activation(out=rstd[:nr, :], in_=mv[:nr, 1:2], func=AF.Sqrt, bias=sb["eps"][:nr, :], scale=1.0),
         r=["mv"], w=["rstd"])
    P.op("dve", lambda e: e.reciprocal(out=rstd[:nr, :], in_=rstd[:nr, :]), r=["rstd"], w=["rstd"])
    P.op("dve", lambda e: e.scalar_tensor_tensor(out=nb[:nr, :], in0=mv[:nr, 0:1], scalar=-1.0, in1=rstd[:nr, :],
                                                 op0=ALU.mult, op1=ALU.mult), r=["mv", "rstd"], w=["nb"])
    P.op("act", lambda e: e.activation(out=out[:nr, :], in_=z[:nr, :], func=AF.Identity, bias=nb[:nr, 0:1], scale=rstd[:nr, 0:1]),
         r=[key_z, "nb", "rstd"], w=[key_out])
    P.op("pool", lambda e: e.tensor_tensor(out=out[:nr, :], in0=out[:nr, :], in1=gt[:nr, :], op=ALU.mult),
         r=[key_out, tag + "g"], w=[key_out])
    P.op("pool", lambda e: e.tensor_tensor(out=out[:nr, :], in0=out[:nr, :], in1=bt[:nr, :], op=ALU.add),
         r=[key_out, tag + "b"], w=[key_out])


def transpose_tile(P, src, nr, key_src, pT, dst_fn, key_dst, ident, evac_engs=("act", "dve")):
    for kh in range(2):
        for kk in range(4):
            k = kh * 4 + kk
            P.op("pe", lambda e, kk=kk, k=k, kh=kh: e.transpose(out=pT[kh][:, kk * 128:kk * 128 + nr],
                                                                 in_=src[:nr, k * 128:(k + 1) * 128], identity=ident[:nr, :nr]),
                 r=[key_src, "ident"], w=[("pT", kh)])
        eng = evac_engs[kh % len(evac_engs)]
        src_ap = pT[kh][:, :].rearrange("p (k t) -> p k t", k=4)[:, :, :nr]
        if eng == "act":
            P.op("act", lambda e, kh=kh, src_ap=src_ap: e.activation(out=dst_fn(kh), in_=src_ap, func=AF.Copy),
                 r=[("pT", kh)], w=[key_dst])
        else:
            P.op(eng, lambda e, kh=kh, src_ap=src_ap: e.tensor_copy(out=dst_fn(kh), in_=src_ap),
                 r=[("pT", kh)], w=[key_dst])


def ffn_phase(kk, P, tag, w_gate, w_up, w_down, ln_g, ln_b, src, dst, ident_d, prologue=None):
    nc = kk.nc
    with ExitStack() as st:
        def sbt(n, s, d=F32):
            return st.enter_context(nc.sbuf_tensor(tag + n, s, d))
        Wg = sbt("Wg", [128, 8, DFF], BF16)
        Wu = sbt("Wu", [128, 8, DFF], BF16)
        Wd = sbt("Wd", [128, NJ, D], BF16)
        ident = sbt("ident", [128, 128])
        gt = sbt("gt", [128, D])
        bt = sbt("bt", [128, D])
        xs = [[sbt("xs%d%d" % (s, t), [128, D]) for t in range(2)] for s in range(2)]
        xT = [sbt("xT%d" % s, [128, 8, 256], BF16) for s in range(2)]
        sg = [sbt("sg%d" % s, [128, 256]) for s in range(2)]
        hT = [sbt("hT%d" % s, [128, 256], BF16) for s in range(3)]
        zt = [sbt("zt%d" % s, [128, D]) for s in range(2)]
        ot = [sbt("ot%d" % s, [128, D]) for s in range(2)]
        sb = dict(stats=sbt("stats", [128, 2, 6]), mv=sbt("mv", [128, 2]), rstd=sbt("rstd", [128, 1]),
                  nb=sbt("nb", [128, 1]), eps=sbt("eps", [128, 1]))
        pT = [st.enter_context(nc.psum_tensor(tag + "pT%d" % i, [128, 512], F32)) for i in range(2)]
        pGU = [st.enter_context(nc.psum_tensor(tag + "pGU%d" % i, [128, 512], F32)) for i in range(2)]
        pD = [[st.enter_context(nc.psum_tensor(tag + "pD%d%d" % (t, h), [128, 512], F32)) for h in range(2)] for t in range(2)]
        extra = prologue.alloc(st) if prologue is not None else None

        P.op("pool", lambda e: e.memset(sb["eps"][:, :], LN_EPS), w=["eps"])
        P.dma("sp", ident[:, :], ident_d[:, :], w=["ident"])
        P.dma("sp", gt[:, :], ln_g[0:1, :].partition_broadcast(128), w=[tag + "g"])
        P.dma("sp", bt[:, :], ln_b[0:1, :].partition_broadcast(128), w=[tag + "b"])
        if prologue is not None:
            prologue.load_consts(P, extra)
        for k in range(8):
            P.dma("pool", Wg[:, k, :], w_gate[k * 128:(k + 1) * 128, :], w=[("Wg", k)])
            P.dma("pool", Wu[:, k, :], w_up[k * 128:(k + 1) * 128, :], w=[("Wu", k)])
        for j in range(NJ):
            P.dma("pool", Wd[:, j, :], w_down[j * 128:(j + 1) * 128, :], w=[("Wd", j)])

        def load_group(G):
            slot = G % 2
            for ti, (r0, nr) in enumerate(GROUPS[G]):
                key = ("xs", slot, ti)
                if prologue is None:
                    P.dma("sp", xs[slot][ti][:nr, :], src[r0:r0 + nr, :], w=[key])
                else:
                    prologue.emit(P, extra, slot, ti, r0, nr, xs[slot][ti], key, pT, ident)
                transpose_tile(P, xs[slot][ti], nr, key, pT,
                               lambda kh, slot=slot, ti=ti, nr=nr: xT[slot][:, kh * 4:(kh + 1) * 4, ti * 128:ti * 128 + nr],
                               ("xT", slot), ident)
                P.op("pool", lambda e, slot=slot, ti=ti, nr=nr: e.tensor_scalar(out=xs[slot][ti][:nr, :], in0=xs[slot][ti][:nr, :],
                                                                                 scalar1=ALPHA, scalar2=None, op0=ALU.mult),
                     r=[key], w=[key])

        def gu(G, j):
            slot = G % 2
            ntok = sum(nr for _, nr in GROUPS[G])
            b = j % 2
            for (W, off, wk) in ((Wg, 0, "Wg"), (Wu, 256, "Wu")):
                for k in range(8):
                    P.op("pe", lambda e, W=W, off=off, k=k: e.matmul(pGU[b][:, off:off + ntok], lhsT=W[:, k, j * 128:(j + 1) * 128],
                                                                      rhs=xT[slot][:, k, :ntok], start=(k == 0), stop=(k == 7)),
                         r=[("xT", slot), (wk, k)], w=[("pGU", b)])
            P.op("act", lambda e: e.activation(out=sg[b][:, :ntok], in_=pGU[b][:, 0:ntok], func=AF.Silu),
                 r=[("pGU", b)], w=[("sg", b)])
            h3 = j % 3
            P.op("dve", lambda e: e.tensor_tensor(out=hT[h3][:, :ntok], in0=sg[b][:, :ntok], in1=pGU[b][:, 256:256 + ntok], op=ALU.mult),
                 r=[("sg", b), ("pGU", b)], w=[("hT", h3)])

        def down(G, j):
            h3 = j % 3
            for ti, (r0, nr) in enumerate(GROUPS[G]):
                for half in range(2):
                    P.op("pe", lambda e, ti=ti, nr=nr, half=half: e.matmul(pD[ti][half][:nr, :], lhsT=hT[h3][:, ti * 128:ti * 128 + nr],
                                                                            rhs=Wd[:, j, half * 512:(half + 1) * 512],
                                                                            start=(j == 0), stop=(j == NJ - 1)),
                         r=[("hT", h3), ("Wd", j)], w=[("pD", ti, half)])

        def epilogue(G):
            slot = G % 2
            for ti, (r0, nr) in enumerate(GROUPS[G]):
                key = ("xs", slot, ti)
                zs = (G * 2 + ti) % 2
                for half in range(2):
                    P.op("dve", lambda e, ti=ti, nr=nr, half=half, zs=zs: e.scalar_tensor_tensor(
                        out=zt[zs][:nr, half * 512:(half + 1) * 512], in0=pD[ti][half][:nr, :], scalar=0.5,
                        in1=xs[slot][ti][:nr, half * 512:(half + 1) * 512], op0=ALU.mult, op1=ALU.add),
                        r=[("pD", ti, half), key], w=[("zt", zs)])
                layer_norm_tile(P, sb, zt[zs], nr, gt, bt, ot[zs], ("zt", zs), ("ot", zs), tag)
                P.dma("sp", dst[r0:r0 + nr, :], ot[zs][:nr, :], r=[("ot", zs)], w=[("dst", r0)])

        nG = len(GROUPS)
        load_group(0)
        for G in range(nG):
            for j in range(NJ):
                gu(G, j)
                if j > 0:
                    down(G, j - 1)
                if j == 8 and G + 1 < nG:
                    load_group(G + 1)
            down(G, NJ - 1)
            epilogue(G)
        P.emit_phase()


def inproj_phase(kk, P, h1, projd, kv_out, w_in, b_gate, cos_d, sin_d, ident_d):
    nc = kk.nc
    with ExitStack() as st:
        def sbt(n, s, d=F32):
            return st.enter_context(nc.sbuf_tensor("ip_" + n, s, d))
        Win = sbt("Win", [128, 8, DIN], BF16)
        ident = sbt("ident", [128, 128])
        cosT = sbt("cos", [128, 33, 32])
        sinT = sbt("sin", [128, 33, 32])
        bg = sbt("bg", [128, 24])
        ht = [sbt("ht%d" % s, [128, D]) for s in range(2)]
        hT = [sbt("hT%d" % s, [128, 8, 128], BF16) for s in range(2)]
        proj = [sbt("proj%d" % s, [128, DIN]) for s in range(2)]
        tmps = {"dve": [sbt("tmpa%d" % i, [128, 8, 32]) for i in range(4)],
                "pool": [sbt("tmpb%d" % i, [128, 8, 32]) for i in range(4)]}
        pT = [st.enter_context(nc.psum_tensor("ip_pT%d" % i, [128, 512], F32)) for i in range(2)]
        pP = [st.enter_context(nc.psum_tensor("ip_pP%d" % i, [128, 512], F32)) for i in range(6)]
        P.dma("sp", ident[:, :], ident_d[:, :], w=["ident"])
        P.dma("sp", cosT[:, 0:32, :], cos_d[0:SEQ, :].rearrange("(t p) d -> p t d", p=128), w=["cos"])
        P.dma("sp", sinT[:, 0:32, :], sin_d[0:SEQ, :].rearrange("(t p) d -> p t d", p=128), w=["sin"])
        P.dma("sp", cosT[0:64, 32, :], cos_d[SEQ:NTOK, :], w=["cos"])
        P.dma("sp", sinT[0:64, 32, :], sin_d[SEQ:NTOK, :], w=["sin"])
        P.dma("sp", bg[:, :], b_gate[0:1, :].partition_broadcast(128), w=["bg"])
        for k in range(8):
            P.dma("pool", Win[:, k, :], w_in[k * 128:(k + 1) * 128, :], w=[("Win", k)])
        chunks = [(0, 512), (512, 512), (1024, 264), (1288, 512), (1800, 512), (2312, 280)]
        for t, (r0, nr) in enumerate(TILES):
            s = t % 2
            P.dma("sp", ht[s][:nr, :], h1[r0:r0 + nr, :], w=[("ht", s)])
            transpose_tile(P, ht[s], nr, ("ht", s), pT, lambda kh, s=s, nr=nr: hT[s][:, kh * 4:(kh + 1) * 4, :nr], ("hT", s), ident)
            for ci, (c0, wd) in enumerate(chunks):
                for k in range(8):
                    P.op("pe", lambda e, ci=ci, c0=c0, wd=wd, k=k, s=s, nr=nr: e.matmul(
                        pP[ci][:nr, :wd], lhsT=hT[s][:, k, :nr], rhs=Win[:, k, c0:c0 + wd], start=(k == 0), stop=(k == 7)),
                        r=[("hT", s), ("Win", k)], w=[("pP", ci)])
                if ci % 2 == 0:
                    P.op("act", lambda e, ci=ci, c0=c0, wd=wd, s=s, nr=nr: e.activation(out=proj[s][:nr, c0:c0 + wd], in_=pP[ci][:nr, :wd], func=AF.Copy),
                         r=[("pP", ci)], w=[("proj", s, ci)])
                else:
                    P.op("dve", lambda e, ci=ci, c0=c0, wd=wd, s=s, nr=nr: e.tensor_copy(out=proj[s][:nr, c0:c0 + wd], in_=pP[ci][:nr, :wd]),
                         r=[("pP", ci)], w=[("proj", s, ci)])
            groups = [(O_Q, 8, 3), (O_KV, 2, 4), (O_KV + 256, 2, 4), (O_KV + 512, 2, 5)]
            for gi, (c0, H, ci) in enumerate(groups):
                eng = "dve" if gi % 2 == 0 else "pool"
                X = proj[s][:nr, c0:c0 + H * 64].rearrange("p (h d) -> p h d", h=H)
                x1 = X[:, :, 0:32]
                x2 = X[:, :, 32:64]
                cb = bcast(cosT[:nr, t, :], [[0, H], [1, 32]])
                sn = bcast(sinT[:nr, t, :], [[0, H], [1, 32]])
                key = ("proj", s, ci)
                t1, t2, t3, t4 = [tmps[eng][i][:nr, 0:H, :] for i in range(4)]
                tk = ("tmp", eng)
                P.op(eng, lambda e, t1=t1, x1=x1, cb=cb: e.tensor_tensor(out=t1, in0=x1, in1=cb, op=ALU.mult), r=[key, "cos"], w=[tk])
                P.op(eng, lambda e, t2=t2, x2=x2, sn=sn: e.tensor_tensor(out=t2, in0=x2, in1=sn, op=ALU.mult), r=[key, "sin"], w=[tk])
                P.op(eng, lambda e, t3=t3, x2=x2, cb=cb: e.tensor_tensor(out=t3, in0=x2, in1=cb, op=ALU.mult), r=[key, "cos"], w=[tk])
                P.op(eng, lambda e, t4=t4, x1=x1, sn=sn: e.tensor_tensor(out=t4, in0=x1, in1=sn, op=ALU.mult), r=[key, "sin"], w=[tk])
                P.op(eng, lambda e, t1=t1, t2=t2, x1=x1: e.tensor_tensor(out=x1, in0=t1, in1=t2, op=ALU.subtract), r=[tk], w=[key])
                P.op(eng, lambda e, t3=t3, t4=t4, x2=x2: e.tensor_tensor(out=x2, in0=t3, in1=t4, op=ALU.add), r=[tk], w=[key])
            P.op("dve", lambda e, s=s, nr=nr: e.tensor_tensor(out=proj[s][:nr, O_GATE:O_GATE + 24], in0=proj[s][:nr, O_GATE:O_GATE + 24],
                                                              in1=bg[:nr, :], op=ALU.add), r=[("proj", s, 5), "bg"], w=[("proj", s, 5)])
            allk = [("proj", s, ci) for ci in range(6)]
            P.dma("sp", projd[r0:r0 + nr, :], proj[s][:nr, :], r=allk, w=[("projd", t)])
            P.dma("sp", kv_out[r0:r0 + nr, :], proj[s][:nr, O_KV:O_KV + 768], r=allk, w=[("kvo", t)])
        P.emit_phase()


def conv_silu(P, nr, xsh, s, cw, cb, acc, acc2, tP, tD, xa):
    kA, kB, kX = ("acc", s), ("acc2", s), ("xa", s)
    P.op("pool", lambda e: e.tensor_tensor(out=acc[s][:nr, :], in0=xsh[0][s][:nr, :], in1=cw[:nr, 0, :], op=ALU.mult), r=[("xsh", 0, s), "cw"], w=[kA])
    P.op("pool", lambda e: e.tensor_tensor(out=tP[:nr, :], in0=xsh[1][s][:nr, :], in1=cw[:nr, 1, :], op=ALU.mult), r=[("xsh", 1, s), "cw"], w=["tP"])
    P.op("pool", lambda e: e.tensor_tensor(out=acc[s][:nr, :], in0=acc[s][:nr, :], in1=tP[:nr, :], op=ALU.add), r=[kA, "tP"], w=[kA])
    P.op("dve", lambda e: e.tensor_tensor(out=acc2[s][:nr, :], in0=xsh[2][s][:nr, :], in1=cw[:nr, 2, :], op=ALU.mult), r=[("xsh", 2, s), "cw"], w=[kB])
    P.op("dve", lambda e: e.tensor_tensor(out=tD[:nr, :], in0=xsh[3][s][:nr, :], in1=cw[:nr, 3, :], op=ALU.mult), r=[("xsh", 3, s), "cw"], w=["tD"])
    P.op("dve", lambda e: e.tensor_tensor(out=acc2[s][:nr, :], in0=acc2[s][:nr, :], in1=tD[:nr, :], op=ALU.add), r=[kB, "tD"], w=[kB])
    P.op("dve", lambda e: e.tensor_tensor(out=acc2[s][:nr, :], in0=acc2[s][:nr, :], in1=cb[:nr, :], op=ALU.add), r=[kB, "cb"], w=[kB])
    P.op("dve", lambda e: e.tensor_tensor(out=acc2[s][:nr, :], in0=acc2[s][:nr, :], in1=acc[s][:nr, :], op=ALU.add), r=[kB, kA], w=[kB])
    P.op("act", lambda e: e.activation(out=xa[s][:nr, :], in_=acc2[s][:nr, :], func=AF.Silu), r=[kB], w=[kX])


def softplus_dt(P, nr, dtr, s, dtb, aneg, dt, dA):
    P.op("dve", lambda e: e.tensor_tensor(out=dtr[s][:nr, :], in0=dtr[s][:nr, :], in1=dtb[:nr, :], op=ALU.add), r=[("dtr", s), "dtb"], w=[("dtr", s)])
    P.op("act", lambda e: e.activation(out=dtr[s][:nr, :], in_=dtr[s][:nr, :], func=AF.Exp), r=[("dtr", s)], w=[("dtr", s)])
    P.op("act", lambda e: e.activation(out=dt[s][:nr, :], in_=dtr[s][:nr, :], func=AF.Ln, bias=1.0, scale=1.0), r=[("dtr", s)], w=[("dt", s)])
    P.op("dve", lambda e: e.tensor_tensor(out=dA[s][:nr, :], in0=dt[s][:nr, :], in1=aneg[:nr, :], op=ALU.mult), r=[("dt", s), "aneg"], w=[("dA", s)])


def gate_norm_store(P, nr, yt, ky, xa_x, kx, zt, kz, dsk, nw, xd, sz, junk, ss, rstd, eps, dst_ap):
    P.op("pool", lambda e: e.tensor_tensor(out=xd[:nr, :].rearrange("p (h d) -> p h d", h=8), in0=xa_x.rearrange("p (h d) -> p h d", h=8),
                                           in1=bcast(dsk[:nr, :], [[1, 8], [0, 64]]), op=ALU.mult), r=[kx, "dsk"], w=["xd"])
    P.op("pool", lambda e: e.tensor_tensor(out=yt[:nr, :], in0=yt[:nr, :], in1=xd[:nr, :], op=ALU.add), r=[ky, "xd"], w=[ky])
    P.op("act", lambda e: e.activation(out=sz[:nr, :], in_=zt[:nr, :], func=AF.Silu), r=[kz], w=["sz"])
    P.op("pool", lambda e: e.tensor_tensor(out=yt[:nr, :], in0=yt[:nr, :], in1=sz[:nr, :], op=ALU.mult), r=[ky, "sz"], w=[ky])
    P.op("pool", lambda e: e.memset(ss[:nr, :], 0.0), w=["ss"])
    P.op("act", lambda e: e.activation(out=junk[:nr, :], in_=yt[:nr, :], func=AF.Square, accum_out=ss[:nr, :]), r=[ky, "ss"], w=["junk", "ss"])
    P.op("act", lambda e: e.activation(out=rstd[:nr, :], in_=ss[:nr, :], func=AF.Sqrt, bias=eps[:nr, :], scale=1.0 / 512.0), r=["ss", "eps"], w=["rstd"])
    P.op("dve", lambda e: e.reciprocal(out=rstd[:nr, :], in_=rstd[:nr, :]), r=["rstd"], w=["rstd"])
    P.op("dve", lambda e: e.scalar_tensor_tensor(out=yt[:nr, :], in0=yt[:nr, :], scalar=rstd[:nr, 0:1], in1=nw[:nr, :], op0=ALU.mult, op1=ALU.mult),
         r=[ky, "rstd", "nw"], w=[ky])
    P.dma("sp", dst_ap, yt[:nr, :], r=[ky])


def ssd_phase(kk, P, projd, mixd, p_ssm, s_ssm, state_ssm, state_conv, conv_w, conv_b, dt_bias, a_log, d_skip, ssm_norm_w, consts_d):
    nc = kk.nc
    hist_s = kk.dscr("hist_s", [112, 768])
    xs_d = kk.dscr("xs_d", [64, 512])
    bc_d = kk.dscr("bc_d", [64, 256])
    dts_d = kk.dscr("dts_d", [64, 16])
    ys_d = kk.dscr("ys_d", [64, 512])
    XBC = slice(O_XBC, O_XBC + 768)
    with ExitStack() as st:
        def sbt(n, s, d=F32):
            return st.enter_context(nc.sbuf_tensor("sd_" + n, s, d))
        cst = sbt("cst", [128, 514])
        ident, U, Lt, ones = cst[:, 0:128], cst[:, 128:256], cst[:, 256:384], cst[:, 384:512]
        rmask = cst[:, 512:514]
        cw = sbt("cw", [128, 4, 768]); cb = sbt("cb", [128, 768])
        dtb = sbt("dtb", [128, 8]); aneg = sbt("aneg", [128, 8]); dsk = sbt("dsk", [128, 8]); nw = sbt("nw", [128, 512])
        eps = sbt("eps", [128, 1])
        xsh = [[sbt("xsh%d%d" % (i, s), [128, 768]) for s in range(2)] for i in range(4)]
        zt = [sbt("zt%d" % s, [128, 512]) for s in range(2)]
        acc = [sbt("acc%d" % s, [128, 768]) for s in range(2)]
        acc2 = [sbt("acc2%d" % s, [128, 768]) for s in range(2)]
        xa = [sbt("xa%d" % s, [128, 768]) for s in range(2)]
        tP = sbt("tP", [128, 768]); tD = sbt("tD", [128, 768])
        dtr = [sbt("dtr%d" % s, [128, 8]) for s in range(2)]
        dt = [sbt("dt%d" % s, [128, 8]) for s in range(2)]
        dA = [sbt("dA%d" % s, [128, 8]) for s in range(2)]
        xdt = [sbt("xdt%d" % s, [128, 512], BF16) for s in range(2)]
        Bbf = [sbt("Bbf%d" % s, [128, 128], BF16) for s in range(2)]
        BCT = [[sbt("BCT%d%d" % (s, g), [128, 256], BF16) for g in range(2)] for s in range(2)]
        LdA = [sbt("LdA%d" % s, [128, 8, 128]) for s in range(2)]
        dec = [sbt("dec%d" % s, [128, 8, 128]) for s in range(2)]
        CBm = [sbt("CBm%d" % s, [128, 2, 128]) for s in range(2)]
        WT = [sbt("WT%d" % s, [128, 8, 128], BF16) for s in range(2)]
        expcum = [sbt("expcum%d" % s, [128, 8]) for s in range(2)]
        Edec = sbt("Edec", [128, 4]); dtt = sbt("dtt", [128, 8]); xdtt = sbt("xdtt", [128, 512], BF16)
        ST = sbt("ST", [128, 256]); STb = sbt("STb", [128, 256], BF16); tmpS = sbt("tmpS", [128, 256])
        yt = [sbt("yt%d" % s, [128, 512]) for s in range(2)]
        xd = sbt("xd", [128, 512]); sz = sbt("sz", [128, 512]); junk = sbt("junk", [128, 512])
        ss = sbt("ss", [128, 1]); rstd = sbt("rstd", [128, 1])
        stT = sbt("stT", [128, 4, 64])
        pA = st.enter_context(nc.psum_tensor("sd_pA", [128, 512], F32))
        pM = [st.enter_context(nc.psum_tensor("sd_pM%d" % i, [128, 512], F32)) for i in range(2)]
        pY = st.enter_context(nc.psum_tensor("sd_pY", [128, 512], F32))
        pY0 = st.enter_context(nc.psum_tensor("sd_pY0", [128, 512], F32))
        pC = st.enter_context(nc.psum_tensor("sd_pC", [128, 512], F32))
        pS = st.enter_context(nc.psum_tensor("sd_pS", [128, 512], F32))

        P.dma("sp", cst[:, :], consts_d[:, 0:514], w=["cst"])
        SKIP = os.environ.get("SSD_SKIP", "")
        if "b" not in SKIP:
            P.dma("sp", cw[:, :, :].rearrange("p a b -> p (a b)"), conv_w.rearrange("(o a) b -> o (a b)", o=1).partition_broadcast(128), w=["cw"])
        P.dma("sp", cb[:, :], conv_b[0:1, :].partition_broadcast(128), w=["cb"])
        P.dma("sp", dtb[:, :], dt_bias[0:1, :].partition_broadcast(128), w=["dtb"])
        P.dma("sp", aneg[:, :], a_log[0:1, :].partition_broadcast(128), w=["aneg"])
        P.dma("sp", dsk[:, :], d_skip[0:1, :].partition_broadcast(128), w=["dsk"])
        P.dma("sp", nw[:, :], ssm_norm_w[0:1, :].partition_broadcast(128), w=["nw"])
        P.op("pool", lambda e: e.memset(eps[:, :], 1e-5), w=["eps"])
        P.op("act", lambda e: e.activation(out=aneg[:, :], in_=aneg[:, :], func=AF.Exp), r=["aneg"], w=["aneg"])
        P.op("dve", lambda e: e.tensor_scalar(out=aneg[:, :], in0=aneg[:, :], scalar1=-1.0, scalar2=None, op0=ALU.mult), r=["aneg"], w=["aneg"])
        P.op("pool", lambda e: e.memset(ST[:, :], 0.0), w=["ST"])
        P.op("pool", lambda e: e.memset(STb[:, :], 0.0), w=["STb"])
        if "c" not in SKIP:
            P.dma("act", hist_s[0:48, :].rearrange("(j s) f -> s j f", j=3), state_conv[:, :, :], w=["hist"])
            P.dma("act", hist_s[48:112, :], projd[SEQ:NTOK, XBC], w=["hist2"])

        NCH = int(os.environ.get("SSD_NCH", "32"))
        DO_S = os.environ.get("SSD_SAMPLE", "1") == "1"
        for c in range(NCH):
            s = c % 2
            r0 = c * 128
            for i in range(4):
                key = ("xsh", i, s)
                off = 3 - i
                if c == 0 and off > 0:
                    P.op("pool", lambda e, i=i, s=s: e.memset(xsh[i][s][:, :], 0.0), w=[key])
                    P.dma("sp", xsh[i][s][off:128, :], projd[0:128 - off, XBC], w=[key])
                else:
                    P.dma("sp", xsh[i][s][:, :], projd[r0 - off:r0 - off + 128, XBC], w=[key])
            P.dma("sp", zt[s][:, :], projd[r0:r0 + 128, 0:512], w=[("zt", s)])
            P.dma("sp", dtr[s][:, :], projd[r0:r0 + 128, O_DT:O_DT + 8], w=[("dtr", s)])
            conv_silu(P, 128, xsh, s, cw, cb, acc, acc2, tP, tD, xa)
            softplus_dt(P, 128, dtr, s, dtb, aneg, dt, dA)
            kX = ("xa", s)
            xv = xa[s][:, 0:512].rearrange("p (h d) -> p h d", h=8)
            P.op("dve", lambda e, s=s, xv=xv: e.tensor_tensor(out=xdt[s][:, :].rearrange("p (h d) -> p h d", h=8), in0=xv,
                                                              in1=bcast(dt[s][:, :], [[1, 8], [0, 64]]), op=ALU.mult), r=[kX, ("dt", s)], w=[("xdt", s)])
            P.op("pool", lambda e, s=s: e.tensor_copy(out=Bbf[s][:, :], in_=xa[s][:, 512:640]), r=[kX], w=[("Bbf", s)])
            P.op("pe", lambda e, s=s: e.transpose(out=pA[:, 0:128], in_=xa[s][:, 512:640], identity=ident), r=[kX, "cst"], w=["pA0"])
            P.op("pe", lambda e, s=s: e.transpose(out=pA[:, 128:256], in_=xa[s][:, 640:768], identity=ident), r=[kX, "cst"], w=["pA0"])
            for g in range(2):
                P.op("act", lambda e, s=s, g=g: e.activation(out=BCT[s][g][:, :], in_=pA[:, 0:256], func=AF.Copy, scale=rmask[:, g:g + 1]),
                     r=["pA0", "cst"], w=[("BCT", s)])
            for g in range(2):
                P.op("pe", lambda e, s=s, g=g: e.matmul(pA[:, 256 + g * 128:256 + (g + 1) * 128], lhsT=BCT[s][g][:, 0:128],
                                                         rhs=BCT[s][g][:, 128:256], start=True, stop=True), r=[("BCT", s)], w=["pA1"])
            P.op("dve", lambda e, s=s: e.tensor_tensor(out=CBm[s][:, :, :], in0=pA[:, 256:512].rearrange("p (g l) -> p g l", g=2),
                                                       in1=bcast(U, [[0, 2], [1, 128]]), op=ALU.mult), r=["pA1", "cst"], w=[("CBm", s)])
            P.op("dve", lambda e, s=s: e.tensor_tensor(out=LdA[s][:, :, :], in0=bcast(Lt, [[0, 8], [1, 128]]),
                                                       in1=bcast(dA[s][:, :], [[1, 8], [0, 128]]), op=ALU.mult), r=[("dA", s), "cst"], w=[("LdA", s)])
            for h in range(8):
                P.op("pe", lambda e, s=s, h=h: e.matmul(pM[h // 4][:, (h % 4) * 128:(h % 4 + 1) * 128], lhsT=LdA[s][:, h, :], rhs=U,
                                                         start=True, stop=True), r=[("LdA", s), "cst"], w=[("pM", h // 4)])
            for b in range(2):
                P.op("act", lambda e, s=s, b=b: e.activation(out=dec[s][:, b * 4:(b + 1) * 4, :].rearrange("p h l -> p (h l)"), in_=pM[b][:, :], func=AF.Exp),
                     r=[("pM", b)], w=[("dec", s)])
            P.op("dve", lambda e, s=s: e.tensor_tensor(out=WT[s][:, :, :].rearrange("p (g h) l -> p g h l", g=2),
                                                       in0=dec[s][:, :, :].rearrange("p (g h) l -> p g h l", g=2),
                                                       in1=bcast(CBm[s][:, :, :], [[128, 2], [0, 4], [1, 128]]), op=ALU.mult),
                 r=[("dec", s), ("CBm", s)], w=[("WT", s)])
            P.op("pe", lambda e, s=s: e.matmul(pC[:, 0:8], lhsT=U, rhs=dA[s][:, :], start=True, stop=True), r=[("dA", s), "cst"], w=["pC"])
            P.op("pe", lambda e, s=s: e.matmul(pC[:, 8:16], lhsT=ones, rhs=dA[s][:, :], start=True, stop=True), r=[("dA", s), "cst"], w=["pC"])
            P.op("act", lambda e, s=s: e.activation(out=expcum[s][:, :], in_=pC[:, 0:8], func=AF.Exp), r=["pC"], w=[("expcum", s)])
            P.op("act", lambda e: e.activation(out=Edec[0:64, :], in_=pC[0:64, 8:12], func=AF.Exp), r=["pC"], w=["Edec"])
            if "h" not in SKIP:
                P.op("act", lambda e: e.activation(out=Edec[64:128, :], in_=pC[64:128, 12:16], func=AF.Exp), r=["pC"], w=["Edec"])
            for h in range(8):
                P.op("pe", lambda e, s=s, h=h: e.matmul(pY[:, h * 64:(h + 1) * 64], lhsT=WT[s][:, h, :], rhs=xdt[s][:, h * 64:(h + 1) * 64],
                                                         start=True, stop=True), r=[("WT", s), ("xdt", s)], w=["pY"])
            for g in range(2):
                P.op("pe", lambda e, s=s, g=g: e.matmul(pY0[:, g * 256:(g + 1) * 256], lhsT=BCT[s][g][:, 128:256],
                                                         rhs=STb[:, :], start=True, stop=True), r=[("BCT", s), "STb"], w=["pY0"])
            ky = ("yt", s)
            P.op("dve", lambda e, s=s: e.tensor_tensor(out=yt[s][:, :].rearrange("p (h d) -> p h d", h=8), in0=pY0[:, :].rearrange("p (h d) -> p h d", h=8),
                                                       in1=bcast(expcum[s][:, :], [[1, 8], [0, 64]]), op=ALU.mult), r=["pY0", ("expcum", s)], w=[ky])
            P.op("dve", lambda e, s=s: e.tensor_tensor(out=yt[s][:, :], in0=yt[s][:, :], in1=pY[:, :], op=ALU.add), r=[ky, "pY"], w=[ky])
            gate_norm_store(P, 128, yt[s], ky, xa[s][:, 0:512], kX, zt[s], ("zt", s), dsk, nw, xd, sz, junk, ss, rstd, eps, mixd[r0:r0 + 128, 0:512])
            P.op("dve", lambda e, s=s: e.tensor_tensor(out=dtt[:, :], in0=dt[s][:, :], in1=dec[s][:, :, 127], op=ALU.mult), r=[("dt", s), ("dec", s)], w=["dtt"])
            P.op("dve", lambda e, s=s, xv=xv: e.tensor_tensor(out=xdtt[:, :].rearrange("p (h d) -> p h d", h=8), in0=xv,
                                                              in1=bcast(dtt[:, :], [[1, 8], [0, 64]]), op=ALU.mult), r=[kX, "dtt"], w=["xdtt"])
            P.op("pe", lambda e, s=s: e.matmul(pS[:, :], lhsT=Bbf[s][:, :], rhs=xdtt[:, :], start=True, stop=True), r=[("Bbf", s), "xdtt"], w=["pS"])
            for g in range(1 if "h" in SKIP else 2):
                rows = slice(g * 64, (g + 1) * 64)
                P.op("dve", lambda e, rows=rows: e.tensor_tensor(out=tmpS[rows, :].rearrange("p (h d) -> p h d", h=4), in0=ST[rows, :].rearrange("p (h d) -> p h d", h=4),
                                                                 in1=bcast(Edec[rows, :], [[1, 4], [0, 64]]), op=ALU.mult), r=["ST", "Edec"], w=["tmpS"])
                P.op("dve", lambda e, rows=rows, g=g: e.tensor_tensor(out=ST[rows, :], in0=tmpS[rows, :], in1=pS[rows, g * 256:(g + 1) * 256], op=ALU.add),
                     r=["tmpS", "pS"], w=["ST"])
            P.op("act", lambda e: e.activation(out=STb[:, :], in_=ST[:, :], func=AF.Copy), r=["ST"], w=["STb"])
        for g in (range(2) if "d" not in SKIP else []):
            for c2 in range(2):
                rows = slice(g * 64, (g + 1) * 64)
                idx = g * 2 + c2
                P.op("pe", lambda e, g=g, c2=c2, idx=idx: e.matmul(pA[:, idx * 64:(idx + 1) * 64], lhsT=ST[:, c2 * 128:(c2 + 1) * 128],
                                                                   rhs=cst[:, g * 64:(g + 1) * 64], start=True, stop=True), r=["ST", "cst"], w=["pA0", "pA1"])
        if "e" not in SKIP:
            P.op("dve", lambda e: e.tensor_copy(out=stT[:, :, :].rearrange("p a n -> p (a n)"), in_=pA[:, 0:256]), r=["pA0", "pA1"], w=["stT"])
            P.dma("sp", p_ssm.rearrange("(a q) p n -> (q p) a n", q=2), stT[:, :, :], r=["stT"])

        s = 0
        if not DO_S:
            P.emit_phase()
            return
        for i in range(4):
            P.dma("sp", xsh[i][s][0:64, :], hist_s[16 * i:16 * i + 64, :], r=["hist", "hist2"], w=[("xsh", i, s)])
        P.dma("sp", zt[s][0:64, :], projd[SEQ:NTOK, 0:512], w=[("zt", s)])
        P.dma("sp", dtr[s][0:64, :], projd[SEQ:NTOK, O_DT:O_DT + 8], w=[("dtr", s)])
        conv_silu(P, 64, xsh, s, cw, cb, acc, acc2, tP, tD, xa)
        softplus_dt(P, 64, dtr, s, dtb, aneg, dt, dA)
        kX = ("xa", s)
        P.dma("sp", xs_d[:, :], xa[s][0:64, 0:512], r=[kX], w=["xs_d"])
        P.dma("sp", bc_d[:, :], xa[s][0:64, 512:768], r=[kX], w=["bc_d"])
        P.dma("sp", dts_d[:, 0:8], dt[s][0:64, :], r=[("dt", s)], w=["dts_d"])
        P.dma("sp", dts_d[:, 8:16], dA[s][0:64, :], r=[("dA", s)], w=["dts_d"])
        with ExitStack() as st2:
            def sb2(n, sh, d=F32):
                return st2.enter_context(nc.sbuf_tensor("sd2_" + n, sh, d))
            S = sb2("S", [128, 64, 64]); T1 = sb2("T1", [128, 64, 64])
            X = sb2("X", [128, 4, 64]); Bt = sb2("Bt", [128, 4, 64]); Ct = sb2("Ct", [128, 4, 64])
            dtA = sb2("dtA", [128, 2, 4]); ea = sb2("ea", [128, 4]); Xdt = sb2("Xdt", [128, 4, 64]); Yv = sb2("Yv", [128, 4, 64])
            P.dma("sp", S[:, :, :].rearrange("p a b -> p (a b)"), state_ssm.rearrange("s h p n -> (s h) (p n)"), w=["S"])
            P.dma("sp", X[:, :, :], xs_d.rearrange("(t s) (h d) -> (s h) t d", t=4, h=8), r=["xs_d"], w=["X"])
            for hh in range(4):
                for g in range(2):
                    sB = AP(bc_d.tensor, bc_d.offset + g * 64, [[256, 16], [16 * 256, 4], [1, 64]])
                    sC = AP(bc_d.tensor, bc_d.offset + 128 + g * 64, [[256, 16], [16 * 256, 4], [1, 64]])
                    P.dma("sp", Bt[4 * g + hh:128:8, :, :], sB, r=["bc_d"], w=["Bt"])
                    P.dma("sp", Ct[4 * g + hh:128:8, :, :], sC, r=["bc_d"], w=["Ct"])
            for seq in range(16):
                for j in range(2):
                    P.dma("sp", dtA[seq * 8:(seq + 1) * 8, j, :], AP(dts_d.tensor, dts_d.offset + seq * 16 + j * 8, [[1, 8], [16 * 16, 4]]),
                          r=["dts_d"], w=["dtA"], allow_slow_non_contiguous=True)
            P.op("act", lambda e: e.activation(out=ea[:, :], in_=dtA[:, 1, :], func=AF.Exp), r=["dtA"], w=["ea"])
            P.op("dve", lambda e: e.tensor_tensor(out=Xdt[:, :, :], in0=X[:, :, :], in1=bcast(dtA[:, 0, :], [[1, 4], [0, 64]]), op=ALU.mult), r=["X", "dtA"], w=["Xdt"])
            for t in range(TS):
                P.op("dve", lambda e, t=t: e.tensor_scalar(out=S[:, :, :], in0=S[:, :, :], scalar1=ea[:, t:t + 1], scalar2=None, op0=ALU.mult), r=["S", "ea"], w=["S"])
                P.op("pool", lambda e, t=t: e.tensor_tensor(out=T1[:, :, :], in0=bcast(Xdt[:, t, :], [[1, 64], [0, 64]]), in1=bcast(Bt[:, t, :], [[0, 64], [1, 64]]), op=ALU.mult),
                     r=["Xdt", "Bt"], w=["T1"])
                P.op("dve", lambda e: e.tensor_tensor(out=S[:, :, :], in0=S[:, :, :], in1=T1[:, :, :], op=ALU.add), r=["S", "T1"], w=["S"])
                P.op("pool", lambda e, t=t: e.tensor_tensor(out=T1[:, :, :], in0=S[:, :, :], in1=bcast(Ct[:, t, :], [[0, 64], [1, 64]]), op=ALU.mult), r=["S", "Ct"], w=["T1"])
                P.op("dve", lambda e, t=t: e.tensor_reduce(out=Yv[:, t, :], in_=T1[:, :, :], axis=AX.X, op=ALU.add), r=["T1"], w=["Yv"])
            P.dma("sp", s_ssm.rearrange("s h p n -> (s h) (p n)"), S[:, :, :].rearrange("p a b -> p (a b)"), r=["S"])
            P.dma("sp", ys_d.rearrange("(t s) (h d) -> (s h) t d", t=4, h=8), Yv[:, :, :], r=["Yv"], w=["ys_d"])
            ky = ("yt", 0)
            P.dma("sp", yt[0][0:64, :], ys_d[:, :], r=["ys_d"], w=[ky])
            gate_norm_store(P, 64, yt[0], ky, xa[s][0:64, 0:512], kX, zt[s], ("zt", s), dsk, nw, xd, sz, junk, ss, rstd, eps, mixd[SEQ:NTOK, 0:512])
            P.emit_phase()


def outproj_phase(kk, P, mixd, h1, h2, w_out, ln_g, ln_b, ident_d):
    nc = kk.nc
    with ExitStack() as st:
        def sbt(n, s, d=F32):
            return st.enter_context(nc.sbuf_tensor("op_" + n, s, d))
        Wo = sbt("Wo", [128, 8, D], BF16)
        ident = sbt("ident", [128, 128])
        gt = sbt("gt", [128, D]); bt = sbt("bt", [128, D])
        mt = [sbt("mt%d" % s, [128, D]) for s in range(2)]
        h1t = [sbt("h1t%d" % s, [128, D]) for s in range(2)]
        mT = [sbt("mT%d" % s, [128, 8, 128], BF16) for s in range(2)]
        zt = [sbt("zt%d" % s, [128, D]) for s in range(2)]
        ot = [sbt("ot%d" % s, [128, D]) for s in range(2)]
        sb = dict(stats=sbt("stats", [128, 2, 6]), mv=sbt("mv", [128, 2]), rstd=sbt("rstd", [128, 1]),
                  nb=sbt("nb", [128, 1]), eps=sbt("eps", [128, 1]))
        pT = [st.enter_context(nc.psum_tensor("op_pT%d" % i, [128, 512], F32)) for i in range(2)]
        pO = [[st.enter_context(nc.psum_tensor("op_pO%d%d" % (s, h), [128, 512], F32)) for h in range(2)] for s in range(2)]
        P.op("pool", lambda e: e.memset(sb["eps"][:, :], LN_EPS), w=["eps"])
        P.dma("sp", ident[:, :], ident_d[:, :], w=["ident"])
        P.dma("sp", gt[:, :], ln_g[0:1, :].partition_broadcast(128), w=["op_g"])
        P.dma("sp", bt[:, :], ln_b[0:1, :].partition_broadcast(128), w=["op_b"])
        for k in range(8):
            P.dma("pool", Wo[:, k, :], w_out[k * 128:(k + 1) * 128, :], w=[("Wo", k)])
        for t, (r0, nr) in enumerate(TILES):
            s = t % 2
            P.dma("sp", mt[s][:nr, :], mixd[r0:r0 + nr, :], w=[("mt", s)])
            P.dma("sp", h1t[s][:nr, :], h1[r0:r0 + nr, :], w=[("h1t", s)])
            transpose_tile(P, mt[s], nr, ("mt", s), pT, lambda kh, s=s, nr=nr: mT[s][:, kh * 4:(kh + 1) * 4, :nr], ("mT", s), ident)
            for half in range(2):
                for k in range(8):
                    P.op("pe", lambda e, s=s, nr=nr, half=half, k=k: e.matmul(pO[s][half][:nr, :], lhsT=mT[s][:, k, :nr],
                                                                               rhs=Wo[:, k, half * 512:(half + 1) * 512], start=(k == 0), stop=(k == 7)),
                         r=[("mT", s), ("Wo", k)], w=[("pO", s, half)])
                P.op("dve", lambda e, s=s, nr=nr, half=half: e.scalar_tensor_tensor(
                    out=zt[s][:nr, half * 512:(half + 1) * 512], in0=h1t[s][:nr, half * 512:(half + 1) * 512], scalar=ALPHA,
                    in1=pO[s][half][:nr, :], op0=ALU.mult, op1=ALU.add), r=[("h1t", s), ("pO", s, half)], w=[("zt", s)])
            layer_norm_tile(P, sb, zt[s], nr, gt, bt, ot[s], ("zt", s), ("ot", s), "op_")
            P.dma("sp", h2[r0:r0 + nr, :], ot[s][:nr, :], r=[("ot", s)], w=[("h2", t)])
        P.emit_phase()


NEG = -30000.0
GELU_C = 0.044715
GELU_S = 2.0 * 0.7978845608028654


def nsa_prompt_phase(kk, P, projd, kv_out, mixd, cw, consts_d, consts2_d, seltab_d):
    nc = kk.nc
    NQT = int(os.environ.get("NSA_NQT", "32"))
    with ExitStack() as st:
        def sbt(n, s, d=F32):
            return st.enter_context(nc.sbuf_tensor("na_" + n, s, d))
        cst = sbt("cst", [128, 514])
        ident = cst[:, 0:128]
        rmask = cst[:, 512:514]
        c2 = sbt("c2", [128, 1024])
        cval, causb, wbias, cvalid = c2[:, 0:255], c2[:, 255:383], c2[:, 383:1023], c2[:, 1023:1024]
        identb = sbt("identb", [128, 128], BF16)
        rm8 = sbt("rm8", [128, 2])
        kTs = sbt("kTs", [128, SEQ], BF16); kTw = sbt("kTw", [128, SEQ], BF16)
        Vs = sbt("Vs", [128, 32, 128], BF16); Vw = sbt("Vw", [128, 32, 128], BF16)
        KT2 = sbt("KT2", [128, 4, 2048], BF16)
        W1 = [sbt("W1%d" % x, [128, 16, 128], BF16) for x in range(2)]
        pec = [sbt("pec%d" % x, [128, 16], BF16) for x in range(2)]
        perow = [sbt("perow%d" % x, [16, 128]) for x in range(2)]
        w2p = [[sbt("w2p%d%d" % (x, g), [128, 128], BF16) for g in range(2)] for x in range(2)]
        peterm = [sbt("peterm%d" % x, [128, 1]) for x in range(2)]
        gl = [[sbt("gl%d%d" % (x, g), [128, 256], BF16) for g in range(2)] for x in range(2)]
        gx = sbt("gx", [128, 256]); gu = sbt("gu", [128, 256])
        kcT = sbt("kcT", [128, 256], BF16); vc = sbt("vc", [128, 2, 128], BF16)
        kvt = [sbt("kvt%d" % s, [128, 768]) for s in range(2)]
        pair = [sbt("pair%d" % s, [128, 4, 128]) for s in range(2)]
        qt = [sbt("qt%d" % s, [128, 512]) for s in range(2)]
        gt_ = [sbt("gate%d" % s, [128, 24]) for s in range(2)]
        sg = [sbt("sg%d" % s, [128, 24]) for s in range(2)]
        qTh = [sbt("qTh%d" % s, [128, 8, 128], BF16) for s in range(2)]
        bias_c = [sbt("biasc%d" % s, [128, 255]) for s in range(2)]
        Ssb = [sbt("Ssb%d" % s, [128, SEQ]) for s in range(2)]
        Pbf = [sbt("Pbf%d" % s, [128, SEQ], BF16) for s in range(2)]
        PTsb = [sbt("PTsb%d" % s, [128, 32, 128], BF16) for s in range(2)]
        Pf = [sbt("Pf%d" % s, [128, 255]) for s in range(2)]
        pacc = [sbt("pacc%d" % g, [128, 260]) for g in range(2)]
        imp = sbt("imp", [128, 64]); cand = sbt("cand", [128, 64]); cand2 = sbt("cand2", [128, 64])
        m8 = sbt("m8", [128, 16]); selb = [sbt("selb%d" % g, [128, 64]) for g in range(2)]
        seltab = [sbt("seltab%d" % s, [128, 128]) for s in range(2)]
        mx = [sbt("mx%d" % s, [128, 1]) for s in range(4)]
        rsum = [sbt("rsum%d" % s, [128, 1]) for s in range(4)]
        gs = [sbt("gs%d" % s, [128, 1]) for s in range(4)]
        onsa = [sbt("onsa%d" % s, [128, 512]) for s in range(2)]
        pS = [st.enter_context(nc.psum_tensor("na_pS%d" % i, [128, 512], F32)) for i in range(2)]
        pPT = [st.enter_context(nc.psum_tensor("na_pPT%d" % i, [128, 1024], BF16)) for i in range(2)]
        pO = [st.enter_context(nc.psum_tensor("na_pO%d" % i, [128, 512], F32)) for i in range(3)]
        pT = st.enter_context(nc.psum_tensor("na_pT", [128, 512], F32))

        P.dma("sp", cst[:, :], consts_d[:, 0:514], w=["cst"])
        P.dma("sp", c2[:, :], consts2_d[:, :], w=["c2"])
        P.op("dve", lambda e: e.tensor_copy(out=identb[:, :], in_=ident), r=["cst"], w=["identb"])
        P.op("dve", lambda e: e.tensor_scalar(out=rm8[:, :], in0=rmask, scalar1=0.125, scalar2=None, op0=ALU.mult), r=["cst"], w=["rm8"])
        for x, pre in enumerate(("cmp_k_", "cmp_v_") if "w" not in os.environ.get("NSA_SKIP", "") else ()):
            w1 = cw[pre + "w1"]
            for j in range(16):
                P.dma("pool", W1[x][:, j, :], w1[j * 128:(j + 1) * 128, :], w=[("W1", x)])
            P.dma("sp", perow[x][:, :], cw[pre + "pe"].rearrange("(j p) o -> j (p o)", p=128), w=[("perow", x)])
            P.op("pe", lambda e, x=x: e.transpose(out=pT[:, 0:16], in_=perow[x][:, :], identity=cst[0:16, 0:16]), r=[("perow", x), "cst"], w=["pT"])
            P.op("act", lambda e, x=x: e.activation(out=pec[x][:, :], in_=pT[:, 0:16], func=AF.Copy), r=["pT"], w=[("pec", x)])
            for g in range(2):
                P.op("pool", lambda e, x=x, g=g: e.memset(w2p[x][g][:, :], 0.0), w=[("w2p", x, g)])
                P.dma("pool", w2p[x][g][:, g * 64:(g + 1) * 64], cw[pre + "w2"][:, :], w=[("w2p", x, g)])
        for g in range(2):
            P.op("pool", lambda e, g=g: e.memset(pacc[g][:, :], 0.0), w=[("pacc", g)])

        for t in (range(32) if "k" not in os.environ.get("NSA_SKIP", "") else []):
            s = t % 2
            P.dma("sp", kvt[s][:, :], kv_out[t * 128:(t + 1) * 128, :], w=[("kvt", s)])
            KVL = int(os.environ.get("NSA_KV", "9"))
            if KVL >= 1:
                P.op("pe", lambda e, s=s: e.transpose(out=pT[:, 0:128], in_=kvt[s][:, 256:384], identity=ident), r=[("kvt", s), "cst"], w=["pT"])
                P.op("pe", lambda e, s=s: e.transpose(out=pT[:, 128:256], in_=kvt[s][:, 512:640], identity=ident), r=[("kvt", s), "cst"], w=["pT"])
            if KVL >= 2:
                P.op("act", lambda e, t=t: e.activation(out=kTs[:, t * 128:(t + 1) * 128], in_=pT[:, 0:128], func=AF.Copy), r=["pT"], w=["kTs"])
            if KVL >= 3:
                P.op("dve", lambda e, t=t: e.tensor_copy(out=kTw[:, t * 128:(t + 1) * 128], in_=pT[:, 128:256]), r=["pT"], w=["kTw"])
            if KVL >= 4:
                P.op("pool", lambda e, s=s, t=t: e.tensor_copy(out=Vs[:, t, :], in_=kvt[s][:, 384:512]), r=[("kvt", s)], w=["Vs"])
                P.op("pool", lambda e, s=s, t=t: e.tensor_copy(out=Vw[:, t, :], in_=kvt[s][:, 640:768]), r=[("kvt", s)], w=["Vw"])

        for b in (range(16) if "r" not in os.environ.get("NSA_SKIP", "") else []):
            s = b % 2
            for x in range(2):
                for g in range(2):
                    src = AP(kv_out.tensor, kv_out.offset + b * 256 * 768 + x * 128 + g * 64, [[1536, 128], [768, 2], [1, 64]])
                    P.dma("sp", pair[s][:, x * 2 + g, :].rearrange("p (e d) -> p e d", e=2), src, w=[("pair", s)])
            for xg in range(4):
                P.op("pe", lambda e, s=s, xg=xg: e.transpose(out=pT[:, xg * 128:(xg + 1) * 128], in_=pair[s][:, xg, :], identity=ident),
                     r=[("pair", s), "cst"], w=["pT"])
            P.op("act", lambda e, b=b: e.activation(out=KT2[:, :, b * 128:(b + 1) * 128], in_=pT[:, :].rearrange("p (a m) -> p a m", a=4), func=AF.Copy),
                 r=["pT"], w=["KT2"])
        NSKIP = os.environ.get("NSA_SKIP", "")
        for x in (range(2) if "z" not in NSKIP else []):
            for j in (range(16) if "p" not in NSKIP else []):
                P.op("pe", lambda e, x=x, j=j: e.matmul(pO[0][:, 0:1], lhsT=W1[x][:, j, :], rhs=pec[x][:, j:j + 1], start=(j == 0), stop=(j == 15)),
                     r=[("W1", x), ("pec", x)], w=["pO0"])
            P.op("dve", lambda e, x=x: e.tensor_copy(out=peterm[x][:, :], in_=pO[0][:, 0:1]), r=["pO0"], w=[("peterm", x)])
            for g in range(2):
                b = g % 2
                for j in (range(16) if "c" not in NSKIP else []):
                    P.op("pe", lambda e, x=x, g=g, j=j, b=b: e.matmul(pS[b][:, 0:255], lhsT=W1[x][:, j, :], rhs=KT2[:, x * 2 + g, j:j + 8 * 254 + 1:8],
                                                                       start=(j == 0), stop=(j == 15)), r=[("W1", x), "KT2"], w=[("pS", b)])
                P.op("act", lambda e, x=x, b=b: e.activation(out=gx[:, 0:255], in_=pS[b][:, 0:255], func=AF.Identity, bias=peterm[x][:, 0:1], scale=1.0),
                     r=[("pS", b), ("peterm", x)], w=["gx"])
                P.op("dve", lambda e: e.tensor_tensor(out=gu[:, 0:255], in0=gx[:, 0:255], in1=gx[:, 0:255], op=ALU.mult), r=["gx"], w=["gu"])
                P.op("dve", lambda e: e.tensor_scalar(out=gu[:, 0:255], in0=gu[:, 0:255], scalar1=GELU_C, scalar2=1.0, op0=ALU.mult, op1=ALU.add), r=["gu"], w=["gu"])
                P.op("dve", lambda e: e.tensor_tensor(out=gu[:, 0:255], in0=gu[:, 0:255], in1=gx[:, 0:255], op=ALU.mult), r=["gu", "gx"], w=["gu"])
                P.op("act", lambda e: e.activation(out=gu[:, 0:255], in_=gu[:, 0:255], func=AF.Sigmoid, scale=GELU_S), r=["gu"], w=["gu"])
                P.op("dve", lambda e, x=x, g=g: e.tensor_tensor(out=gl[x][g][:, 0:255], in0=gu[:, 0:255], in1=gx[:, 0:255], op=ALU.mult), r=["gu", "gx"], w=[("gl", x, g)])
        for g in (range(2) if "z" not in NSKIP else []):
            P.op("pe", lambda e, g=g: e.matmul(pS[0][:, 0:255], lhsT=w2p[0][g][:, :], rhs=gl[0][g][:, 0:255], start=(g == 0), stop=(g == 1)),
                 r=[("w2p", 0, g), ("gl", 0, g)], w=[("pS", 0)])
        P.op("act", lambda e: e.activation(out=kcT[:, 0:255], in_=pS[0][:, 0:255], func=AF.Copy), r=[("pS", 0)], w=["kcT"])
        for ct, (c0, cn) in enumerate(((0, 128), (128, 127)) if "v" not in NSKIP else ()):
            for g in range(2):
                P.op("pe", lambda e, g=g, c0=c0, cn=cn, ct=ct: e.matmul(pS[1][:cn, ct * 128:(ct + 1) * 128], lhsT=gl[1][g][:, c0:c0 + cn], rhs=w2p[1][g][:, :],
                                                                         start=(g == 0), stop=(g == 1)), r=[("w2p", 1, g), ("gl", 1, g)], w=[("pS", 1)])
            P.op("act", lambda e, cn=cn, ct=ct: e.activation(out=vc[:cn, ct, :], in_=pS[1][:cn, ct * 128:(ct + 1) * 128], func=AF.Copy), r=[("pS", 1)], w=["vc"])

        cnt = {"a": 0}

        def attend(i, h, br, qs, kT, kkey, V, vkey, tile_lo, ntile, mode, first):
            g = h // 4
            a = cnt["a"]; cnt["a"] += 1
            b = a % 2
            m4 = a % 4
            nk = ntile * 128 if mode != "cmp" else 255
            kS, kP, kPT = ("Ssb", b), ("Pbf", b), ("PTsb", b)
            nch = (nk + 511) // 512
            for ch in range(nch):
                k0 = ch * 512
                w = min(512, nk - k0)
                pb = (a + ch) % 2
                kcol = tile_lo * 128 + k0
                P.op("pe", lambda e, pb=pb, w=w, kcol=kcol: e.matmul(pS[pb][:, 0:w], lhsT=qTh[qs][:, h, :], rhs=kT[:, kcol:kcol + w], start=True, stop=True),
                     r=[("qTh", qs), kkey], w=[("pS", pb)])
                if mode == "cmp":
                    P.op("dve", lambda e, pb=pb, w=w, k0=k0: e.tensor_tensor(out=Ssb[b][:, k0:k0 + w], in0=pS[pb][:, 0:w], in1=bias_c[qs][:, 0:w], op=ALU.add),
                         r=[("pS", pb), ("biasc", qs)], w=[kS])
                elif mode == "win":
                    woff = (5 - ntile) * 128 + k0
                    P.op("dve", lambda e, pb=pb, w=w, k0=k0, woff=woff: e.tensor_tensor(out=Ssb[b][:, k0:k0 + w], in0=pS[pb][:, 0:w], in1=wbias[:, woff:woff + w], op=ALU.add),
                         r=[("pS", pb), "c2"], w=[kS])
                elif mode == "sel" and i >= 8:
                    nb = w // 64
                    blk0 = (tile_lo * 128 + k0) // 64
                    P.op("dve", lambda e, pb=pb, w=w, k0=k0, nb=nb, blk0=blk0: e.tensor_tensor(
                        out=Ssb[b][:, k0:k0 + w].rearrange("p (n d) -> p n d", d=64), in0=pS[pb][:, 0:w].rearrange("p (n d) -> p n d", d=64),
                        in1=bcast(selb[g][:, blk0:blk0 + nb], [[1, nb], [0, 64]]), op=ALU.add), r=[("pS", pb), ("selb", g)], w=[kS])
                else:
                    P.op("act", lambda e, pb=pb, w=w, k0=k0: e.activation(out=Ssb[b][:, k0:k0 + w], in_=pS[pb][:, 0:w], func=AF.Copy), r=[("pS", pb)], w=[kS])
            if mode == "sel":
                d0 = (ntile - 1) * 128
                P.op("pool", lambda e, d0=d0: e.tensor_tensor(out=Ssb[b][:, d0:d0 + 128], in0=Ssb[b][:, d0:d0 + 128], in1=causb, op=ALU.add), r=[kS, "c2"], w=[kS])
            P.op("dve", lambda e: e.tensor_reduce(out=mx[m4][:, :], in_=Ssb[b][:, 0:nk], axis=AX.X, op=ALU.max, negate=True), r=[kS], w=[("mx", m4)])
            P.op("pool", lambda e: e.memset(rsum[m4][:, :], 0.0), w=[("rsum", m4)])
            if mode == "cmp":
                pf = Pf[b]
                P.op("act", lambda e: e.activation(out=pf[:, 0:255], in_=Ssb[b][:, 0:255], func=AF.Exp, bias=mx[m4][:, 0:1], scale=1.0, accum_out=rsum[m4][:, :]),
                     r=[kS, ("mx", m4), ("rsum", m4)], w=[("Pf", b), ("rsum", m4)])
                P.op("pool", lambda e: e.tensor_copy(out=Pbf[b][:, 0:255], in_=pf[:, 0:255]), r=[("Pf", b)], w=[kP])
            else:
                P.op("act", lambda e: e.activation(out=Pbf[b][:, 0:nk], in_=Ssb[b][:, 0:nk], func=AF.Exp, bias=mx[m4][:, 0:1], scale=1.0, accum_out=rsum[m4][:, :]),
                     r=[kS, ("mx", m4), ("rsum", m4)], w=[kP, ("rsum", m4)])
            P.op("dve", lambda e: e.reciprocal(out=rsum[m4][:, :], in_=rsum[m4][:, :]), r=[("rsum", m4)], w=[("rsum", m4)])
            if mode == "cmp" and i >= 8:
                r = h % 4
                if r == 0:
                    P.op("dve", lambda e: e.tensor_scalar(out=pacc[g][:, 1:256], in0=Pf[b][:, 0:255], scalar1=rsum[m4][:, 0:1], scalar2=None, op0=ALU.mult),
                         r=[("Pf", b), ("rsum", m4)], w=[("pacc", g)])
                else:
                    P.op("dve", lambda e: e.scalar_tensor_tensor(out=pacc[g][:, 1:256], in0=Pf[b][:, 0:255], scalar=rsum[m4][:, 0:1], in1=pacc[g][:, 1:256],
                                                                 op0=ALU.mult, op1=ALU.add), r=[("Pf", b), ("rsum", m4), ("pacc", g)], w=[("pacc", g)])
            tiles = [(j * 128, min(128, nk - j * 128)) for j in range((nk + 127) // 128)]
            for j0 in range(0, len(tiles), 8):
                grp = tiles[j0:j0 + 8]
                pb = (a + j0 // 8) % 2
                for jj, (c0, cn) in enumerate(grp):
                    P.op("pe", lambda e, pb=pb, jj=jj, c0=c0, cn=cn: e.transpose(out=pPT[pb][:cn, jj * 128:(jj + 1) * 128], in_=Pbf[b][:, c0:c0 + cn], identity=identb[:, :]),
                         r=[kP, "identb"], w=[("pPT", pb)])
                ng = len(grp)
                eng = "act" if (j0 // 8) % 2 == 0 else "dve"
                if all(cn == 128 for _, cn in grp):
                    parts = [(pPT[pb][:, 0:ng * 128].rearrange("p (j t) -> p j t", t=128), PTsb[b][:, j0:j0 + ng, :])]
                else:
                    parts = [(pPT[pb][:cn, jj * 128:(jj + 1) * 128], PTsb[b][:cn, j0 + jj, :]) for jj, (_, cn) in enumerate(grp)]
                for (srcv, dstv) in parts:
                    if eng == "act":
                        P.op("act", lambda e, srcv=srcv, dstv=dstv: e.activation(out=dstv, in_=srcv, func=AF.Copy), r=[("pPT", pb)], w=[kPT])
                    else:
                        P.op("dve", lambda e, srcv=srcv, dstv=dstv: e.tensor_copy(out=dstv, in_=srcv), r=[("pPT", pb)], w=[kPT])
            po = pO[br]
            for j, (c0, cn) in enumerate(tiles):
                if mode == "cmp":
                    rhs = V[:cn, j, g * 64:(g + 1) * 64]
                else:
                    rhs = V[:cn, tile_lo + j, g * 64:(g + 1) * 64]
                P.op("pe", lambda e, j=j, cn=cn, rhs=rhs: e.matmul(po[:, h * 64:(h + 1) * 64], lhsT=PTsb[b][:cn, j, :], rhs=rhs, start=(j == 0), stop=(j == len(tiles) - 1)),
                     r=[kPT, vkey], w=[("pO", br)])
            P.op("dve", lambda e: e.tensor_tensor(out=gs[m4][:, :], in0=rsum[m4][:, :], in1=sg[qs][:, 3 * h + br:3 * h + br + 1], op=ALU.mult),
                 r=[("rsum", m4), ("sg", qs)], w=[("gs", m4)])
            if mode == "cmp" and i == 0:
                P.op("dve", lambda e: e.tensor_tensor(out=gs[m4][:, :], in0=gs[m4][:, :], in1=cvalid, op=ALU.mult), r=[("gs", m4), "c2"], w=[("gs", m4)])
            ko = ("onsa", qs)
            if first:
                P.op("dve", lambda e: e.tensor_scalar(out=onsa[qs][:, h * 64:(h + 1) * 64], in0=po[:, h * 64:(h + 1) * 64], scalar1=gs[m4][:, 0:1], scalar2=None, op0=ALU.mult),
                     r=[("pO", br), ("gs", m4)], w=[ko])
            else:
                P.op("dve", lambda e: e.scalar_tensor_tensor(out=onsa[qs][:, h * 64:(h + 1) * 64], in0=po[:, h * 64:(h + 1) * 64], scalar=gs[m4][:, 0:1],
                                                             in1=onsa[qs][:, h * 64:(h + 1) * 64], op0=ALU.mult, op1=ALU.add), r=[("pO", br), ("gs", m4), ko], w=[ko])

        for i in range(NQT):
            qs = i % 2
            r0 = i * 128
            P.dma("sp", qt[qs][:, :], projd[r0:r0 + 128, O_Q:O_Q + 512], w=[("qt", qs)])
            P.dma("sp", gt_[qs][:, :], projd[r0:r0 + 128, O_GATE:O_GATE + 24], w=[("gate", qs)])
            P.op("act", lambda e, qs=qs: e.activation(out=sg[qs][:, :], in_=gt_[qs][:, :], func=AF.Sigmoid), r=[("gate", qs)], w=[("sg", qs)])
            if i >= 8:
                P.dma("sp", seltab[qs][:, :], seltab_d[i, :, :], w=[("seltab", qs)])
            wins = [(0, [(0, 0)]), (64, [(1, 0)]), (128, [(2, 0)]), (192, [(3, 0), (4, 1)]), (256, [(5, 1)]), (320, [(6, 1)]), (384, [(7, 1)])]
            for batch in (wins[0:4], wins[4:7]):
                for slot, (c0, heads) in enumerate(batch):
                    P.op("pe", lambda e, qs=qs, c0=c0, slot=slot: e.transpose(out=pT[:, slot * 128:(slot + 1) * 128], in_=qt[qs][:, c0:c0 + 128], identity=ident),
                         r=[("qt", qs), "cst"], w=["pT"])
                for slot, (c0, heads) in enumerate(batch):
                    for (h, g) in heads:
                        P.op("act", lambda e, qs=qs, h=h, g=g, slot=slot: e.activation(out=qTh[qs][:, h, :], in_=pT[:, slot * 128:(slot + 1) * 128], func=AF.Copy, scale=rm8[:, g:g + 1]),
                             r=["pT", "rm8"], w=[("qTh", qs)])
            P.op("dve", lambda e, qs=qs, r0=r0: e.tensor_scalar(out=bias_c[qs][:, :], in0=cval, scalar1=float(r0), scalar2=0.0, op0=ALU.add, op1=ALU.is_ge),
                 r=["c2"], w=[("biasc", qs)])
            P.op("dve", lambda e, qs=qs: e.tensor_scalar(out=bias_c[qs][:, :], in0=bias_c[qs][:, :], scalar1=-1.0, scalar2=-NEG, op0=ALU.add, op1=ALU.mult),
                 r=[("biasc", qs)], w=[("biasc", qs)])
            for h in range(8):
                attend(i, h, 0, qs, kcT, "kcT", vc, "vc", 0, 2, "cmp", True)
            if i >= 8:
                for g in range(2):
                    kpa = ("pacc", g)
                    P.op("dve", lambda e, g=g: e.tensor_tensor(out=imp[:, :], in0=pacc[g][:, 0:253:4], in1=pacc[g][:, 1:254:4], op=ALU.add), r=[kpa], w=["imp"])
                    for o in (2, 3, 4):
                        P.op("dve", lambda e, g=g, o=o: e.tensor_tensor(out=imp[:, :], in0=imp[:, :], in1=pacc[g][:, o:o + 253:4], op=ALU.add), r=[kpa, "imp"], w=["imp"])
                    P.op("dve", lambda e, qs=qs: e.tensor_tensor(out=cand[:, :], in0=imp[:, :], in1=seltab[qs][:, 0:64], op=ALU.mult), r=["imp", ("seltab", qs)], w=["cand"])
                    P.op("dve", lambda e, qs=qs: e.scalar_tensor_tensor(out=cand[:, :], in0=seltab[qs][:, 0:64], scalar=-1.0, in1=cand[:, :], op0=ALU.add, op1=ALU.add),
                         r=["cand", ("seltab", qs)], w=["cand"])
                    P.op("dve", lambda e: e.max(out=m8[:, 0:8], in_=cand[:, :]), r=["cand"], w=["m8"])
                    P.op("dve", lambda e: e.match_replace(out=cand2[:, :], in_to_replace=m8[:, 0:8], in_values=cand[:, :], imm_value=-2.0), r=["cand", "m8"], w=["cand2"])
                    P.op("dve", lambda e: e.max(out=m8[:, 8:16], in_=cand2[:, :]), r=["cand2"], w=["m8"])
                    P.op("dve", lambda e, g=g: e.tensor_scalar(out=selb[g][:, :], in0=cand[:, :], scalar1=m8[:, 12:13], scalar2=None, op0=ALU.is_ge), r=["cand", "m8"], w=[("selb", g)])
                    P.op("dve", lambda e, g=g, qs=qs: e.tensor_tensor(out=selb[g][:, :], in0=selb[g][:, :], in1=seltab[qs][:, 64:128], op=ALU.max), r=[("selb", g), ("seltab", qs)], w=[("selb", g)])
                    P.op("dve", lambda e, g=g: e.tensor_scalar(out=selb[g][:, :], in0=selb[g][:, :], scalar1=-1.0, scalar2=-NEG, op0=ALU.add, op1=ALU.mult), r=[("selb", g)], w=[("selb", g)])
            for h in range(8):
                attend(i, h, 1, qs, kTs, "kTs", Vs, "Vs", 0, i + 1, "sel", False)
            lo = max(0, i - 4)
            for h in range(8):
                attend(i, h, 2, qs, kTw, "kTw", Vw, "Vw", lo, i - lo + 1, "win", False)
            P.dma("sp", mixd[r0:r0 + 128, 512:1024], onsa[qs][:, :], r=[("onsa", qs)])
        P.emit_phase()


def nsa_sample_phase(kk, P, projd, kv_out, mixd, cw, consts_d, consts3_d, caches, ckw, cvw, page_table):
    nc = kk.nc
    NS = int(os.environ.get("NSA_NSEQ", str(NSEQ_S)))
    NPG = 64
    NK = NPG * 128 + 4
    with ExitStack() as st:
        def sbt(n, s, d=F32):
            return st.enter_context(nc.sbuf_tensor("ns_" + n, s, d))
        cst = sbt("cst", [128, 514]); ident = cst[:, 0:128]
        c3 = sbt("c3", [128, 1024])
        Amat, A2, bias4, wbias, candm, forced = c3[0:64, 0:8], c3[0:8, 8:72], c3[0:64, 72:76], c3[0:64, 76:592], c3[0:8, 592:720], c3[0:8, 720:848]
        identb = sbt("identb", [128, 128], BF16)
        idx = sbt("idx", [128, NSEQ_S * NPG], I32); ptb = idx; ptf = sbt("ptf", [128, NSEQ_S * NPG])
        kTs = sbt("kTs", [128, NK + 4], BF16); Vs = sbt("Vs", [128, NPG + 1, 128], BF16)
        kTc = sbt("kTc", [128, NPG * 128], BF16)
        W1g = [[sbt("W1g%d%d" % (x, g), [128, 32, 128], BF16) for g in range(2)] for x in range(2)]
        W1 = [sbt("W1%d" % x, [128, 16, 128], BF16) for x in range(2)]
        pec = [sbt("pec%d" % x, [128, 16], BF16) for x in range(2)]
        perow = [sbt("perow%d" % x, [16, 128]) for x in range(2)]
        w2p = [[sbt("w2p%d%d" % (x, g), [128, 128], BF16) for g in range(2)] for x in range(2)]
        peterm = [sbt("peterm%d" % x, [128, 1]) for x in range(2)]
        gl = [[sbt("gl%d%d" % (x, g), [128, 512], BF16) for g in range(2)] for x in range(2)]
        gx = sbt("gx", [128, 512]); gu = sbt("gu", [128, 512])
        kcT = sbt("kcT", [128, 512], BF16); vc = sbt("vc", [128, 4, 128], BF16)
        pg = [sbt("pg%d" % i, [128, 4, 128]) for i in range(4)]
        newT = sbt("newT", [128, 2, 64], BF16)
        stile = sbt("stile", [64, 768])
        Q64 = sbt("Q64", [64, NSEQ_S, 128]); G64 = sbt("G64", [64, NSEQ_S, 3]); SG64 = sbt("SG64", [64, NSEQ_S, 3])
        qT64 = sbt("qT64", [128, 64], BF16)
        kTw = sbt("kTw", [128, 520], BF16); Vw = sbt("Vw", [128, 5, 128], BF16)
        wtile = sbt("wtile", [128, 4, 256])
        Ssb = sbt("Ssb", [64, NK + 4]); Pbf = sbt("Pbf", [64, NK + 4], BF16); PTsb = sbt("PTsb", [128, NPG + 1, 64], BF16)
        Pf = sbt("Pf", [64, 512]); pacc8 = sbt("pacc8", [8, 520]); imp = sbt("imp", [8, 128]); cand = sbt("cand", [8, 128]); cand2 = sbt("cand2", [8, 128])
        m8 = sbt("m8", [8, 16]); selb8 = sbt("selb8", [8, 132]); selb64 = sbt("selb64", [64, 132])
        mx = sbt("mx", [64, 1]); rsum = sbt("rsum", [64, 1]); gs = sbt("gs", [64, 1]); osum = sbt("osum", [64, 64])
        pS = [st.enter_context(nc.psum_tensor("ns_pS%d" % i, [128, 512], F32)) for i in range(2)]
        pPT = [st.enter_context(nc.psum_tensor("ns_pPT%d" % i, [128, 1024], BF16)) for i in range(2)]
        pO = st.enter_context(nc.psum_tensor("ns_pO", [128, 512], F32))
        pT = [st.enter_context(nc.psum_tensor("ns_pT%d" % i, [128, 512], F32)) for i in range(2)]
        pI = st.enter_context(nc.psum_tensor("ns_pI", [128, 512], F32))

        P.dma("sp", cst[:, :], consts_d[:, 0:514], w=["cst"])
        P.dma("sp", c3[:, :], consts3_d[:, :], w=["c3"])
        P.dma("sp", ptb[:, :], page_table.rearrange("s (o j) -> o (s j)", o=1).partition_broadcast(128), w=["idx"])
        P.op("dve", lambda e: e.tensor_copy(out=ptf[:, :], in_=ptb[:, :]), r=["idx"], w=["ptf"])
        P.op("dve", lambda e: e.tensor_scalar(out=ptf[:, :], in0=ptf[:, :], scalar1=128.0, scalar2=c3[:, 848:849], op0=ALU.mult, op1=ALU.add), r=["ptf", "c3"], w=["ptf"])
        P.op("dve", lambda e: e.tensor_copy(out=idx[:, :], in_=ptf[:, :]), r=["ptf"], w=["idx"])
        P.op("dve", lambda e: e.tensor_copy(out=identb[:, :], in_=ident), r=["cst"], w=["identb"])
        for x, pre in enumerate(("cmp_k_", "cmp_v_")):
            w1 = cw[pre + "w1"]
            for j in range(16):
                P.dma("pool", W1[x][:, j, :], w1[j * 128:(j + 1) * 128, :], w=[("W1", x)])
            for g in range(2):
                P.op("pool", lambda e, x=x, g=g: e.memset(W1g[x][g][:, :, :], 0.0), w=[("W1g", x, g)])
                P.dma("pool", W1g[x][g][g * 64:(g + 1) * 64, :, :], w1.rearrange("(s d) h -> d s h", d=64), w=[("W1g", x, g)])
                P.op("pool", lambda e, x=x, g=g: e.memset(w2p[x][g][:, :], 0.0), w=[("w2p", x, g)])
                P.dma("pool", w2p[x][g][:, g * 64:(g + 1) * 64], cw[pre + "w2"][:, :], w=[("w2p", x, g)])
            P.dma("sp", perow[x][:, :], cw[pre + "pe"].rearrange("(j p) o -> j (p o)", p=128), w=[("perow", x)])
            P.op("pe", lambda e, x=x: e.transpose(out=pT[0][:, 0:16], in_=perow[x][:, :], identity=cst[0:16, 0:16]), r=[("perow", x), "cst"], w=[("pT", 0)])
            P.op("act", lambda e, x=x: e.activation(out=pec[x][:, :], in_=pT[0][:, 0:16], func=AF.Copy), r=[("pT", 0)], w=[("pec", x)])
            for j in range(16):
                P.op("pe", lambda e, x=x, j=j: e.matmul(pO[:, 0:1], lhsT=W1[x][:, j, :], rhs=pec[x][:, j:j + 1], start=(j == 0), stop=(j == 15)),
                     r=[("W1", x), ("pec", x)], w=["pO"])
            P.op("dve", lambda e, x=x: e.tensor_copy(out=peterm[x][:, :], in_=pO[:, 0:1]), r=["pO"], w=[("peterm", x)])
        P.op("pool", lambda e: e.memset(pacc8[:, :], 0.0), w=["pacc8"])
        P.op("pool", lambda e: e.memset(selb8[:, :], 0.0), w=["selb8"])
        P.dma("sp", stile[:, :], kv_out[SEQ:NTOK, :], w=["stile"])
        P.op("pe", lambda e: e.transpose(out=pT[0][:, 0:64], in_=stile[:, 256:384], identity=cst[0:64, 0:64]), r=["stile", "cst"], w=[("pT", 0)])
        P.op("pe", lambda e: e.transpose(out=pT[0][:, 64:128], in_=stile[:, 512:640], identity=cst[0:64, 0:64]), r=["stile", "cst"], w=[("pT", 0)])
        P.op("act", lambda e: e.activation(out=newT[:, :, :].rearrange("p a c -> p (a c)"), in_=pT[0][:, 0:128], func=AF.Copy), r=[("pT", 0)], w=["newT"])
        P.op("pool", lambda e: e.memset(Q64[:, :, :], 0.0), w=["Q64"])
        P.op("pool", lambda e: e.memset(G64[:, :, :], 0.0), w=["G64"])
        for g in range(2):
            for r in range(4):
                h = 4 * g + r
                row0 = g * 32 + r * 4
                srcq = AP(projd.tensor, projd.offset + SEQ * DIN + O_Q + h * 64, [[16 * DIN, 4], [DIN, NSEQ_S], [1, 64]])
                P.dma("sp", Q64[row0:row0 + 4, :, g * 64:(g + 1) * 64], srcq, w=["Q64"])
                srcg = AP(projd.tensor, projd.offset + SEQ * DIN + O_GATE + 3 * h, [[16 * DIN, 4], [DIN, NSEQ_S], [1, 3]])
                P.dma("sp", G64[row0:row0 + 4, :, :], srcg, w=["G64"])
        P.op("act", lambda e: e.activation(out=SG64[:, :, :], in_=G64[:, :, :], func=AF.Sigmoid), r=["G64"], w=["SG64"])

        cnt = {"g": 0}

        def gather(seq, cache, grp):
            slot = cnt["g"] % 4
            cnt["g"] += 1
            rows = cache.rearrange("n t f -> (n t) f")
            for a_ in range(4):
                j = seq * NPG + grp * 4 + a_
                P.op("pool", lambda e, slot=slot, a_=a_, j=j, rows=rows: e.indirect_dma_start(
                    out=pg[slot][:, a_, :], out_offset=None, in_=rows, in_offset=bass.IndirectOffsetOnAxis(ap=idx[:, j:j + 1], axis=0)),
                    r=["idx"], w=[("pg", slot, a_)], dma=True)
            return slot, 0

        def wait_pages(eng, slot, need):
            return None

        def attend_s(seq, br, kT, kkey, nk, Vfn, vkey, bias_fn, first):
            nch = (nk + 511) // 512
            for ch in range(nch):
                k0 = ch * 512
                w = min(512, nk - k0)
                pb = ch % 2
                P.op("pe", lambda e, pb=pb, w=w, k0=k0: e.matmul(pS[pb][0:64, 0:w], lhsT=qT64[:, :], rhs=kT[:, k0:k0 + w], start=True, stop=True),
                     r=["qT64", kkey], w=[("pS", pb)])
                bias_fn(ch, k0, w, pb)
            P.op("dve", lambda e: e.tensor_reduce(out=mx[:, :], in_=Ssb[:, 0:nk], axis=AX.X, op=ALU.max, negate=True), r=["Ssb"], w=["mx"])
            P.op("dve", lambda e: e.memset(rsum[:, :], 0.0), w=["rsum"])
            if br == 0:
                P.op("act", lambda e: e.activation(out=Pf[:, 0:nk], in_=Ssb[:, 0:nk], func=AF.Exp, bias=mx[:, 0:1], scale=1.0, accum_out=rsum[:, :]),
                     r=["Ssb", "mx", "rsum"], w=["Pf", "rsum"])
                P.op("dve", lambda e: e.tensor_copy(out=Pbf[:, 0:nk], in_=Pf[:, 0:nk]), r=["Pf"], w=["Pbf"])
            else:
                P.op("act", lambda e: e.activation(out=Pbf[:, 0:nk], in_=Ssb[:, 0:nk], func=AF.Exp, bias=mx[:, 0:1], scale=1.0, accum_out=rsum[:, :]),
                     r=["Ssb", "mx", "rsum"], w=["Pbf", "rsum"])
            P.op("dve", lambda e: e.reciprocal(out=rsum[:, :], in_=rsum[:, :]), r=["rsum"], w=["rsum"])
            tiles = [(j * 128, min(128, nk - j * 128)) for j in range((nk + 127) // 128)]
            for j0 in range(0, len(tiles), 16):
                grp = tiles[j0:j0 + 16]
                pb = (j0 // 16) % 2
                for jj, (c0, cn) in enumerate(grp):
                    P.op("pe", lambda e, pb=pb, jj=jj, c0=c0, cn=cn: e.transpose(out=pPT[pb][:cn, jj * 64:(jj + 1) * 64], in_=Pbf[:, c0:c0 + cn], identity=identb[0:64, 0:64]),
                         r=["Pbf", "identb"], w=[("pPT", pb)])
                if all(cn == 128 for _, cn in grp):
                    parts = [(pPT[pb][:, 0:len(grp) * 64].rearrange("p (j t) -> p j t", t=64), PTsb[:, j0:j0 + len(grp), :])]
                else:
                    parts = [(pPT[pb][:cn, jj * 64:(jj + 1) * 64], PTsb[:cn, j0 + jj, :]) for jj, (_, cn) in enumerate(grp)]
                eng = "act" if (j0 // 16) % 2 == 0 else "dve"
                for (srcv, dstv) in parts:
                    if eng == "act":
                        P.op("act", lambda e, srcv=srcv, dstv=dstv: e.activation(out=dstv, in_=srcv, func=AF.Copy), r=[("pPT", pb)], w=["PTsb"])
                    else:
                        P.op("dve", lambda e, srcv=srcv, dstv=dstv: e.tensor_copy(out=dstv, in_=srcv), r=[("pPT", pb)], w=["PTsb"])
            for j, (c0, cn) in enumerate(tiles):
                P.op("pe", lambda e, j=j, cn=cn: e.matmul(pO[0:64, 0:128], lhsT=PTsb[:cn, j, :], rhs=Vfn(j, cn), start=(j == 0), stop=(j == len(tiles) - 1)),
                     r=["PTsb", vkey], w=["pO"])
            P.op("dve", lambda e: e.tensor_tensor(out=gs[:, :], in0=rsum[:, :], in1=SG64[:, seq, br:br + 1], op=ALU.mult), r=["rsum", "SG64"], w=["gs"])
            for g in range(2):
                rows = slice(g * 32, g * 32 + 16)
                if first:
                    P.op("dve", lambda e, rows=rows, g=g: e.tensor_scalar(out=osum[rows, :], in0=pO[rows, g * 64:(g + 1) * 64], scalar1=gs[rows, 0:1], scalar2=None, op0=ALU.mult),
                         r=["pO", "gs"], w=["osum"])
                else:
                    P.op("dve", lambda e, rows=rows, g=g: e.scalar_tensor_tensor(out=osum[rows, :], in0=pO[rows, g * 64:(g + 1) * 64], scalar=gs[rows, 0:1], in1=osum[rows, :],
                                                                                 op0=ALU.mult, op1=ALU.add), r=["pO", "gs", "osum"], w=["osum"])

        for seq in range(NS):
            P.op("pe", lambda e, seq=seq: e.transpose(out=pT[1][:, 0:64], in_=Q64[:, seq, :], identity=cst[0:64, 0:64]), r=["Q64", "cst"], w=[("pT", 1)])
            P.op("act", lambda e: e.activation(out=qT64[:, :], in_=pT[1][:, 0:64], func=AF.Copy, scale=0.125), r=[("pT", 1)], w=["qT64"])
            def gather_cache(cache, kind):
                for grp in range(NPG // 4):
                    slot, need = gather(seq, cache, grp)
                    if kind == "vs":
                        wait_pages("pool", slot, need)
                        P.op("act", lambda e, slot=slot, grp=grp: e.activation(out=Vs[:, grp * 4:(grp + 1) * 4, :], in_=pg[slot][:, :, :], func=AF.Copy),
                             r=[("pg", slot, a_) for a_ in range(4)], w=["Vs"])
                    else:
                        wait_pages("pe", slot, need)
                        tb = grp % 2
                        for a in range(4):
                            P.op("pe", lambda e, slot=slot, a=a, tb=tb: e.transpose(out=pT[tb][:, a * 128:(a + 1) * 128], in_=pg[slot][:, a, :], identity=ident),
                                 r=[("pg", slot, a), "cst"], w=[("pT", tb)])
                        if kind == "ks":
                            dst = kTs[:, grp * 512:(grp + 1) * 512]
                            dkey = "kTs"
                        else:
                            dst = kTc[:, grp * 512:(grp + 1) * 512]
                            dkey = "kTc"
                        if grp % 2 == 0:
                            P.op("act", lambda e, dst=dst, tb=tb: e.activation(out=dst, in_=pT[tb][:, :], func=AF.Copy), r=[("pT", tb)], w=[dkey])
                        else:
                            P.op("dve", lambda e, dst=dst, tb=tb: e.tensor_copy(out=dst, in_=pT[tb][:, :]), r=[("pT", tb)], w=[dkey])
            gather_cache(caches[2], "ks")
            gather_cache(caches[3], "vs")
            P.op("act", lambda e, seq=seq: e.activation(out=kTs[:, NPG * 128:NPG * 128 + 4], in_=newT[:, 0, seq:64:16], func=AF.Copy), r=["newT"], w=["kTs"])
            P.dma("pool", Vs[0:4, NPG, :], kv_out[SEQ + seq:NTOK:16, 384:512], w=["Vs"])
            P.dma("sp", wtile[:, :, 0:128], AP(ckw.tensor, ckw.offset + seq * 512 * 128, [[128, 128], [128 * 128, 4], [1, 128]]), w=["wtileK"])
            for a in range(4):
                P.op("pe", lambda e, a=a: e.transpose(out=pT[0][:, a * 128:(a + 1) * 128], in_=wtile[:, a, 0:128], identity=ident), r=["wtileK", "cst"], w=[("pT", 0)])
            P.op("act", lambda e: e.activation(out=kTw[:, 0:512], in_=pT[0][:, :], func=AF.Copy), r=[("pT", 0)], w=["kTw"])
            P.op("act", lambda e, seq=seq: e.activation(out=kTw[:, 512:516], in_=newT[:, 1, seq:64:16], func=AF.Copy), r=["newT"], w=["kTw"])
            P.dma("pool", Vw[:, 0:4, :], AP(cvw.tensor, cvw.offset + seq * 512 * 128, [[128, 128], [128 * 128, 4], [1, 128]]), w=["Vw"])
            P.dma("pool", Vw[0:4, 4, :], kv_out[SEQ + seq:NTOK:16, 640:768], w=["Vw"])
            for x in range(2):
                gather_cache(caches[x], "kc")
                for g in range(2):
                    b = g % 2
                    for sp_ in range(32):
                        P.op("pe", lambda e, x=x, g=g, sp_=sp_, b=b: e.matmul(pS[b][:, 0:511], lhsT=W1g[x][g][:, sp_, :], rhs=kTc[:, sp_:sp_ + 16 * 510 + 1:16],
                                                                              start=(sp_ == 0), stop=(sp_ == 31)), r=[("W1g", x, g), "kTc"], w=[("pS", b)])
                    P.op("act", lambda e, x=x, b=b: e.activation(out=gx[:, 0:511], in_=pS[b][:, 0:511], func=AF.Identity, bias=peterm[x][:, 0:1], scale=1.0),
                         r=[("pS", b), ("peterm", x)], w=["gx"])
                    P.op("dve", lambda e: e.tensor_tensor(out=gu[:, 0:511], in0=gx[:, 0:511], in1=gx[:, 0:511], op=ALU.mult), r=["gx"], w=["gu"])
                    P.op("dve", lambda e: e.tensor_scalar(out=gu[:, 0:511], in0=gu[:, 0:511], scalar1=GELU_C, scalar2=1.0, op0=ALU.mult, op1=ALU.add), r=["gu"], w=["gu"])
                    P.op("dve", lambda e: e.tensor_tensor(out=gu[:, 0:511], in0=gu[:, 0:511], in1=gx[:, 0:511], op=ALU.mult), r=["gu", "gx"], w=["gu"])
                    P.op("act", lambda e: e.activation(out=gu[:, 0:511], in_=gu[:, 0:511], func=AF.Sigmoid, scale=GELU_S), r=["gu"], w=["gu"])
                    P.op("dve", lambda e, x=x, g=g: e.tensor_tensor(out=gl[x][g][:, 0:511], in0=gu[:, 0:511], in1=gx[:, 0:511], op=ALU.mult), r=["gu", "gx"], w=[("gl", x, g)])
            for g in range(2):
                P.op("pe", lambda e, g=g: e.matmul(pS[0][:, 0:511], lhsT=w2p[0][g][:, :], rhs=gl[0][g][:, 0:511], start=(g == 0), stop=(g == 1)),
                     r=[("w2p", 0, g), ("gl", 0, g)], w=[("pS", 0)])
            P.op("act", lambda e: e.activation(out=kcT[:, 0:511], in_=pS[0][:, 0:511], func=AF.Copy), r=[("pS", 0)], w=["kcT"])
            for ct in range(4):
                c0 = ct * 128
                cn = min(128, 511 - c0)
                for g in range(2):
                    P.op("pe", lambda e, g=g, c0=c0, cn=cn, ct=ct: e.matmul(pS[1][:cn, ct * 128:(ct + 1) * 128], lhsT=gl[1][g][:, c0:c0 + cn], rhs=w2p[1][g][:, :],
                                                                             start=(g == 0), stop=(g == 1)), r=[("w2p", 1, g), ("gl", 1, g)], w=[("pS", 1)])
                P.op("act", lambda e, cn=cn, ct=ct: e.activation(out=vc[:cn, ct, :], in_=pS[1][:cn, ct * 128:(ct + 1) * 128], func=AF.Copy), r=[("pS", 1)], w=["vc"])
            def bias_cmp(ch, k0, w, pb):
                P.op("act", lambda e: e.activation(out=Ssb[:, k0:k0 + w], in_=pS[pb][0:64, 0:w], func=AF.Copy), r=[("pS", pb)], w=["Ssb"])
            attend_s(seq, 0, kcT, "kcT", 511, lambda j, cn: vc[:cn, j, :], "vc", bias_cmp, True)
            P.op("dve", lambda e: e.tensor_scalar(out=Pf[:, 0:511], in0=Pf[:, 0:511], scalar1=rsum[:, 0:1], scalar2=None, op0=ALU.mult), r=["Pf", "rsum"], w=["Pf"])
            P.op("pe", lambda e: e.matmul(pI[0:8, 0:511], lhsT=Amat, rhs=Pf[:, 0:511], start=True, stop=True), r=["Pf", "c3"], w=["pI"])
            P.op("dve", lambda e: e.tensor_copy(out=pacc8[:, 1:512], in_=pI[0:8, 0:511]), r=["pI"], w=["pacc8"])
            P.op("dve", lambda e: e.tensor_tensor(out=imp[:, :], in0=pacc8[:, 0:509:4], in1=pacc8[:, 1:510:4], op=ALU.add), r=["pacc8"], w=["imp"])
            for o in (2, 3, 4):
                P.op("dve", lambda e, o=o: e.tensor_tensor(out=imp[:, :], in0=imp[:, :], in1=pacc8[:, o:o + 509:4], op=ALU.add), r=["pacc8", "imp"], w=["imp"])
            P.op("dve", lambda e: e.tensor_tensor(out=cand[:, :], in0=imp[:, :], in1=candm, op=ALU.mult), r=["imp", "c3"], w=["cand"])
            P.op("dve", lambda e: e.scalar_tensor_tensor(out=cand[:, :], in0=candm, scalar=-1.0, in1=cand[:, :], op0=ALU.add, op1=ALU.add), r=["cand", "c3"], w=["cand"])
            P.op("dve", lambda e: e.max(out=m8[:, 0:8], in_=cand[:, :]), r=["cand"], w=["m8"])
            P.op("dve", lambda e: e.match_replace(out=cand2[:, :], in_to_replace=m8[:, 0:8], in_values=cand[:, :], imm_value=-2.0), r=["cand", "m8"], w=["cand2"])
            P.op("dve", lambda e: e.max(out=m8[:, 8:16], in_=cand2[:, :]), r=["cand2"], w=["m8"])
            P.op("dve", lambda e: e.tensor_scalar(out=selb8[:, 0:128], in0=cand[:, :], scalar1=m8[:, 12:13], scalar2=None, op0=ALU.is_ge), r=["cand", "m8"], w=["selb8"])
            P.op("dve", lambda e: e.tensor_tensor(out=selb8[:, 0:128], in0=selb8[:, 0:128], in1=forced, op=ALU.max), r=["selb8", "c3"], w=["selb8"])
            P.op("dve", lambda e: e.tensor_scalar(out=selb8[:, 0:128], in0=selb8[:, 0:128], scalar1=-1.0, scalar2=-NEG, op0=ALU.add, op1=ALU.mult), r=["selb8"], w=["selb8"])
            P.op("pe", lambda e: e.matmul(pI[0:64, 0:132], lhsT=A2, rhs=selb8[:, 0:132], start=True, stop=True), r=["selb8", "c3"], w=["pI"])
            P.op("dve", lambda e: e.tensor_copy(out=selb64[:, :], in_=pI[0:64, 0:132]), r=["pI"], w=["selb64"])
            def bias_sel(ch, k0, w, pb):
                if w == 512:
                    P.op("dve", lambda e: e.tensor_tensor(out=Ssb[:, k0:k0 + 512].rearrange("p (n d) -> p n d", d=64), in0=pS[pb][0:64, 0:512].rearrange("p (n d) -> p n d", d=64),
                                                          in1=bcast(selb64[:, ch * 8:ch * 8 + 8], [[1, 8], [0, 64]]), op=ALU.add), r=[("pS", pb), "selb64"], w=["Ssb"])
                else:
                    P.op("dve", lambda e: e.tensor_tensor(out=Ssb[:, k0:k0 + w], in0=pS[pb][0:64, 0:w], in1=bias4, op=ALU.add), r=[("pS", pb), "c3"], w=["Ssb"])
            attend_s(seq, 1, kTs, "kTs", NK, lambda j, cn: Vs[:cn, j, :], "Vs", bias_sel, False)
            def bias_win(ch, k0, w, pb):
                P.op("dve", lambda e: e.tensor_tensor(out=Ssb[:, k0:k0 + w], in0=pS[pb][0:64, 0:w], in1=wbias[:, k0:k0 + w], op=ALU.add), r=[("pS", pb), "c3"], w=["Ssb"])
            attend_s(seq, 2, kTw, "kTw", 516, lambda j, cn: Vw[:cn, j, :], "Vw", bias_win, False)
            for g in range(2):
                for r in range(4):
                    h = 4 * g + r
                    row0 = g * 32 + r * 4
                    dst = AP(mixd.tensor, mixd.offset + (SEQ + seq) * D + 512 + h * 64, [[16 * D, 4], [1, 64]])
                    P.dma("sp", dst, osum[row0:row0 + 4, :], r=["osum"])
        P.emit_phase()


def build_program():
    kk = K()
    nc = kk.nc
    x = kk.din("x", [NTOK, D])
    consts_d = kk.din("consts", [128, 514])
    ident_d = consts_d[:, 0:128]
    state_ssm = kk.din("state_ssm", [NSEQ_S, 8, 64, 64])
    state_conv = kk.din("state_conv", [NSEQ_S, 3, 768])
    conv_w = kk.din("conv_w", [4, 768])
    conv_b = kk.din("conv_b", [1, 768])
    dt_bias = kk.din("dt_bias", [1, 8])
    a_log = kk.din("a_log", [1, 8])
    d_skip = kk.din("d_skip", [1, 8])
    ssm_norm_w = kk.din("ssm_norm_w", [1, 512])
    consts2_d = kk.din("consts2", [128, 1024])
    seltab_d = kk.din("seltab", [32, 128, 128])
    consts3_d = kk.din("consts3", [128, 1024])
    page_table = kk.din("page_table", [NSEQ_S, 64], I32)
    caches = [kk.din(n, [N_POOL, 128, 128]) for n in ("cache_k_cmp", "cache_v_cmp", "cache_k_slc", "cache_v_slc")]
    cw = {}
    for pre in ("cmp_k_", "cmp_v_"):
        cw[pre + "w1"] = kk.din(pre + "w1", [2048, 128])
        cw[pre + "w2"] = kk.din(pre + "w2", [128, 64])
        cw[pre + "pe"] = kk.din(pre + "pe", [2048, 1])
    cos_d = kk.din("cos", [NTOK, 32])
    sin_d = kk.din("sin", [NTOK, 32])
    w_in = kk.din("w_in", [D, DIN])
    b_gate = kk.din("b_gate", [1, 24])
    f1g = kk.din("ffn1_w_gate", [D, DFF])
    f1u = kk.din("ffn1_w_up", [D, DFF])
    f1d = kk.din("ffn1_w_down", [DFF, D])
    f2g = kk.din("ffn2_w_gate", [D, DFF])
    f2u = kk.din("ffn2_w_up", [D, DFF])
    f2d = kk.din("ffn2_w_down", [DFF, D])
    w_out = kk.din("w_out", [D, D])
    ln2g = kk.din("ln2_g", [1, D]); ln2b = kk.din("ln2_b", [1, D])
    ln3g = kk.din("ln3_g", [1, D]); ln3b = kk.din("ln3_b", [1, D])
    ln1g = kk.din("ln1_g", [1, D])
    ln1b = kk.din("ln1_b", [1, D])
    ckw = kk.din("cache_k_win", [NSEQ_S, 512, 128])
    cvw = kk.din("cache_v_win", [NSEQ_S, 512, 128])

    kv_out = kk.dout("kv_out", [NTOK, 768])
    p_win = kk.dout("p_win", [2, 512, 128])
    s_win = kk.dout("s_win", [2, NSEQ_S, 512, 128])
    p_conv = kk.dout("p_conv", [3, 768])
    s_conv = kk.dout("s_conv", [NSEQ_S, 3, 768])
    p_ssm = kk.dout("p_ssm", [8, 64, 64])
    s_ssm = kk.dout("s_ssm", [NSEQ_S, 8, 64, 64])
    mixd = kk.dscr("mixd", [NTOK, D])
    y_out = kk.dout("y_out", [NTOK, D])
    h2 = kk.dscr("h2", [NTOK, D])
    h1 = kk.dscr("h1", [NTOK, D])
    projd = kk.dscr("projd", [NTOK, DIN])

    with ExitStack() as st:
        P = Prog(nc, st)
        import os
        PH = os.environ.get("KPHASES", "ffn1,inproj,ssd,nsa,nsas,outproj,ffn2,tail").split(",")
        if "ffn1" in PH:
            ffn_phase(kk, P, "f1_", f1g, f1u, f1d, ln1g, ln1b, x, h1, ident_d)
        if "inproj" in PH:
            inproj_phase(kk, P, h1, projd, kv_out, w_in, b_gate, cos_d, sin_d, ident_d)
        if "ssd" in PH:
            ssd_phase(kk, P, projd, mixd, p_ssm, s_ssm, state_ssm, state_conv, conv_w, conv_b, dt_bias, a_log, d_skip, ssm_norm_w, consts_d)
        if "nsa" in PH:
            nsa_prompt_phase(kk, P, projd, kv_out, mixd, cw, consts_d, consts2_d, seltab_d)
        if "nsas" in PH:
            nsa_sample_phase(kk, P, projd, kv_out, mixd, cw, consts_d, consts3_d, caches, ckw, cvw, page_table)
        if "outproj" in PH:
            outproj_phase(kk, P, mixd, h1, h2, w_out, ln2g, ln2b, ident_d)
        if "ffn2" in PH:
            ffn_phase(kk, P, "f2_", f2g, f2u, f2d, ln3g, ln3b, h2, y_out, ident_d)
        for i in range(2):
            c0 = 512 + 128 * i
            P.dma("sp", p_win[i, :, :], kv_out[SEQ - 512:SEQ, c0:c0 + 128])
            cw = (ckw, cvw)[i]
            P.dma("act", s_win[i, :, 0:508, :], cw[:, 4:512, :])
            P.dma("sp", s_win[i, :, 508:512, :], kv_out[SEQ:NTOK, c0:c0 + 128].rearrange("(t s) f -> s t f", t=TS))
        P.dma("sp", p_conv[:, :], projd[SEQ - 3:SEQ, O_XBC:O_XBC + 768])
        P.dma("sp", s_conv[:, :, :], projd[SEQ:NTOK, O_XBC:O_XBC + 768].rearrange("(t s) f -> s t f", t=TS)[:, 1:4, :])
        P.emit_phase()
        kk.n_inst = P.n_inst
    return kk


_CACHE = {}


def _rope_tables():
    half = 32
    inv = (10000.0 ** (-np.arange(half, dtype=np.float32) / half)).astype(np.float32)
    pos = np.concatenate([np.arange(SEQ), np.repeat(8192 + np.arange(TS), NSEQ_S)]).astype(np.float32)
    ang = (pos[:, None] * inv[None, :]).astype(np.float32)
    return np.cos(ang).astype(np.float32), np.sin(ang).astype(np.float32)


def kernel(**inp):
    if "kk" not in _CACHE:
        _CACHE["kk"] = build_program()
    kk = _CACHE["kk"]
    f32 = np.float32
    cos, sin = _rope_tables()
    ii = np.arange(128)
    consts = np.concatenate([np.eye(128, dtype=f32), (ii[:, None] <= ii[None, :]).astype(f32),
                             (ii[None, :] < ii[:, None]).astype(f32), np.ones((128, 128), f32),
                             (ii[:, None] < 64).astype(f32), (ii[:, None] >= 64).astype(f32)], axis=1)
    tl = ii[:, None].astype(np.int64)
    cvalm = (tl - 16 * np.arange(255)[None, :] - 31).astype(f32)
    causb = np.where(ii[None, :] <= ii[:, None], 0.0, -30000.0).astype(f32)
    jj = np.arange(640)[None, :]
    delta = (jj // 128 - 4) * 128 + (jj % 128) - tl
    wbias = np.where((delta <= 0) & (delta > -512), 0.0, -30000.0).astype(f32)
    cvalid = (tl >= 31).astype(f32)
    consts2 = np.concatenate([cvalm, causb, wbias, cvalid], axis=1).astype(f32)
    qpos = (np.arange(32)[:, None] * 128 + np.arange(128)[None, :])
    cur = (qpos // 64)[:, :, None]
    blk = np.arange(64)[None, None, :]
    candm = ((blk >= 1) & (blk <= cur - 2)).astype(f32)
    forced = (((blk == 0) | (blk == cur) | (blk == cur - 1))).astype(f32)
    seltab = np.concatenate([candm, forced], axis=2).astype(f32)
    c3 = np.zeros((128, 1024), f32)
    rows = np.arange(64)
    rg, rr, rt = rows // 32, (rows % 32) // 4, rows % 4
    used = (rows % 32) < 16
    for rw in rows[used]:
        c3[rw, rg[rw] * 4 + rt[rw]] = 1.0
        c3[rg[rw] * 4 + rt[rw], 8 + rw] = 1.0
    c3[0:64, 72:76] = np.where(np.arange(4)[None, :] <= rt[:, None], 0.0, -30000.0)
    wr = np.arange(516)[None, :]
    wvalid = np.where(wr < 512, wr > rt[:, None], (wr - 512) <= rt[:, None])
    c3[0:64, 76:592] = np.where(wvalid, 0.0, -30000.0)
    c3[0:8, 592:720] = ((np.arange(128) >= 1) & (np.arange(128) <= 126)).astype(f32)[None, :]
    c3[0:8, 720:848] = ((np.arange(128) == 0) | (np.arange(128) == 127)).astype(f32)[None, :]
    c3[:, 848] = np.arange(128, dtype=f32)
    shared = {"consts": consts, "cos": cos, "sin": sin, "consts2": consts2, "seltab": seltab, "consts3": c3}
    for name in ("cache_k_cmp", "cache_v_cmp", "cache_k_slc", "cache_v_slc"):
        if name in kk.ins:
            shared[name] = np.asarray(inp[name])[0].reshape(-1, 128, 128)
    for name in ("w_in", "b_gate", "ffn1_w_gate", "ffn1_w_up", "ffn1_w_down", "ln1_g", "ln1_b",
                 "conv_w", "conv_b", "dt_bias", "a_log", "d_skip", "ssm_norm_w",
                 "ffn2_w_gate", "ffn2_w_up", "ffn2_w_down", "w_out", "ln2_g", "ln2_b", "ln3_g", "ln3_b",
                 "cmp_k_w1", "cmp_k_w2", "cmp_k_pe", "cmp_v_w1", "cmp_v_w2", "cmp_v_pe"):
        if name in kk.ins:
            a = np.asarray(inp[name])
            shared[name] = np.ascontiguousarray(a[0].reshape(kk.ins[name].shape))
    in_maps = []
    for c in range(NCORES):
        m = dict(shared)
        xs = np.asarray(inp["x_sample"])[c * NSEQ_S:(c + 1) * NSEQ_S].transpose(1, 0, 2).reshape(NSEQ_S * TS, D)
        m["x"] = np.ascontiguousarray(np.concatenate([np.asarray(inp["x_prompt"])[c], xs], axis=0))
        m["page_table"] = np.ascontiguousarray(np.asarray(inp["page_table"])[c * NSEQ_S:(c + 1) * NSEQ_S]).astype(np.int32)
        m["state_ssm"] = np.ascontiguousarray(np.asarray(inp["state_ssm"])[0, c * NSEQ_S:(c + 1) * NSEQ_S])
        m["state_conv"] = np.ascontiguousarray(np.asarray(inp["state_conv"])[0, c * NSEQ_S:(c + 1) * NSEQ_S])
        m["cache_k_win"] = np.ascontiguousarray(np.asarray(inp["cache_k_win"])[0, c * NSEQ_S:(c + 1) * NSEQ_S].reshape(NSEQ_S, 512, 128))
        m["cache_v_win"] = np.ascontiguousarray(np.asarray(inp["cache_v_win"])[0, c * NSEQ_S:(c + 1) * NSEQ_S].reshape(NSEQ_S, 512, 128))
        in_maps.append({k: v for k, v in m.items() if k in kk.ins})
    res = run_bass_kernel_spmd(kk.nc, in_maps, core_ids=list(range(NCORES)))
    R = res.results
    _CACHE["last"] = R

    def cat(name, sl=None):
        return np.stack([np.asarray(r[name]) if sl is None else np.asarray(r[name])[sl] for r in R], axis=0)

    kv = cat("kv_out")
    kvp = kv[:, :SEQ].reshape(NCORES, SEQ, 6, 2, 64)
    kvs = kv[:, SEQ:].reshape(NCORES, TS, NSEQ_S, 6, 2, 64).transpose(0, 2, 1, 3, 4, 5).reshape(NCORES * NSEQ_S, TS, 6, 2, 64)
    outs = []
    yo = cat("y_out")
    y_p = np.ascontiguousarray(yo[:, :SEQ])
    y_s = np.ascontiguousarray(yo[:, SEQ:].reshape(NCORES, TS, NSEQ_S, D).transpose(0, 2, 1, 3).reshape(NCORES * NSEQ_S, TS, D))
    outs += [y_p, y_s]
    for i in range(4):
        outs.append(np.ascontiguousarray(kvp[:, :, i])[None])
        outs.append(np.ascontiguousarray(kvs[:, :, i])[None])
    pw = cat("p_win")
    sw = cat("s_win")
    for i in range(2):
        outs.append(np.ascontiguousarray(pw[:, i]).reshape(1, 8, 512, 2, 64))
        outs.append(np.ascontiguousarray(sw[:, i]).reshape(1, 128, 512, 2, 64))
    outs.append(cat("p_ssm")[None])
    outs.append(cat("s_ssm").reshape(1, 128, 8, 64, 64))
    outs.append(cat("p_conv")[None])
    outs.append(cat("s_conv").reshape(1, 128, 3, 768))
    return tuple(outs)
```
